# Optimizing a Trainium2 kernel written in Bass

```python
import jax, jax.numpy as jnp
from jax import lax
import numpy as np

D_MODEL = 1024
BATCH = 16
SEQ = 256
DEPTH = 2
DEC_BATCH = 4
DEC_SEQ = 2048
PAST_LEN = 256

GRID_W = 64
N_HEADS = 8
N_KV_HEADS = 2
HEAD_DIM = D_MODEL // 16
ATTN_W = N_HEADS * HEAD_DIM
KV_W = N_KV_HEADS * HEAD_DIM
CONV_W = D_MODEL // 4
POOL_W = D_MODEL // 4
POOL_WINDOWS = (2, 4, 8, 16)
POOL_GROUP = POOL_W // len(POOL_WINDOWS)
MIX_W = ATTN_W + CONV_W + POOL_W
SPLIT_SIZES = (ATTN_W, KV_W, KV_W, ATTN_W, CONV_W, CONV_W, CONV_W, CONV_W, POOL_W, POOL_W)
IN_W = sum(SPLIT_SIZES)
SPLIT_IDX = [int(i) for i in np.cumsum(SPLIT_SIZES)[:-1]]
CONV_K = 3
ROPE_THETA = 10000.0
Q_BLOCK = 128
EPS = 1e-6

kernel_name = "hybrid_dit_parallel_heads_step"


def _rmsnorm(x, g):
    xf = x.astype(jnp.float32)
    y = xf * lax.rsqrt(jnp.mean(xf * xf, axis=-1, keepdims=True) + EPS)
    return (y * g.astype(jnp.float32)).astype(x.dtype)


def _rope_angles(n):
    rows = n // GRID_W
    row = jnp.repeat(jnp.arange(rows), GRID_W).astype(jnp.float32)
    col = jnp.tile(jnp.arange(GRID_W), rows).astype(jnp.float32)
    half = HEAD_DIM // 2
    inv = 1.0 / (ROPE_THETA ** (jnp.arange(0, half, 2, dtype=jnp.float32) / half))
    return row[:, None] * inv, col[:, None] * inv


def _rot(x, ang):
    xf = x.astype(jnp.float32)
    x1, x2 = jnp.split(xf, 2, axis=-1)
    cs = jnp.cos(ang)[None, :, None, :]
    sn = jnp.sin(ang)[None, :, None, :]
    return jnp.concatenate([x1 * cs - x2 * sn, x2 * cs + x1 * sn], axis=-1).astype(x.dtype)


def _apply_rope(x, angs):
    half = HEAD_DIM // 2
    return jnp.concatenate([_rot(x[..., :half], angs[0]), _rot(x[..., half:], angs[1])], axis=-1)


def _attend(q, k, v):
    B, S, H, D = q.shape
    KV = k.shape[2]
    G = H // KV
    nb = S // Q_BLOCK
    qb = q.reshape(B, nb, Q_BLOCK, KV, G, D).transpose(1, 0, 2, 3, 4, 5)
    scale = HEAD_DIM ** -0.5

    def one(qblk):
        s = jnp.einsum('bqkgd,btkd->bkgqt', qblk, k).astype(jnp.float32) * scale
        p = jax.nn.softmax(s, axis=-1)
        return jnp.einsum('bkgqt,btkd->bqkgd', p.astype(v.dtype), v)

    o = lax.map(one, qb)
    return o.transpose(1, 0, 2, 3, 4, 5).reshape(B, S, H * D)


def _short_conv(x, w, b):
    n = x.shape[1]
    xp = jnp.pad(x, ((0, 0), (1, 1), (0, 0)))
    return xp[:, 0:n] * w[0] + xp[:, 1:n + 1] * w[1] + xp[:, 2:n + 2] * w[2] + b


def _pool_mix(u, w_pool, scale):
    n = u.shape[1]
    uf = u.astype(jnp.float32)
    cs = jnp.concatenate([jnp.zeros_like(uf[:, :1]), jnp.cumsum(uf, axis=1)], axis=1)
    t = jnp.arange(n)
    outs = []
    for gi, win in enumerate(POOL_WINDOWS):
        sl = slice(gi * POOL_GROUP, (gi + 1) * POOL_GROUP)
        lo = jnp.clip(t - win // 2, 0, n - 1)
        hi = jnp.clip(t - win // 2 + win - 1, 0, n - 1)
        cnt = (hi - lo + 1).astype(jnp.float32)
        d = (cs[:, hi + 1, sl] - cs[:, lo, sl]) / cnt[None, :, None] - uf[:, :, sl]
        outs.append(jnp.einsum('bnc,cd->bnd', d.astype(u.dtype), w_pool[gi]))
    return jnp.concatenate(outs, axis=-1) * scale


def _layer(x, mod, norm_g, w_in, q_g, k_g, conv_w, conv_b, pool_w, pool_scale, w_out,
           angs, ctx_k, ctx_v):
    B, n, _ = x.shape
    shift, scl, gate = jnp.split(mod, 3, axis=-1)
    xn = _rmsnorm(x, norm_g) * (1 + scl) + shift
    proj = xn @ w_in
    q, k, v, z_a, h_c, b_c, c_c, z_c, u_p, z_p = jnp.split(proj, SPLIT_IDX, axis=-1)
    q = _rmsnorm(q.reshape(B, n, N_HEADS, HEAD_DIM), q_g)
    k = _rmsnorm(k.reshape(B, n, N_KV_HEADS, HEAD_DIM), k_g)
    v = v.reshape(B, n, N_KV_HEADS, HEAD_DIM)
    if angs is None:
        k_all, v_all = k, v
    else:
        q = _apply_rope(q, angs)
        k_all = jnp.concatenate([_apply_rope(k, angs), ctx_k], axis=1)
        v_all = jnp.concatenate([v, ctx_v], axis=1)
    attn = _attend(q, k_all, v_all) * jax.nn.silu(z_a)
    conv = b_c * _short_conv(c_c * h_c, conv_w, conv_b) * jax.nn.silu(z_c)
    pool = _pool_mix(u_p, pool_w, pool_scale) * jax.nn.silu(z_p)
    out = jnp.concatenate([attn, conv, pool], axis=-1) @ w_out
    return x + gate * out, k, v


def setup_inputs(seed: int = 0) -> dict:
    key = jax.random.key(seed)
    ks = jax.random.split(key, 20)
    f32 = jnp.float32
    nrm = lambda k, shape, s: jax.random.normal(k, shape, f32) * s
    return {
        "x_prompt": nrm(ks[0], (BATCH, SEQ, D_MODEL), 1.0),
        "x_sample": nrm(ks[1], (DEC_BATCH, DEC_SEQ, D_MODEL), 1.0),
        "cache_k": nrm(ks[2], (DEC_BATCH, DEPTH, PAST_LEN, N_KV_HEADS, HEAD_DIM), 1.0),
        "cache_v": nrm(ks[3], (DEC_BATCH, DEPTH, PAST_LEN, N_KV_HEADS, HEAD_DIM), 1.0),
        "c": nrm(ks[4], (DEC_BATCH, D_MODEL), 1.0),
        "c_ctx": nrm(ks[5], (D_MODEL,), 1.0),
        "norm_g": 1.0 + nrm(ks[6], (DEPTH, D_MODEL), 0.05),
        "w_ada": nrm(ks[7], (DEPTH, D_MODEL, 3 * D_MODEL), 0.5 * D_MODEL ** -0.5),
        "b_ada": nrm(ks[8], (DEPTH, 3 * D_MODEL), 0.02),
        "w_in": nrm(ks[9], (DEPTH, D_MODEL, IN_W), D_MODEL ** -0.5),
        "q_norm_g": 1.0 + nrm(ks[10], (DEPTH, HEAD_DIM), 0.05),
        "k_norm_g": 1.0 + nrm(ks[11], (DEPTH, HEAD_DIM), 0.05),
        "conv_w": nrm(ks[12], (DEPTH, CONV_K, CONV_W), CONV_K ** -0.5),
        "conv_b": nrm(ks[13], (DEPTH, CONV_W), 0.02),
        "pool_w": nrm(ks[14], (DEPTH, len(POOL_WINDOWS), POOL_GROUP, POOL_GROUP), POOL_GROUP ** -0.5),
        "pool_scale": 1.0 + nrm(ks[15], (DEPTH, POOL_W), 0.1),
        "w_out": nrm(ks[16], (DEPTH, MIX_W, D_MODEL), MIX_W ** -0.5),
        "final_g": 1.0 + nrm(ks[17], (D_MODEL,), 0.05),
    }


def reference(x_prompt, x_sample, cache_k, cache_v, c, c_ctx, norm_g, w_ada, b_ada, w_in,
              q_norm_g, k_norm_g, conv_w, conv_b, pool_w, pool_scale, w_out, final_g):
    h = x_prompt
    new_ks, new_vs = [], []
    for l in range(DEPTH):
        mod = (jax.nn.silu(c_ctx) @ w_ada[l] + b_ada[l])[None, None, :]
        h, k_l, v_l = _layer(h, mod, norm_g[l], w_in[l], q_norm_g[l], k_norm_g[l], conv_w[l],
                             conv_b[l], pool_w[l], pool_scale[l], w_out[l], None, None, None)
        new_ks.append(k_l)
        new_vs.append(v_l)
    y_prompt = _rmsnorm(h, final_g)
    new_k = jnp.stack(new_ks, axis=1)
    new_v = jnp.stack(new_vs, axis=1)

    angs = _rope_angles(x_sample.shape[1])
    z = x_sample
    for l in range(DEPTH):
        mod = (jax.nn.silu(c) @ w_ada[l] + b_ada[l])[:, None, :]
        z, _, _ = _layer(z, mod, norm_g[l], w_in[l], q_norm_g[l], k_norm_g[l], conv_w[l],
                         conv_b[l], pool_w[l], pool_scale[l], w_out[l], angs,
                         cache_k[:, l], cache_v[:, l])
    y_sample = _rmsnorm(z, final_g)
    return (y_prompt, y_sample, new_k, new_v)
```

```python
import contextlib
import numpy as np
import concourse.bass as bass
import concourse.mybir as mybir
from concourse.bass_utils import run_bass_kernel_spmd

F32 = mybir.dt.float32
BF16 = mybir.dt.bfloat16
F32R = mybir.dt.float32r
ALU = mybir.AluOpType
AF = mybir.ActivationFunctionType

NCORES = 8
D = 1024
QO, KO, VO, ZAO, HCO, BCO, CCO, ZCO, UPO, ZPO = 0, 512, 640, 768, 1280, 1536, 1792, 2048, 2304, 2560
IN_W = 2816
EPS = 1e-6

C_CC, C_NG, C_QG, C_KG, C_CW, C_CB, C_PS, C_MSK = 0, 16, 32, 34, 36, 48, 52, 56
NCST = 64

import os as _osk
STRICT = _osk.environ.get("KSTRICT", "1") == "1"
COMPUTE = ("pe", "act", "dve", "pool")
ALL_ENG = COMPUTE + ("sp",)


_DBG = {}


class Sched:
    def __init__(self, nc, sem_alloc):
        self.nc = nc
        self.sem_alloc = sem_alloc
        self.items = {e: [] for e in ALL_ENG}
        self.sems = {}
        self.eng_cnt = {e: 0 for e in COMPUTE}
        for e in COMPUTE:
            self.sems["c_" + e] = sem_alloc("c_" + e)
        self.known = {e: {} for e in ALL_ENG}
        self.stream_cnt = {}
        self.last_w = {}
        self.readers = {}
        self.all_tokens = {}
        self.nops = 0

    def _deps(self, eng, reads, writes):
        toks = []
        for r in reads:
            t = self.last_w.get(r)
            if t is not None:
                toks.append(t)
        for w in writes:
            t = self.last_w.get(w)
            if t is not None and (t[2] != eng or STRICT):
                toks.append(t)
            for t in self.readers.get(w, ()):
                if t[2] != eng or STRICT:
                    toks.append(t)
        return toks

    def _emit_waits(self, eng, toks):
        need = {}
        for (sname, val, src) in toks:
            if src == eng and eng == "pe":
                continue
            if self.known[eng].get(sname, 0) >= val:
                continue
            if need.get(sname, 0) < val:
                need[sname] = val
        for sname, val in need.items():
            self.known[eng][sname] = val
            self.items[eng].append(("wait", sname, val))

    def _commit(self, tok, reads, writes):
        for r in reads:
            self.readers.setdefault(r, []).append(tok)
        for w in writes:
            self.last_w[w] = tok
            self.readers[w] = []
        self.all_tokens[tok[0]] = max(self.all_tokens.get(tok[0], 0), tok[1])

    def op(self, eng, fn, reads=(), writes=()):
        reads = tuple(reads)
        writes = tuple(writes)
        writes = writes + tuple(r for r in reads if r.startswith("ps") and r not in writes)
        self._emit_waits(eng, self._deps(eng, reads, writes))
        self.eng_cnt[eng] += 1
        tok = ("c_" + eng, self.eng_cnt[eng], eng)
        self.items[eng].append(("op", fn, "c_" + eng, 1))
        self._commit(tok, reads, writes)
        self.nops += 1
        return tok

    def dma(self, q, fn, stream, reads=(), writes=()):
        reads = tuple(reads)
        writes = tuple(writes)
        sname = "d_" + stream
        if sname not in self.sems:
            self.sems[sname] = self.sem_alloc(sname)
            self.stream_cnt[sname] = 0
        self._emit_waits(q, self._deps(None, reads, writes))
        self.stream_cnt[sname] += 16
        tok = (sname, self.stream_cnt[sname], "dma")
        self.items[q].append(("op", fn, sname, 16))
        self._commit(tok, reads, writes)
        return tok

    def final_wait(self, eng="sp"):
        self._emit_waits(eng, [(s, v, "x") for s, v in self.all_tokens.items()])

    def replay(self, block):
        sems = self.sems

        def run(eng_name):
            def body(e):
                for it in self.items[eng_name]:
                    if it[0] == "wait":
                        e.wait_ge(sems[it[1]], it[2])
                    else:
                        it[1](e).then_inc(sems[it[2]], it[3])
            return body
        block.tensor(run("pe"))
        block.scalar(run("act"))
        block.vector(run("dve"))
        block.gpsimd(run("pool"))
        block.sync(run("sp"))


def build_nc():
    nc = bass.Bass("TRN2", target_bir_lowering=False)

    def din(name, shape):
        return nc.dram_tensor(name, list(shape), F32, kind="ExternalInput").ap()

    def dout(name, shape):
        return nc.dram_tensor(name, list(shape), F32, kind="ExternalOutput").ap()

    xs_d = din("xs", [2048, D])
    xp_d = din("xp", [512, D])
    ck_d = din("ck", [2, 256, 128])
    cv_d = din("cv", [2, 256, 128])
    wada_d = din("w_ada", [2, D, 3 * D])
    bada_d = din("b_ada", [2, 3 * D])
    win_d = din("w_in", [2, D, IN_W])
    wout_d = din("w_out", [2, D, D])
    cst_d = din("cst", [128, NCST])
    mats_d = din("mats", [128, 4, 128])
    pwm_d = din("pwm", [128, 4, 128])
    kgb_d = din("kgb", [128, 2, 128])
    fgb_d = din("fgb", [128, D])
    ropec_d = din("ropec", [128, 2048])
    ropes_d = din("ropes", [128, 2048])
    prs_d = din("prs", [128, 2, 2048])
    prp_d = din("prp", [128, 2, 512])
    yp_d = dout("yp", [512, D])
    ys_d = dout("ys", [1024, D])
    nk_d = dout("nk", [2, 2, 256, 128])
    nv_d = dout("nv", [2, 2, 256, 128])
    gsc_d = nc.dram_tensor("gsc", [4, D], F32).ap()

    es = contextlib.ExitStack()
    with es:
        def T(name, shape, dt):
            return es.enter_context(nc.sbuf_tensor(name, list(shape), dt))

        xb = T("xb", [128, 16, D], F32)
        wi = T("wi", [128, 8, IN_W], BF16)
        wo = T("wo", [128, 8, D], BF16)
        stg = T("stg", [128, 2, D], F32)
        xn = T("xn", [128, 8, 512], BF16)
        KT = T("KT", [128, 2304], BF16)
        Vst = T("Vst", [128, 18, 192], BF16)
        qT = T("qT", [128, 4, 2, 512], BF16)
        ga = T("ga", [128, 4, 512], BF16)
        mix = T("mix", [128, 8, 512], BF16)
        ybf2 = T("ybf", [128, 2, D], BF16)
        ybf = ybf2[:, 0, :]
        rope = T("rope", [128, 2, 512], F32)
        ptb = T("ptb", [128, 4, 512], BF16)
        tA = T("tA", [128, 544], F32)
        tB = T("tB", [128, 544], F32)
        tC = T("tC", [128, 544], F32)
        tD = T("tD", [128, 544], F32)
        tE = T("tE", [128, 512], BF16)
        tF = T("tF", [128, 512], BF16)
        pB1 = T("pB1", [128, 544], BF16)
        pC1 = T("pC1", [128, 544], BF16)
        pD1 = T("pD1", [128, 544], BF16)
        pA1 = T("pA1", [128, 512], BF16)
        pE1 = T("pE1", [128, 512], BF16)
        xe = T("xe", [128, 8, 4, 2, 8], BF16)
        xh = T("xh", [128, 8, 16], BF16)
        cst = T("cst_s", [128, NCST], F32)
        matf = T("matf", [128, 4, 128], F32)
        identb = T("identb", [128, 128], BF16)
        rmb = T("rmb", [128, 128], BF16)
        bonesb = T("bonesb", [128, 128], BF16)
        pwb = T("pwb", [128, 4, 128], BF16)
        kgbs = T("kgbs", [128, 2, 128], F32)
        epsc = T("epsc", [128, 1], F32)
        scs = T("scs", [128, 16], F32)
        gsc_s = T("gsc_s", [128, 32], F32)
        shc_s = T("shc_s", [128, 32], F32)
        ms = T("ms", [128, 16], F32)
        lnm = T("lnm", [128, 16], F32)
        rstd = T("rstd", [128, 16], F32)
        ssk = T("ssk", [128, 2], F32)
        lnk = T("lnk", [128, 2], F32)
        rk = T("rk", [128, 2], F32)
        hhs = T("hhs", [128, 32], F32)
        uh = T("uh", [128, 32], F32)
        uph = T("uph", [128, 32], F32)
        nks = T("nks", [128, 128], F32)
        nvs = T("nvs", [128, 128], F32)

        pp = [es.enter_context(nc.psum_tensor("pp%d" % i, [128, 1024], F32)) for i in range(4)]
        banks = [pp[i // 2][:, (i % 2) * 512:(i % 2 + 1) * 512] for i in range(8)]
        bankb = [b.bitcast(BF16) for b in banks]

        identf = matf[:, 0, :]
        swpf = matf[:, 3, :]
        gbc = rope[:].rearrange("p a b -> p (a b)")
        prc = ptb[:].rearrange("p a b -> p (a b)").bitcast(F32).rearrange("p (c n) -> p c n", c=2)
        cks = tA[:, 0:256].rearrange("p (t c) -> p t c", t=2)
        cvs = tB[:, 0:256].rearrange("p (t c) -> p t c", t=2)
        ckb = tE[:, 0:256]

        wst = [xb[:, 0:3, :].rearrange("p a b -> p (a b)"), xb[:, 3:6, :].rearrange("p a b -> p (a b)")]
        mrow = xb[0:2, 6:9, :].rearrange("p a b -> p (a b)")
        brow = xb[0:2, 9:12, :].rearrange("p a b -> p (a b)")
        XBR = ["xb%d" % t for t in range(16)]
        WIRA = ["wi%da" % k for k in range(8)]
        WIRB = ["wi%db" % k for k in range(8)]
        WIR = WIRA + WIRB
        WOR = ["wo%d" % k for k in range(8)]

        block = es.enter_context(nc.Block())
        s = Sched(nc, lambda n: es.enter_context(nc.semaphore(n)))

        def col(c):
            return cst[:, c:c + 1]

        s.dma("sp", lambda e: e.dma_start(out=cst[:], in_=cst_d), "cst", writes=["cst"])
        s.dma("sp", lambda e: e.dma_start(out=matf[:], in_=mats_d), "mats", writes=["matf"])
        s.dma("sp", lambda e: e.dma_start(out=kgbs[:], in_=kgb_d), "kgb", writes=["kgbs"])
        s.dma("sp", lambda e: e.dma_start(out=tA[:, 0:512].rearrange("p (a b) -> p a b", a=4), in_=pwm_d),
              "pwm", writes=["tA"])
        s.op("dve", lambda e: e.tensor_copy(out=identb[:], in_=matf[:, 0, :]), reads=["matf"], writes=["identb"])
        s.op("dve", lambda e: e.tensor_copy(out=rmb[:], in_=matf[:, 1, :]), reads=["matf"], writes=["rmb"])
        s.op("dve", lambda e: e.tensor_copy(out=bonesb[:], in_=matf[:, 2, :]), reads=["matf"], writes=["bonesb"])
        s.op("dve", lambda e: e.tensor_copy(out=pwb[:], in_=tA[:, 0:512].rearrange("p (a b) -> p a b", a=4)),
             reads=["tA"], writes=["pwb"])
        s.op("dve", lambda e: e.memset(epsc[:], EPS), writes=["epsc"])
        s.op("dve", lambda e: e.memset(Vst[:, :, 64:128], 1.0), writes=["Vones"])
        s.op("dve", lambda e: e.memset(qT[64:128, :, 0, :], 0.0), writes=["qTz"])
        s.op("dve", lambda e: e.memset(qT[0:64, :, 1, :], 0.0), writes=["qTz"])
        s.op("act", lambda e: e.activation(out=scs[:], in_=cst[:, C_CC:C_CC + 16], func=AF.Silu),
             reads=["cst"], writes=["scs"])

        def mod_begin(l):
            s.dma("sp", lambda e: e.dma_start(out=brow, in_=bada_d[l:l + 1, :].partition_broadcast(2)), "brow", writes=XBR[9:12])

        def mod_chunk(l, k):
            sl = k % 2
            s.dma("sp", lambda e: e.dma_start(out=wst[sl], in_=wada_d[l, k * 128:(k + 1) * 128, :]),
                  "wst%d" % sl, writes=XBR[3 * sl:3 * sl + 3])

            def mm(e):
                last = None
                for n in range(6):
                    last = e.matmul(banks[n][0:2, :], lhsT=scs[:, 2 * k:2 * k + 2], rhs=wst[sl][:, n * 512:(n + 1) * 512],
                                    start=(k == 0), stop=(k == 7))
                return last
            s.op("pe", mm, reads=["scs"] + XBR[3 * sl:3 * sl + 3], writes=["ps%d" % n for n in range(6)])

        def mod_end(l):
            for n in range(6):
                s.op("dve", lambda e, n=n: e.tensor_tensor(
                    out=mrow[:, n * 512:(n + 1) * 512], in0=banks[n][0:2, :], in1=brow[:, n * 512:(n + 1) * 512], op=ALU.add),
                    reads=["ps%d" % n] + XBR[9:12], writes=XBR[6:9])
            s.dma("act", lambda e: e.dma_start(out=gsc_d[2 * l:2 * l + 2, :], in_=mrow[:, 2048:3072]),
                  "gscw", reads=XBR[6:9], writes=["gsc_d%d" % l])

            def tr(e):
                last = None
                for j in range(16):
                    last = e.transpose(banks[6][:, 2 * j:2 * j + 2], mrow[:, j * 128:(j + 1) * 128], matf[0:2, 0, 0:2])
                return last
            s.op("pe", tr, reads=["matf"] + XBR[6:9], writes=["ps6"])
            tps3 = banks[6][:, 0:32].rearrange("p (j v) -> p j v", v=2)
            for v in range(2):
                lv = l * 2 + v
                s.op("dve", lambda e, v=v, lv=lv: e.tensor_copy(out=shc_s[:, lv * 8:(lv + 1) * 8], in_=tps3[:, 0:8, v]),
                     reads=["ps6"], writes=["shc%d" % lv])
                s.op("dve", lambda e, v=v, lv=lv: e.scalar_tensor_tensor(
                    out=gsc_s[:, lv * 8:(lv + 1) * 8], in0=tps3[:, 8:16, v], scalar=1.0,
                    in1=cst[:, C_NG + l * 8:C_NG + (l + 1) * 8], op0=ALU.add, op1=ALU.mult),
                    reads=["ps6", "cst"], writes=["gsc%d" % lv])

        def mod_layer(l):
            mod_begin(l)
            for k in range(8):
                mod_chunk(l, k)
            mod_end(l)

        stg_i = [0]

        def stage_slot():
            sl = stg_i[0] % 2
            stg_i[0] += 1
            return sl

        def prep_wi(l, ce="pool", kv_first=False):
            if kv_first:
                for k in range(8):
                    prep_wi_k(l, k, ce, pieces=(0,))
                for k in range(8):
                    prep_wi_k(l, k, ce, pieces=(1, 2))
            else:
                for k in range(8):
                    prep_wi_k(l, k, ce)

        def prep_wi_k(l, k, ce="pool", pieces=(0, 1, 2), q="sp"):
            rows = slice(k * 128, (k + 1) * 128)
            qv = wi[:, k, 0:512].rearrange("p (j t d) -> p j t d", j=4, t=2)
            zv = wi[:, k, ZAO:ZAO + 512].rearrange("p (j t d) -> p j t d", j=4, t=2)
            if 0 in pieces:
                sl = stage_slot()
                s.dma(q, lambda e, l=l, rows=rows, sl=sl: e.dma_start(out=stg[:, sl, :], in_=win_d[l, rows, 0:1024]),
                      "stg%d" % sl, writes=["stg%da" % sl, "stg%db" % sl])

                def c0(e, k=k, sl=sl, qv=qv, zv=zv):
                    e.tensor_copy(out=qv[:, :, 0, :], in_=stg[:, sl, 0:256].rearrange("p (j d) -> p j d", j=4))
                    e.tensor_copy(out=qv[:, :, 1, :], in_=stg[:, sl, 256:512].rearrange("p (j d) -> p j d", j=4))
                    e.tensor_copy(out=wi[:, k, 512:768], in_=stg[:, sl, 512:768])
                    return e.tensor_copy(out=zv[:, :, 0, :], in_=stg[:, sl, 768:1024].rearrange("p (j d) -> p j d", j=4))
                s.op(ce, c0, reads=["stg%da" % sl, "stg%db" % sl], writes=["wi%da" % k])
            if 1 in pieces:
                sl = stage_slot()
                s.dma(q, lambda e, l=l, rows=rows, sl=sl: e.dma_start(out=stg[:, sl, :], in_=win_d[l, rows, 1024:2048]),
                      "stg%d" % sl, writes=["stg%da" % sl, "stg%db" % sl])

                def c1(e, k=k, sl=sl, zv=zv):
                    e.tensor_copy(out=zv[:, :, 1, :], in_=stg[:, sl, 0:256].rearrange("p (j d) -> p j d", j=4))
                    return e.tensor_copy(out=wi[:, k, 1280:2048], in_=stg[:, sl, 256:1024])
                s.op(ce, c1, reads=["stg%da" % sl, "stg%db" % sl], writes=["wi%db" % k])
            if 2 in pieces:
                sl = stage_slot()
                s.dma(q, lambda e, l=l, rows=rows, sl=sl: e.dma_start(out=stg[:, sl, 0:768], in_=win_d[l, rows, 2048:2816]),
                      "stg%d" % sl, writes=["stg%da" % sl, "stg%db" % sl])
                s.op(ce, lambda e, k=k, sl=sl: e.tensor_copy(out=wi[:, k, 2048:2816], in_=stg[:, sl, 0:768]),
                     reads=["stg%da" % sl, "stg%db" % sl], writes=["wi%db" % k])

        def prep_wo(l, v, ce="pool"):
            s.dma("sp", lambda e, l=l, v=v: e.dma_start(out=gbc, in_=gsc_d[2 * l + v:2 * l + v + 1, :].partition_broadcast(128)),
                  "gbc", reads=["gsc_d%d" % l], writes=["rope", "rope2"])
            for k in range(8):
                sl = stage_slot()
                if k < 4:
                    s.dma("sp", lambda e, l=l, k=k, sl=sl: e.dma_start(out=stg[0:64, sl, :], in_=wout_d[l, k * 64:(k + 1) * 64, :]),
                          "stgh%da" % sl, writes=["stg%da" % sl])
                    s.dma("sp", lambda e, l=l, k=k, sl=sl: e.dma_start(out=stg[64:128, sl, :], in_=wout_d[l, (k + 4) * 64:(k + 5) * 64, :]),
                          "stgh%db" % sl, writes=["stg%db" % sl])
                else:
                    s.dma("sp", lambda e, l=l, k=k, sl=sl: e.dma_start(out=stg[:, sl, :], in_=wout_d[l, k * 128:(k + 1) * 128, :]),
                          "stg%d" % sl, writes=["stg%da" % sl, "stg%db" % sl])
                s.op(ce, lambda e, k=k, sl=sl: e.tensor_tensor(out=wo[:, k, :], in0=stg[:, sl, :], in1=gbc, op=ALU.mult),
                     reads=["stg%da" % sl, "stg%db" % sl, "rope", "rope2"], writes=["wo%d" % k])

        acc_i = [0]

        def next_acc():
            b = (0, 1, 4, 5)[acc_i[0] % 4]
            acc_i[0] += 1
            return b

        tp_i = [0]
        stats_done = set()

        def make_stats(tiles, l, v):
            t0 = tiles[0]
            key = (tuple(tiles), l, v)
            if key in stats_done:
                return
            stats_done.add(key)
            s.op("dve", lambda e, t0=t0: e.memset(ms[:, t0:t0 + 4], 0.0), writes=["ms%d" % t0])
            for t in tiles:
                s.op("act", lambda e, t=t: e.activation(out=ybf[:], in_=xb[:, t, :], func=AF.Square, scale=1.0 / 32.0,
                                                        accum_out=ms[:, t:t + 1]),
                     reads=[XBR[t], "ms%d" % t0], writes=["ybf", "msv%d" % t])
            s.op("act", lambda e, t0=t0: e.activation(out=lnm[:, t0:t0 + 4], in_=ms[:, t0:t0 + 4], func=AF.Ln, bias=epsc[:, 0:1]),
                 reads=["msv%d" % t for t in tiles] + ["epsc"], writes=["lnm%d" % t0])
            s.op("act", lambda e, t0=t0: e.activation(out=rstd[:, t0:t0 + 4], in_=lnm[:, t0:t0 + 4], func=AF.Exp, scale=-0.5),
                 reads=["lnm%d" % t0], writes=["rstd%d" % t0])

        def make_xn(tiles, l, v):
            lv = l * 2 + v
            t0 = tiles[0]
            make_stats(tiles, l, v)
            import os as _os
            KXN = int(_os.environ.get("KXN", "9"))
            if KXN < 1:
                return
            tpa = [pp[0][:].bitcast(BF16), pp[2][:].bitcast(BF16)]
            TPB = ["ps0", "ps1", "ps4", "ps5"]
            for i, t in enumerate(tiles):
                yb = ybf2[:, i % 2, :]
                ybn = "ybf" if i % 2 == 0 else "ybfB"
                s.op("dve", lambda e, t=t, yb=yb: e.tensor_scalar(out=yb, in0=xb[:, t, :], scalar1=rstd[:, t:t + 1], scalar2=None,
                                                                  op0=ALU.mult),
                     reads=[XBR[t], "rstd%d" % t0], writes=[ybn])

                def tr(e, yb=yb, i=i):
                    last = None
                    for k in range(8):
                        last = e.transpose(tpa[k // 4][:, (k % 4) * 512 + i * 128:(k % 4) * 512 + (i + 1) * 128],
                                           yb[:, k * 128:(k + 1) * 128], identb[:])
                    return last
                s.op("pe", tr, reads=[ybn, "identb"], writes=TPB)
            for k in range(8):
                src = tpa[k // 4][:, (k % 4) * 512:(k % 4 + 1) * 512]
                if k < 4:
                    s.op("act", lambda e, k=k, src=src: e.activation(
                        out=xn[:, k, :], in_=src, func=AF.Identity,
                        scale=gsc_s[:, lv * 8 + k:lv * 8 + k + 1], bias=shc_s[:, lv * 8 + k:lv * 8 + k + 1]),
                        reads=["ps0", "ps1", "gsc%d" % lv, "shc%d" % lv], writes=["xn_%d_%d" % (i, k) for i in range(4)])
                else:
                    s.op("dve", lambda e, k=k, src=src: e.tensor_scalar(
                        out=xn[:, k, :], in0=src, scalar1=gsc_s[:, lv * 8 + k:lv * 8 + k + 1],
                        scalar2=shc_s[:, lv * 8 + k:lv * 8 + k + 1], op0=ALU.mult, op1=ALU.add),
                        reads=["ps4", "ps5", "gsc%d" % lv, "shc%d" % lv], writes=["xn_%d_%d" % (i, k) for i in range(4)])

        XNR = ["xn_%d_%d" % (i, k) for i in range(4) for k in range(8)]

        def proj_fm(col0, nq=512, wres=None):
            b = next_acc()

            def mm(e, b=b, col0=col0):
                last = None
                for k in range(8):
                    last = e.matmul(banks[b][:, 0:nq], lhsT=wi[:, k, col0:col0 + 128], rhs=xn[:, k, 0:nq],
                                    start=(k == 0), stop=(k == 7))
                return last
            s.op("pe", mm, reads=(wres or WIR) + XNR, writes=["ps%d" % b])
            return b

        def load_rope(g):
            s.dma("act", lambda e, g=g: e.dma_start(out=rope[:, 0, :], in_=ropec_d[:, g * 512:(g + 1) * 512]),
                  "rope", writes=["rope"])
            s.dma("act", lambda e, g=g: e.dma_start(out=rope[:, 1, :], in_=ropes_d[:, g * 512:(g + 1) * 512]),
                  "rope2", writes=["rope2"])

        qk_i = [0]

        def qk_post(b, gcol, l_col, out_ap, out_res, use_rope):
            par = qk_i[0] % 2
            qk_i[0] += 1
            rb, rbn = ((tA, "tA"), (tB, "tB"))[par]
            if par == 0:
                sq_t, sqn, qg_t, qgn = tE, ["tE"], tF, ["tF"]
                c_t, cn, d_t, dn = tC, ["tC"], tD, ["tD"]
                ssb, rotb = 2, 3
            else:
                sq_t, sqn, qg_t, qgn = pA1, ["p1sa"], pE1, ["p1dt"]
                c_t, cn, d_t, dn = pB1, ["p1ub", "p1ubl", "p1ubr"], pC1, ["p1wc"]
                ssb, rotb = 6, 7
            s.op("act", lambda e, b=b: e.activation(out=sq_t[:, 0:512], in_=banks[b][:], func=AF.Square),
                 reads=["ps%d" % b], writes=sqn)
            s.op("act", lambda e, b=b: e.activation(out=qg_t[:, 0:512], in_=banks[b][:], func=AF.Copy, scale=col(gcol + l_col)),
                 reads=["ps%d" % b, "cst"], writes=qgn)
            s.op("pe", lambda e: e.matmul(banks[ssb][:], lhsT=bonesb[:], rhs=sq_t[:, 0:512], start=True, stop=True),
                 reads=["bonesb"] + sqn, writes=["ps%d" % ssb])
            if use_rope:
                s.op("pe", lambda e: e.matmul(banks[rotb][:], lhsT=rmb[:], rhs=qg_t[:, 0:512], start=True, stop=True),
                     reads=["rmb"] + qgn, writes=["ps%d" % rotb])
            s.op("act", lambda e: e.activation(out=rb[:, 0:512], in_=banks[ssb][:], func=AF.Ln, scale=1.0 / 64.0, bias=epsc[:, 0:1]),
                 reads=["ps%d" % ssb, "epsc"], writes=[rbn])
            s.op("act", lambda e: e.activation(out=rb[:, 0:512], in_=rb[:, 0:512], func=AF.Exp, scale=-0.5),
                 reads=[rbn], writes=[rbn])
            if use_rope:
                s.op("pool", lambda e: e.tensor_tensor(out=c_t[:, 0:512], in0=qg_t[:, 0:512], in1=rope[:, 0, :], op=ALU.mult),
                     reads=qgn + ["rope"], writes=cn)
                s.op("dve", lambda e: e.tensor_tensor(out=d_t[:, 0:512], in0=banks[rotb][:], in1=rope[:, 1, :], op=ALU.mult),
                     reads=["ps%d" % rotb, "rope2"], writes=dn)
                s.op("dve", lambda e: e.tensor_tensor(out=c_t[:, 0:512], in0=c_t[:, 0:512], in1=d_t[:, 0:512], op=ALU.add),
                     reads=cn + dn, writes=cn)
                src, srcn = c_t[:, 0:512], cn
            else:
                src, srcn = qg_t[:, 0:512], qgn
            if isinstance(out_ap, tuple):
                s.op("dve", lambda e: e.tensor_tensor(out=out_ap[0], in0=src[0:64, :], in1=rb[0:64, 0:512], op=ALU.mult),
                     reads=srcn + [rbn], writes=[out_res])
                s.op("dve", lambda e: e.tensor_tensor(out=out_ap[1], in0=src[64:128, :], in1=rb[64:128, 0:512], op=ALU.mult),
                     reads=srcn + [rbn], writes=[out_res])
            else:
                s.op("dve", lambda e: e.tensor_tensor(out=out_ap, in0=src, in1=rb[:, 0:512], op=ALU.mult),
                     reads=srcn + [rbn], writes=[out_res])

        def kv_pass(l, kcol0, vt0, sample, g=None, prompt_tiles=None):
            if sample:
                load_rope(g)
            b = proj_fm(KO, wres=WIRA)
            qk_post(b, C_KG, l, KT[:, kcol0:kcol0 + 512], "KT%d" % (kcol0 // 512), sample)
            for i in range(4):
                bb = next_acc()
                ncol = 128 if sample else 256
                c0 = VO if sample else KO

                def mm(e, bb=bb, i=i, ncol=ncol, c0=c0):
                    last = None
                    for k in range(8):
                        last = e.matmul(banks[bb][:, 0:ncol], lhsT=xn[:, k, i * 128:(i + 1) * 128], rhs=wi[:, k, c0:c0 + ncol],
                                        start=(k == 0), stop=(k == 7))
                    return last
                s.op("pe", mm, reads=WIRA + XNR, writes=["ps%d" % bb])
                vt = vt0 + i
                voff = 0 if sample else 128
                vout = Vst[:, vt, :].rearrange("p (a d) -> p a d", a=3)
                s.op("act", lambda e, bb=bb, vout=vout, voff=voff: e.activation(
                    out=vout[:, 0:3:2, :], in_=banks[bb][:, voff:voff + 128].rearrange("p (a d) -> p a d", a=2), func=AF.Copy),
                    reads=["ps%d" % bb], writes=["V%d" % vt])
                if not sample:
                    sq, ti = divmod(i, 2)
                    s.op("dve", lambda e, bb=bb: e.tensor_copy(out=nvs[:], in_=banks[bb][:, 128:256]),
                         reads=["ps%d" % bb], writes=["nvs"])
                    s.dma("act", lambda e, sq=sq, ti=ti, l=l: e.dma_start(out=nv_d[sq, l, ti * 128:(ti + 1) * 128, :], in_=nvs[:]),
                          "nvo", reads=["nvs"])
                    s.op("dve", lambda e: e.memset(ssk[:], 0.0), writes=["ssk"])
                    for kv in range(2):
                        s.op("act", lambda e, bb=bb, kv=kv: e.activation(
                            out=tE[:, kv * 64:(kv + 1) * 64], in_=banks[bb][:, kv * 64:(kv + 1) * 64], func=AF.Square,
                            accum_out=ssk[:, kv:kv + 1]),
                            reads=["ps%d" % bb, "ssk"], writes=["tE", "sskv%d" % kv])
                    s.op("act", lambda e: e.activation(out=lnk[:], in_=ssk[:], func=AF.Ln, scale=1.0 / 64.0, bias=epsc[:, 0:1]),
                         reads=["sskv0", "sskv1", "epsc"], writes=["lnk"])
                    s.op("act", lambda e: e.activation(out=rk[:], in_=lnk[:], func=AF.Exp, scale=-0.5),
                         reads=["lnk"], writes=["rk"])
                    for kv in range(2):
                        s.op("dve", lambda e, bb=bb, kv=kv, l=l: e.scalar_tensor_tensor(
                            out=nks[:, kv * 64:(kv + 1) * 64], in0=banks[bb][:, kv * 64:(kv + 1) * 64], scalar=rk[:, kv:kv + 1],
                            in1=kgbs[:, l, kv * 64:(kv + 1) * 64], op0=ALU.mult, op1=ALU.mult),
                            reads=["ps%d" % bb, "rk", "kgbs"], writes=["nks%d" % kv])
                    s.dma("act", lambda e, sq=sq, ti=ti, l=l: e.dma_start(out=nk_d[sq, l, ti * 128:(ti + 1) * 128, :], in_=nks[:]),
                          "nko", reads=["nks0", "nks1"])
            if sample:
                s.op("pool", lambda e, g=g: e.tensor_copy(out=xe[:, :, g, 0, :], in_=xn[:, :, 0:8]), reads=XNR, writes=["xe%d" % g])
                s.op("pool", lambda e, g=g: e.tensor_copy(out=xe[:, :, g, 1, :], in_=xn[:, :, 504:512]), reads=XNR,
                     writes=["xe%db" % g])

        def cache_kv(l):
            s.dma("act", lambda e, l=l: e.dma_start(out=cks, in_=ck_d[l].rearrange("(t p) c -> p t c", p=128)), "ckl",
                  writes=["tA"])
            s.dma("act", lambda e, l=l: e.dma_start(out=cvs, in_=cv_d[l].rearrange("(t p) c -> p t c", p=128)), "cvl",
                  writes=["tB"])
            s.op("dve", lambda e: e.tensor_copy(out=ckb.rearrange("p (t c) -> p t c", t=2), in_=cks), reads=["tA"], writes=["tE"])

            def tr(e):
                e.transpose(bankb[7][:, 0:128], ckb[:, 0:128], identb[:])
                return e.transpose(bankb[7][:, 128:256], ckb[:, 128:256], identb[:])
            s.op("pe", tr, reads=["tE", "identb"], writes=["ps7"])
            s.op("act", lambda e: e.activation(out=KT[:, 2048:2304], in_=bankb[7][:, 0:256], func=AF.Copy),
                 reads=["ps7"], writes=["KTc"])
            for i in range(2):
                vout = Vst[:, 16 + i, :].rearrange("p (a d) -> p a d", a=3)
                s.op("dve", lambda e, i=i, vout=vout: e.tensor_copy(
                    out=vout[:, 0:3:2, :], in_=cvs[:, i, :].rearrange("p (a d) -> p a d", a=2)),
                    reads=["tB"], writes=["V%d" % (16 + i)])

        def step_b(l, sample, g=None, inter=None, after_inter=None):
            inter = list(inter or [])

            def pop_inter(k=1):
                for _ in range(k):
                    if inter:
                        inter.pop(0)()
            nseg, L = (1, 512) if sample else (2, 256)
            W = L + 16

            def seg(t):
                return t[:, 0:nseg * W].rearrange("p (s w) -> p s w", s=nseg)

            def pv(b):
                return banks[b][:].rearrange("p (s w) -> p s w", s=nseg)

            def fl(t):
                return t.rearrange("p (s w) -> p s w", s=nseg)

            if sample:
                s.dma("act", lambda e, g=g: e.dma_start(out=prc, in_=prs_d[:, :, g * 512:(g + 1) * 512]), "prc", writes=["ptb0", "ptb1", "ptb2", "ptb3"])
                load_rope(g)
                gl, ml = [(3, 0), (0, 2), (1, 1), (2, 2)][g]
                gr, mr = [(1, 2), (2, 1), (3, 2), (0, 0)][g]
                s.op("pool", lambda e, gl=gl: e.tensor_copy(out=xh[:, :, 0:8], in_=xe[:, :, gl, 1, :]), reads=["xe%db" % gl],
                     writes=["xha"])
                s.op("pool", lambda e, gr=gr: e.tensor_copy(out=xh[:, :, 8:16], in_=xe[:, :, gr, 0, :]), reads=["xe%d" % gr],
                     writes=["xhb"])

                def hm(e):
                    last = None
                    for idx, c0 in enumerate([HCO, HCO + 128, CCO, CCO + 128, UPO, UPO + 128]):
                        for k in range(8):
                            last = e.matmul(banks[7][:, idx * 16:(idx + 1) * 16], lhsT=wi[:, k, c0:c0 + 128], rhs=xh[:, k, :],
                                            start=(k == 0), stop=(k == 7))
                    return last
                s.op("pe", hm, reads=WIR + ["xha", "xhb"], writes=["ps7"])
                s.op("act", lambda e: e.activation(out=hhs[:], in_=banks[7][:, 0:32], func=AF.Copy), reads=["ps7"], writes=["hhs"])
                s.op("dve", lambda e: e.tensor_tensor(out=uh[:], in0=hhs[:], in1=banks[7][:, 32:64], op=ALU.mult),
                     reads=["hhs", "ps7"], writes=["uh"])
                s.op("dve", lambda e: e.tensor_copy(out=uph[:], in_=banks[7][:, 64:96]), reads=["ps7"], writes=["uph"])
            else:
                s.dma("act", lambda e: e.dma_start(out=prc, in_=prp_d), "prc", writes=["ptb0", "ptb1", "ptb2", "ptb3"])

            bq = [proj_fm(QO + j * 128) for j in range(4)]
            for j in range(4):
                qk_post(bq[j], C_QG, l, (qT[0:64, j, 0, :], qT[64:128, j, 1, :]), "qT%d" % j, sample)
                pop_inter(1)
            for j in range(4):
                b = proj_fm(ZAO + j * 128)
                s.op("act", lambda e, b=b, j=j: e.activation(out=ga[:, j, :], in_=banks[b][:], func=AF.Silu),
                     reads=["ps%d" % b], writes=["ga%d" % j])
                pop_inter(1)
            pop_inter(99)
            if after_inter is not None:
                after_inter()
            for jj in range(2):
                cw = C_CW + (l * 2 + jj) * 3
                if jj == 0:
                    h_t, hn, u_t, un, y_t, yn, z_t, zn = tA, ["tA"], tB, ["tB"], tC, ["tC"], tF, ["tF"]
                    uln, urn = "tBl", "tBr"
                else:
                    h_t, hn, u_t, un, y_t, yn, z_t, zn = pA1, ["p1sa"], pB1, ["p1ub"], pC1, ["p1wc"], pE1, ["p1dt"]
                    uln, urn = "p1ubl", "p1ubr"

                def sgu(t=u_t):
                    return t[:, 0:nseg * W].rearrange("p (s w) -> p s w", s=nseg)
                b = proj_fm(HCO + jj * 128)
                s.op("act", lambda e, b=b, h_t=h_t: e.activation(out=h_t[:, 0:512], in_=banks[b][:], func=AF.Copy),
                     reads=["ps%d" % b], writes=hn)
                b = proj_fm(CCO + jj * 128)
                s.op("dve", lambda e, b=b, h_t=h_t, sgu=sgu: e.tensor_tensor(out=sgu()[:, :, 8:8 + L], in0=pv(b), in1=fl(h_t[:, 0:512]),
                                                                          op=ALU.mult),
                     reads=["ps%d" % b] + hn, writes=un)
                if sample:
                    s.op("dve", lambda e, jj=jj, u_t=u_t: e.tensor_scalar(
                        out=u_t[:, 7:8], in0=uh[:, jj * 16 + 7:jj * 16 + 8], scalar1=col(C_MSK + ml), scalar2=None, op0=ALU.mult),
                        reads=["uh", "cst"], writes=[uln])
                    s.op("dve", lambda e, jj=jj, u_t=u_t: e.tensor_scalar(
                        out=u_t[:, 8 + L:9 + L], in0=uh[:, jj * 16 + 8:jj * 16 + 9], scalar1=col(C_MSK + mr), scalar2=None, op0=ALU.mult),
                        reads=["uh", "cst"], writes=[urn])
                else:
                    s.op("dve", lambda e, sgu=sgu: e.memset(sgu()[:, :, 7:8], 0.0), writes=[uln])
                    s.op("dve", lambda e, sgu=sgu: e.memset(sgu()[:, :, 8 + L:9 + L], 0.0), writes=[urn])
                UA = un + [uln, urn]
                s.op("dve", lambda e, cw=cw, jj=jj, sgu=sgu, y_t=y_t: e.tensor_scalar(
                    out=fl(y_t[:, 0:512]), in0=sgu()[:, :, 7:7 + L], scalar1=col(cw), scalar2=col(C_CB + l * 2 + jj),
                    op0=ALU.mult, op1=ALU.add),
                    reads=UA + ["cst"], writes=yn)
                s.op("dve", lambda e, cw=cw, sgu=sgu, y_t=y_t: e.scalar_tensor_tensor(
                    out=fl(y_t[:, 0:512]), in0=sgu()[:, :, 8:8 + L], scalar=col(cw + 1), in1=fl(y_t[:, 0:512]),
                    op0=ALU.mult, op1=ALU.add),
                    reads=UA + yn + ["cst"], writes=yn)
                s.op("dve", lambda e, cw=cw, sgu=sgu, y_t=y_t: e.scalar_tensor_tensor(
                    out=fl(y_t[:, 0:512]), in0=sgu()[:, :, 9:9 + L], scalar=col(cw + 2), in1=fl(y_t[:, 0:512]),
                    op0=ALU.mult, op1=ALU.add),
                    reads=UA + yn + ["cst"], writes=yn)
                b = proj_fm(BCO + jj * 128)
                s.op("dve", lambda e, b=b, y_t=y_t: e.tensor_tensor(out=y_t[:, 0:512], in0=y_t[:, 0:512], in1=banks[b][:], op=ALU.mult),
                     reads=["ps%d" % b] + yn, writes=yn)
                b = proj_fm(ZCO + jj * 128)
                s.op("act", lambda e, b=b, z_t=z_t: e.activation(out=z_t[:, 0:512], in_=banks[b][:], func=AF.Silu),
                     reads=["ps%d" % b], writes=zn)
                s.op("pool", lambda e, jj=jj, y_t=y_t, z_t=z_t: e.tensor_tensor(out=mix[:, 4 + jj, :], in0=y_t[:, 0:512], in1=z_t[:, 0:512],
                                                                             op=ALU.mult),
                     reads=yn + zn, writes=["mix%d" % (4 + jj)])
            def pool_chain(c, eng, ub_t, wc_t, wd_t, sa_t, dt_t, N):
                def sg(t):
                    return t[:, 0:nseg * W].rearrange("p (s w) -> p s w", s=nseg)
                UBN = [N["ub"], N["ubl"], N["ubr"]]
                if sample:
                    s.op(eng, lambda e: e.tensor_scalar(
                        out=ub_t[:, 0:8], in0=uph[:, c * 16:c * 16 + 8], scalar1=col(C_MSK + ml), scalar2=None, op0=ALU.mult),
                        reads=["uph", "cst"], writes=[N["ubl"]])
                    s.op(eng, lambda e: e.tensor_scalar(
                        out=ub_t[:, 8 + L:16 + L], in0=uph[:, c * 16 + 8:c * 16 + 16], scalar1=col(C_MSK + mr), scalar2=None,
                        op0=ALU.mult),
                        reads=["uph", "cst"], writes=[N["ubr"]])
                else:
                    s.op(eng, lambda e: e.memset(sg(ub_t)[:, :, 0:8], 0.0), writes=[N["ubl"]])
                    s.op(eng, lambda e: e.memset(sg(ub_t)[:, :, 8 + L:16 + L], 0.0), writes=[N["ubr"]])
                s.op(eng, lambda e: e.tensor_tensor(out=sg(wc_t)[:, :, 1:W], in0=sg(ub_t)[:, :, 0:W - 1], in1=sg(ub_t)[:, :, 1:W],
                                                    op=ALU.add),
                     reads=UBN, writes=[N["wc"]])
                if c == 0:
                    s.op(eng, lambda e: e.tensor_tensor(out=sg(wd_t)[64:128, :, 2:W - 1], in0=sg(wc_t)[64:128, :, 1:W - 2],
                                                        in1=sg(wc_t)[64:128, :, 3:W], op=ALU.add),
                         reads=[N["wc"]], writes=[N["wd"]])
                else:
                    s.op(eng, lambda e: e.tensor_tensor(out=sg(wd_t)[:, :, 2:W - 1], in0=sg(wc_t)[:, :, 1:W - 2],
                                                        in1=sg(wc_t)[:, :, 3:W], op=ALU.add),
                         reads=[N["wc"]], writes=[N["wd"]])
                    s.op(eng, lambda e: e.tensor_tensor(out=sg(wc_t)[:, :, 4:W - 3], in0=sg(wd_t)[:, :, 2:W - 5],
                                                        in1=sg(wd_t)[:, :, 6:W - 1], op=ALU.add),
                         reads=[N["wd"]], writes=[N["wc"]])
                    s.op(eng, lambda e: e.tensor_tensor(out=sg(wd_t)[64:128, :, 8:W - 7], in0=sg(wc_t)[64:128, :, 4:W - 11],
                                                        in1=sg(wc_t)[64:128, :, 12:W - 3], op=ALU.add),
                         reads=[N["wc"]], writes=[N["wd"]])
                PTBA = ["ptb0", "ptb1", "ptb2", "ptb3"]
                s.op(eng, lambda e: e.tensor_tensor(out=fl(sa_t[0:64, 0:512]), in0=sg(wc_t)[0:64, :, 8:8 + L],
                                                    in1=fl(prc[0:64, c, :]), op=ALU.mult),
                     reads=[N["wc"]] + PTBA, writes=[N["sa"]])
                s.op(eng, lambda e: e.tensor_tensor(out=fl(sa_t[64:128, 0:512]), in0=sg(wd_t)[64:128, :, 8:8 + L],
                                                    in1=fl(prc[64:128, c, :]), op=ALU.mult),
                     reads=[N["wd"]] + PTBA, writes=[N["sa"]])
                s.op(eng, lambda e: e.tensor_tensor(out=fl(dt_t[:, 0:512]), in0=fl(sa_t[:, 0:512]), in1=sg(ub_t)[:, :, 8:8 + L],
                                                    op=ALU.subtract),
                     reads=[N["sa"]] + UBN, writes=[N["dt"]])

            bu1 = proj_fm(UPO + 128)
            bu0 = proj_fm(UPO)
            bz0 = proj_fm(ZPO)
            bz1 = proj_fm(ZPO + 128)
            s.op("act", lambda e: e.activation(out=pB1[:, 0:nseg * W].rearrange("p (s w) -> p s w", s=nseg)[:, :, 8:8 + L],
                                               in_=pv(bu1), func=AF.Copy),
                 reads=["ps%d" % bu1], writes=["p1ub"])
            s.op("act", lambda e: e.activation(out=seg(tB)[:, :, 8:8 + L], in_=pv(bu0), func=AF.Copy),
                 reads=["ps%d" % bu0], writes=["tB"])
            pool_chain(1, "pool", pB1, pC1, pD1, pA1, pE1,
                       dict(ub="p1ub", ubl="p1ubl", ubr="p1ubr", wc="p1wc", wd="p1wd", sa="p1sa", dt="p1dt"))
            pool_chain(0, "dve", tB, tC, tD, tA, tE,
                       dict(ub="tB", ubl="tBl", ubr="tBr", wc="tC", wd="tD", sa="tA", dt="tE"))
            s.op("act", lambda e: e.activation(out=tF[:], in_=banks[bz0][:], func=AF.Silu), reads=["ps%d" % bz0], writes=["tF"])
            s.op("pe", lambda e: e.matmul(banks[2][:], lhsT=pwb[:, l * 2 + 0, :], rhs=tE[:], start=True, stop=True),
                 reads=["pwb", "tE"], writes=["ps2"])
            s.op("dve", lambda e: e.scalar_tensor_tensor(
                out=mix[:, 6, :], in0=banks[2][:], scalar=col(C_PS + l * 2 + 0), in1=tF[:], op0=ALU.mult, op1=ALU.mult),
                reads=["ps2", "tF", "cst"], writes=["mix6"])
            s.op("act", lambda e: e.activation(out=tA[:, 0:512], in_=banks[bz1][:], func=AF.Silu),
                 reads=["ps%d" % bz1], writes=["tA"])
            s.op("pe", lambda e: e.matmul(banks[3][:], lhsT=pwb[:, l * 2 + 1, :], rhs=pE1[:, 0:512], start=True, stop=True),
                 reads=["pwb", "p1dt"], writes=["ps3"])
            s.op("dve", lambda e: e.scalar_tensor_tensor(
                out=mix[:, 7, :], in0=banks[3][:], scalar=col(C_PS + l * 2 + 1), in1=tA[:, 0:512], op0=ALU.mult, op1=ALU.mult),
                reads=["ps3", "tA", "cst"], writes=["mix7"])

        import os as _os2
        WARM = int(_os2.environ.get("KWARM", "12"))
        FILL = int(_os2.environ.get("KFILL", "4"))
        po_i = [0]
        KTALL = ["KT0", "KT1", "KT2", "KT3", "KTc"]

        def attention(q0, nq, kts, tail_fill=0):
            n = len(kts) // 2
            stpairs = (1, 3)
            pending = [None]

            def make_epilogue(j, pos, on_act=False):
                pa, pb_ = pos
                if nq >= 512 and not on_act:
                    acts = [
                        lambda: s.op("dve", lambda e: e.reciprocal(out=tB[64:128, 0:nq], in_=banks[pa][64:128, 0:nq]),
                                     reads=["ps%d" % pa], writes=["tB"]),
                        lambda: s.op("dve", lambda e: e.reciprocal(out=tB[0:64, 0:nq], in_=banks[pb_][0:64, 0:nq]),
                                     reads=["ps%d" % pb_], writes=["tB"]),
                    ]
                else:
                    acts = [
                        lambda: s.op("act", lambda e: e.activation(out=tA[64:128, 0:nq], in_=banks[pa][64:128, 0:nq], func=AF.Ln),
                                     reads=["ps%d" % pa], writes=["tA"]),
                        lambda: s.op("act", lambda e: e.activation(out=tA[0:64, 0:nq], in_=banks[pb_][0:64, 0:nq], func=AF.Ln),
                                     reads=["ps%d" % pb_], writes=["tA"]),
                        lambda: s.op("act", lambda e: e.activation(out=tB[:, 0:nq], in_=tA[:, 0:nq], func=AF.Exp, scale=-1.0),
                                     reads=["tA"], writes=["tB"]),
                    ]

                def rest():
                    s.dma("act", lambda e: e.dma_start(out=tD[0:64, 0:nq], in_=tB[64:128, 0:nq]), "swpA", reads=["tB"], writes=["tD"])
                    s.dma("act", lambda e: e.dma_start(out=tD[64:128, 0:nq], in_=tB[0:64, 0:nq]), "swpB", reads=["tB"], writes=["tDb"])
                    s.op("dve", lambda e: e.tensor_tensor(out=tC[0:64, 0:nq], in0=banks[pa][0:64, 0:nq],
                                                          in1=ga[0:64, j, q0:q0 + nq], op=ALU.mult),
                         reads=["ps%d" % pa, "ga%d" % j], writes=["tC"])
                    s.op("dve", lambda e: e.tensor_tensor(out=tC[64:128, 0:nq], in0=banks[pb_][64:128, 0:nq],
                                                          in1=ga[64:128, j, q0:q0 + nq], op=ALU.mult),
                         reads=["ps%d" % pb_, "ga%d" % j], writes=["tC"])
                    s.op("dve", lambda e: e.tensor_tensor(out=mix[:, j, q0:q0 + nq], in0=tC[:, 0:nq], in1=tD[:, 0:nq], op=ALU.mult),
                         reads=["tC", "tD", "tDb"], writes=["mix%d" % j])
                return acts, rest

            def flush():
                if pending[0] is not None:
                    acts, rest = pending[0]
                    for a in acts:
                        a()
                    rest()
                    pending[0] = None

            items = [(j, ab, p) for j in range(4) for ab in range(2) for p in range(n)]
            base = po_i[0]
            po_i[0] += 4

            def pos_of(j):
                return (4, 5) if (base + j) % 2 == 0 else (0, 1)

            def qk(idx):
                j, ab, p = items[idx]
                st = stpairs[idx % 2]

                def f(e):
                    e.matmul(pp[st][:, 0:nq], lhsT=KT[:, kts[2 * p][0]:kts[2 * p][0] + 128],
                             rhs=qT[:, j, ab, q0:q0 + nq], start=True, stop=True)
                    return e.matmul(pp[st][:, 512:512 + nq], lhsT=KT[:, kts[2 * p + 1][0]:kts[2 * p + 1][0] + 128],
                                    rhs=qT[:, j, ab, q0:q0 + nq], start=True, stop=True)
                s.op("pe", f, reads=KTALL + ["qT%d" % j, "qTz"], writes=["ps%d" % (2 * st), "ps%d" % (2 * st + 1)])

            def ex(idx):
                st = stpairs[idx % 2]
                sl = 2 * (idx % 2)
                s.op("act", lambda e: e.activation(
                    out=ptb[:, sl:sl + 2, 0:nq], in_=pp[st][:].rearrange("p (b n) -> p b n", b=2)[:, :, 0:nq],
                    func=AF.Exp, scale=0.125),
                    reads=["ps%d" % (2 * st), "ps%d" % (2 * st + 1)], writes=["ptb%d" % sl, "ptb%d" % (sl + 1)])

            def pvm(idx):
                j, ab, p = items[idx]
                po = pos_of(j)[ab]
                lo = 0 if ab == 0 else 64
                sl = 2 * (idx % 2)

                def f(e):
                    e.matmul(banks[po][:, 0:nq], lhsT=Vst[:, kts[2 * p][1], lo:lo + 128], rhs=ptb[:, sl, 0:nq],
                             start=(p == 0), stop=False)
                    return e.matmul(banks[po][:, 0:nq], lhsT=Vst[:, kts[2 * p + 1][1], lo:lo + 128], rhs=ptb[:, sl + 1, 0:nq],
                                    start=False, stop=(p == n - 1))
                s.op("pe", f, reads=["V%d" % kts[2 * p][1], "V%d" % kts[2 * p + 1][1], "Vones", "ptb%d" % sl, "ptb%d" % (sl + 1)],
                     writes=["ps%d" % po])

            qk(0)
            for idx, (j, ab, p) in enumerate(items):
                if idx + 1 < len(items):
                    qk(idx + 1)
                if idx == 0 and nq == 512 and WARM > 0:
                    def burst(e):
                        last = None
                        for r in range(WARM):
                            last = e.matmul(banks[pos_of(0)[1]][:, 0:512], lhsT=bonesb[:], rhs=mix[:, 4 + (r % 4), :], start=True, stop=True)
                        return last
                    s.op("pe", burst, reads=["bonesb", "mix4", "mix5", "mix6", "mix7"], writes=["ps%d" % pos_of(0)[1]])
                ex(idx)
                defer = (n >= 6 and ab == 0 and pending[0] is not None)
                if defer and p in (1, 2):
                    pending[0][0][p - 1]()
                if nq == 512 and j == 0 and ab == 0 and p < 4 and FILL > 0:
                    def fill(e, j=j):
                        last = None
                        for r in range(FILL):
                            last = e.matmul(banks[pos_of(j)[1]][:, 0:512], lhsT=bonesb[:], rhs=mix[:, 4 + (r % 4), :],
                                            start=True, stop=True)
                        return last
                    s.op("pe", fill, reads=["bonesb", "mix4", "mix5", "mix6", "mix7"], writes=["ps%d" % pos_of(j)[1]])
                pvm(idx)
                if defer and p == 4:
                    pending[0][1]()
                    pending[0] = None
                if n < 6 and ab == 0 and p == n - 1:
                    flush()
                if ab == 1 and p == n - 1:
                    pending[0] = make_epilogue(j, pos_of(j), on_act=(j == 3))
            if tail_fill > 0:
                def tfill(e):
                    last = None
                    for r in range(tail_fill):
                        last = e.matmul(pp[1][:, 0:512], lhsT=bonesb[:], rhs=KT[:, (r % 4) * 512:(r % 4 + 1) * 512], start=True, stop=True)
                    return last
                s.op("pe", tfill, reads=["bonesb"] + KTALL, writes=["ps2", "ps3"])
            flush()

        MIXR = ["mix%d" % j for j in range(8)]

        def out_proj_units(tiles, final, out_d, row0):
            units = []
            for i, t in enumerate(tiles):
                for nn in range(2):
                    def unit(i=i, t=t, nn=nn):
                        b = next_acc()

                        def mm(e, b=b):
                            last = None
                            for k in range(8):
                                last = e.matmul(banks[b][:], lhsT=mix[:, k, i * 128:(i + 1) * 128], rhs=wo[:, k, nn * 512:(nn + 1) * 512],
                                                start=(k == 0), stop=(k == 7))
                            return last
                        s.op("pe", mm, reads=MIXR + WOR, writes=["ps%d" % b])
                        s.op("dve", lambda e, b=b: e.tensor_tensor(
                            out=xb[:, t, nn * 512:(nn + 1) * 512], in0=banks[b][:], in1=xb[:, t, nn * 512:(nn + 1) * 512], op=ALU.add),
                            reads=["ps%d" % b, XBR[t]], writes=[XBR[t]])
                        if final and nn == 1:
                            s.op("dve", lambda e: e.memset(ms[:, t:t + 1], 0.0), writes=["msf%d" % t])
                            s.op("act", lambda e: e.activation(out=ybf[:], in_=xb[:, t, :], func=AF.Square, scale=1.0 / 32.0,
                                                               accum_out=ms[:, t:t + 1]),
                                 reads=[XBR[t], "msf%d" % t], writes=["ybf", "msfv%d" % t])
                            s.op("act", lambda e: e.activation(out=lnm[:, t:t + 1], in_=ms[:, t:t + 1], func=AF.Ln, bias=epsc[:, 0:1]),
                                 reads=["msfv%d" % t, "epsc"], writes=["lnf%d" % t])
                            s.op("act", lambda e: e.activation(out=rstd[:, t:t + 1], in_=lnm[:, t:t + 1], func=AF.Exp, scale=-0.5),
                                 reads=["lnf%d" % t], writes=["rsf%d" % t])
                            s.op("dve", lambda e: e.scalar_tensor_tensor(out=xb[:, t, :], in0=xb[:, t, :], scalar=rstd[:, t:t + 1],
                                                                         in1=gbc, op0=ALU.mult, op1=ALU.mult),
                                 reads=[XBR[t], "rsf%d" % t, "rope", "rope2"], writes=[XBR[t]])
                            s.dma("act", lambda e: e.dma_start(out=out_d[row0 + i * 128:row0 + (i + 1) * 128, :], in_=xb[:, t, :]),
                                  "yo%d" % (i % 2), reads=[XBR[t]])
                    units.append(unit)
            return units

        def out_proj(tiles, final, out_d, row0):
            for u in out_proj_units(tiles, final, out_d, row0):
                u()

        def load_x(src_d, g_src, g_dst):
            s.dma("sp", lambda e: e.dma_start(out=xb[:, 4 * g_dst:4 * g_dst + 4, :],
                                              in_=src_d[g_src * 512:(g_src + 1) * 512, :].rearrange("(t p) d -> p t d", p=128)),
                  "xg%d" % g_dst, writes=XBR[4 * g_dst:4 * g_dst + 4])

        def load_fgb():
            s.dma("act", lambda e: e.dma_start(out=gbc, in_=fgb_d), "gbc", writes=["rope", "rope2"])

        import os
        STAGE = os.environ.get("KSTAGE", "all")

        def program():
            PT = [12, 13, 14, 15]
            load_x(xp_d, 0, 3)
            mod_begin(0)
            for k in range(8):
                mod_chunk(0, k)
                prep_wi_k(0, k, "dve" if k % 2 == 0 else "pool", q="act")
            mod_end(0)
            mod_layer(1)
            if STAGE == "mod":
                return
            if STAGE == "prep":
                return
            for l in range(2):
                make_xn(PT, l, 1)
                if STAGE == "p_xn":
                    return
                kv_pass(l, 0, 0, sample=False)
                if l == 0:
                    prep_wo(0, 1, "dve")
                    for g in range(3):
                        load_x(xs_d, g, g)
                if STAGE == "p_kv":
                    return
                step_b(l, sample=False)
                if STAGE == "p_b":
                    return
                if l == 0:
                    prep_wi(1, "pool")
                else:
                    prep_wi(0, "pool", kv_first=True)
                for sq in range(2):
                    attention(sq * 256, 256, [(sq * 256, 2 * sq), (sq * 256 + 128, 2 * sq + 1)])
                if STAGE == "p_att":
                    return
                if l == 1:
                    load_fgb()
                out_proj(PT, final=(l == 1), out_d=yp_d, row0=0)
                if l == 1:
                    for g in range(3):
                        make_stats([4 * g + i for i in range(4)], 0, 0)
                    cache_kv(0)
                    for g in (1, 2):
                        make_xn([4 * g + i for i in range(4)], 0, 0)
                        kv_pass(0, g * 512, 4 * g, sample=True, g=g)
                if STAGE == "p_out":
                    return
                if l == 0:
                    prep_wo(1, 1, "pool")
                else:
                    prep_wo(0, 0, "pool")
            if STAGE == "p":
                return
            load_x(xs_d, 3, 3)
            SKT = [(t * 128, t) for t in range(18)]
            for g in (0, 3):
                make_xn([4 * g + i for i in range(4)], 0, 0)
                kv_pass(0, g * 512, 4 * g, sample=True, g=g)
            if STAGE == "s0kv":
                return
            order0 = [3, 0, 1, 2]
            pend_units = None
            for n_, g in enumerate(order0):
                tiles = [4 * g + i for i in range(4)]
                ai = None
                if pend_units is not None:
                    gp = order0[n_ - 1]
                    ai = (lambda gp=gp: make_stats([4 * gp + i for i in range(4)], 1, 0))
                step_b(0, sample=True, g=g, inter=pend_units, after_inter=ai)
                pend_units = None
                if STAGE == "s0b":
                    return
                if n_ == 3:
                    prep_wi(1)
                else:
                    make_xn([4 * order0[n_ + 1] + i for i in range(4)], 0, 0)
                attention(0, 512, SKT, tail_fill=(28 if n_ == 3 else 0))
                if STAGE == "s0a":
                    return
                if n_ < 3:
                    pend_units = out_proj_units(tiles, final=False, out_d=None, row0=0)
                else:
                    out_proj(tiles, final=False, out_d=None, row0=0)
            prep_wo(1, 0)
            if STAGE == "s0":
                return
            cache_kv(1)
            for g in [2, 3, 0, 1]:
                make_xn([4 * g + i for i in range(4)], 1, 0)
                kv_pass(1, g * 512, 4 * g, sample=True, g=g)
            first = True
            for g in [1, 0]:
                tiles = [4 * g + i for i in range(4)]
                step_b(1, sample=True, g=g)
                if first:
                    make_xn([0, 1, 2, 3], 1, 0)
                attention(0, 512, SKT, tail_fill=28)
                load_fgb()
                first = False
                out_proj(tiles, final=True, out_d=ys_d, row0=g * 512)
        program()
        s.final_wait("sp")
        _DBG["sched"] = s
        s.replay(block)
    return nc


def _const_mats():
    ident = np.eye(128, dtype=np.float32)
    rm = np.zeros((128, 128), np.float32)
    for d in range(128):
        if (d % 32) < 16:
            rm[d + 16, d] = -1.0
        else:
            rm[d - 16, d] = 1.0
    bones = np.zeros((128, 128), np.float32)
    bones[0:64, 0:64] = 1.0
    bones[64:128, 64:128] = 1.0
    swp = np.zeros((128, 128), np.float32)
    swp[np.arange(128), (np.arange(128) + 64) % 128] = 1.0
    return np.ascontiguousarray(np.stack([ident, rm, bones, swp], axis=1))


def _rope_tables(nat):
    half = 32
    inv = (1.0 / (10000.0 ** (np.arange(0, half, 2, dtype=np.float32) / np.float32(half)))).astype(np.float32)
    row = (nat // 64).astype(np.float32)
    colp = (nat % 64).astype(np.float32)
    c = np.zeros((128, nat.shape[0]), np.float32)
    sn = np.zeros((128, nat.shape[0]), np.float32)
    for q in range(128):
        d = q % 64
        idx = d % 16
        ang = (row if d < 32 else colp) * inv[idx]
        ang = ang.astype(np.float32)
        c[q] = np.cos(ang)
        sn[q] = np.sin(ang)
    return c, sn


def _pool_rc(nat, n):
    wins = (2, 4, 8, 16)
    out = np.zeros((128, 2, nat.shape[0]), np.float32)
    for c in range(2):
        for hf in range(2):
            win = wins[2 * c + hf]
            lo = np.clip(nat - win // 2, 0, n - 1)
            hi = np.clip(nat - win // 2 + win - 1, 0, n - 1)
            cnt = (hi - lo + 1).astype(np.float32)
            out[hf * 64:(hf + 1) * 64, c, :] = (1.0 / cnt)[None, :]
    return out


_NC_CACHE = {}


def kernel(x_prompt, x_sample, cache_k, cache_v, c, c_ctx, norm_g, w_ada, b_ada, w_in,
           q_norm_g, k_norm_g, conv_w, conv_b, pool_w, pool_scale, w_out, final_g):
    f = lambda a: np.ascontiguousarray(np.asarray(a, dtype=np.float32))
    x_prompt, x_sample, cache_k, cache_v = f(x_prompt), f(x_sample), f(cache_k), f(cache_v)
    c, c_ctx, norm_g, w_ada, b_ada, w_in = f(c), f(c_ctx), f(norm_g), f(w_ada), f(b_ada), f(w_in)
    q_norm_g, k_norm_g, conv_w, conv_b = f(q_norm_g), f(k_norm_g), f(conv_w), f(conv_b)
    pool_w, pool_scale, w_out, final_g = f(pool_w), f(pool_scale), f(w_out), f(final_g)

    if "nc" not in _NC_CACHE:
        _NC_CACHE["nc"] = build_nc()
    nc = _NC_CACHE["nc"]

    mats = _const_mats()
    p = np.arange(128)
    pwm = np.zeros((128, 4, 128), np.float32)
    for l in range(2):
        for cc in range(2):
            for hf in range(2):
                pwm[hf * 64:(hf + 1) * 64, l * 2 + cc, hf * 64:(hf + 1) * 64] = pool_w[l, 2 * cc + hf]
    kgb = np.ascontiguousarray(np.broadcast_to(np.tile(k_norm_g, (1, 2))[None, :, :], (128, 2, 128))).astype(np.float32)
    fgb = np.ascontiguousarray(np.broadcast_to(final_g[None, :], (128, D))).astype(np.float32)
    prp = _pool_rc(np.tile(np.arange(256), 2), 256)

    in_maps = []
    for i in range(NCORES):
        b, h = i // 2, i % 2
        own = slice(h * 1024, (h + 1) * 1024)
        oth = slice((1 - h) * 1024, (2 - h) * 1024)
        xs = np.ascontiguousarray(np.concatenate([x_sample[b, own], x_sample[b, oth]], axis=0))
        xp = np.ascontiguousarray(x_prompt[2 * i:2 * i + 2].reshape(512, D))
        nat = np.concatenate([np.arange(h * 1024, (h + 1) * 1024), np.arange((1 - h) * 1024, (2 - h) * 1024)])
        rc, rs = _rope_tables(nat)
        prs = _pool_rc(nat, 2048)
        cst = np.zeros((128, NCST), np.float32)
        for k in range(8):
            cst[:, C_CC + 2 * k] = c[b, k * 128:(k + 1) * 128]
            cst[:, C_CC + 2 * k + 1] = c_ctx[k * 128:(k + 1) * 128]
        for l in range(2):
            for k in range(8):
                cst[:, C_NG + l * 8 + k] = norm_g[l, k * 128:(k + 1) * 128]
            cst[:, C_QG + l] = q_norm_g[l, p % 64]
            cst[:, C_KG + l] = k_norm_g[l, p % 64]
            for jj in range(2):
                for tap in range(3):
                    cst[:, C_CW + (l * 2 + jj) * 3 + tap] = conv_w[l, tap, jj * 128:(jj + 1) * 128]
                cst[:, C_CB + l * 2 + jj] = conv_b[l, jj * 128:(jj + 1) * 128]
                cst[:, C_PS + l * 2 + jj] = pool_scale[l, jj * 128:(jj + 1) * 128]
        cst[:, C_MSK + 0] = float(h)
        cst[:, C_MSK + 1] = float(1 - h)
        cst[:, C_MSK + 2] = 1.0
        in_maps.append({
            "xs": xs, "xp": xp,
            "ck": np.ascontiguousarray(cache_k[b].reshape(2, 256, 128)),
            "cv": np.ascontiguousarray(cache_v[b].reshape(2, 256, 128)),
            "w_ada": w_ada, "b_ada": b_ada, "w_in": w_in, "w_out": w_out,
            "cst": cst, "mats": mats, "pwm": pwm, "kgb": kgb, "fgb": fgb,
            "ropec": rc, "ropes": rs, "prs": prs, "prp": prp,
        })

    res = run_bass_kernel_spmd(nc, in_maps, core_ids=list(range(NCORES)))
    y_prompt = np.zeros((16, 256, D), np.float32)
    y_sample = np.zeros((4, 2048, D), np.float32)
    new_k = np.zeros((16, 2, 256, 2, 64), np.float32)
    new_v = np.zeros((16, 2, 256, 2, 64), np.float32)
    for i in range(NCORES):
        r = res.results[i]
        b, h = i // 2, i % 2
        y_prompt[2 * i:2 * i + 2] = np.asarray(r["yp"]).reshape(2, 256, D)
        y_sample[b, h * 1024:(h + 1) * 1024] = np.asarray(r["ys"])
        new_k[2 * i:2 * i + 2] = np.asarray(r["nk"]).reshape(2, 2, 256, 2, 64)
        new_v[2 * i:2 * i + 2] = np.asarray(r["nv"]).reshape(2, 2, 256, 2, 64)
    return (y_prompt, y_sample, new_k, new_v)
```

```python
import contextlib
import numpy as np
import concourse.bass as bass
import concourse.mybir as mybir
from concourse.bass_utils import run_bass_kernel_spmd

F32 = mybir.dt.float32
BF16 = mybir.dt.bfloat16
F32R = mybir.dt.float32r
ALU = mybir.AluOpType
AF = mybir.ActivationFunctionType

NCORES = 8
D = 1024
QO, KO, VO, ZAO, HCO, BCO, CCO, ZCO, UPO, ZPO = 0, 512, 640, 768, 1280, 1536, 1792, 2048, 2304, 2560
IN_W = 2816
EPS = 1e-6

C_CC, C_NG, C_QG, C_KG, C_CW, C_CB, C_PS, C_MSK = 0, 16, 32, 34, 36, 48, 52, 56
NCST = 64

import os as _osk
STRICT = _osk.environ.get("KSTRICT", "1") == "1"
COMPUTE = ("pe", "act", "dve", "pool")
ALL_ENG = COMPUTE + ("sp",)


_DBG = {}


class Sched:
    def __init__(self, nc, sem_alloc):
        self.nc = nc
        self.sem_alloc = sem_alloc
        self.items = {e: [] for e in ALL_ENG}
        self.sems = {}
        self.eng_cnt = {e: 0 for e in COMPUTE}
        for e in COMPUTE:
            self.sems["c_" + e] = sem_alloc("c_" + e)
        self.known = {e: {} for e in ALL_ENG}
        self.stream_cnt = {}
        self.last_w = {}
        self.readers = {}
        self.all_tokens = {}
        self.nops = 0

    def _deps(self, eng, reads, writes):
        toks = []
        for r in reads:
            t = self.last_w.get(r)
            if t is not None:
                toks.append(t)
        for w in writes:
            t = self.last_w.get(w)
            if t is not None and (t[2] != eng or STRICT):
                toks.append(t)
            for t in self.readers.get(w, ()):
                if t[2] != eng or STRICT:
                    toks.append(t)
        return toks

    def _emit_waits(self, eng, toks):
        need = {}
        for (sname, val, src) in toks:
            if src == eng and eng == "pe":
                continue
            if self.known[eng].get(sname, 0) >= val:
                continue
            if need.get(sname, 0) < val:
                need[sname] = val
        for sname, val in need.items():
            self.known[eng][sname] = val
            self.items[eng].append(("wait", sname, val))

    def _commit(self, tok, reads, writes):
        for r in reads:
            self.readers.setdefault(r, []).append(tok)
        for w in writes:
            self.last_w[w] = tok
            self.readers[w] = []
        self.all_tokens[tok[0]] = max(self.all_tokens.get(tok[0], 0), tok[1])

    def op(self, eng, fn, reads=(), writes=()):
        reads = tuple(reads)
        writes = tuple(writes)
        writes = writes + tuple(r for r in reads if r.startswith("ps") and r not in writes)
        self._emit_waits(eng, self._deps(eng, reads, writes))
        self.eng_cnt[eng] += 1
        tok = ("c_" + eng, self.eng_cnt[eng], eng)
        self.items[eng].append(("op", fn, "c_" + eng, 1))
        self._commit(tok, reads, writes)
        self.nops += 1
        return tok

    def dma(self, q, fn, stream, reads=(), writes=()):
        reads = tuple(reads)
        writes = tuple(writes)
        sname = "d_" + stream
        if sname not in self.sems:
            self.sems[sname] = self.sem_alloc(sname)
            self.stream_cnt[sname] = 0
        self._emit_waits(q, self._deps(None, reads, writes))
        self.stream_cnt[sname] += 16
        tok = (sname, self.stream_cnt[sname], "dma")
        self.items[q].append(("op", fn, sname, 16))
        self._commit(tok, reads, writes)
        return tok

    def final_wait(self, eng="sp"):
        self._emit_waits(eng, [(s, v, "x") for s, v in self.all_tokens.items()])

    def replay(self, block):
        sems = self.sems

        def run(eng_name):
            def body(e):
                for it in self.items[eng_name]:
                    if it[0] == "wait":
                        e.wait_ge(sems[it[1]], it[2])
                    else:
                        it[1](e).then_inc(sems[it[2]], it[3])
            return body
        block.tensor(run("pe"))
        block.scalar(run("act"))
        block.vector(run("dve"))
        block.gpsimd(run("pool"))
        block.sync(run("sp"))


def build_nc():
    nc = bass.Bass("TRN2", target_bir_lowering=False)

    def din(name, shape):
        return nc.dram_tensor(name, list(shape), F32, kind="ExternalInput").ap()

    def dout(name, shape):
        return nc.dram_tensor(name, list(shape), F32, kind="ExternalOutput").ap()

    xs_d = din("xs", [2048, D])
    xp_d = din("xp", [512, D])
    ck_d = din("ck", [2, 256, 128])
    cv_d = din("cv", [2, 256, 128])
    wada_d = din("w_ada", [2, D, 3 * D])
    bada_d = din("b_ada", [2, 3 * D])
    win_d = din("w_in", [2, D, IN_W])
    wout_d = din("w_out", [2, D, D])
    cst_d = din("cst", [128, NCST])
    mats_d = din("mats", [128, 4, 128])
    pwm_d = din("pwm", [128, 4, 128])
    kgb_d = din("kgb", [128, 2, 128])
    fgb_d = din("fgb", [128, D])
    ropec_d = din("ropec", [128, 2048])
    ropes_d = din("ropes", [128, 2048])
    prs_d = din("prs", [128, 2, 2048])
    prp_d = din("prp", [128, 2, 512])
    yp_d = dout("yp", [512, D])
    ys_d = dout("ys", [1024, D])
    nk_d = dout("nk", [2, 2, 256, 128])
    nv_d = dout("nv", [2, 2, 256, 128])
    gsc_d = nc.dram_tensor("gsc", [4, D], F32).ap()

    es = contextlib.ExitStack()
    with es:
        def T(name, shape, dt):
            return es.enter_context(nc.sbuf_tensor(name, list(shape), dt))

        xb = T("xb", [128, 16, D], F32)
        wi = T("wi", [128, 8, IN_W], BF16)
        wo = T("wo", [128, 8, D], BF16)
        stg = T("stg", [128, 2, D], F32)
        xn = T("xn", [128, 8, 512], BF16)
        KT = T("KT", [128, 2304], BF16)
        Vst = T("Vst", [128, 18, 192], BF16)
        qT = T("qT", [128, 4, 2, 512], BF16)
        ga = T("ga", [128, 4, 512], BF16)
        mix = T("mix", [128, 8, 512], BF16)
        ybf2 = T("ybf", [128, 2, D], BF16)
        ybf = ybf2[:, 0, :]
        rope = T("rope", [128, 2, 512], F32)
        ptb = T("ptb", [128, 4, 512], BF16)
        tA = T("tA", [128, 544], F32)
        tB = T("tB", [128, 544], F32)
        tC = T("tC", [128, 544], F32)
        tD = T("tD", [128, 544], F32)
        tE = T("tE", [128, 512], BF16)
        tF = T("tF", [128, 512], BF16)
        pB1 = T("pB1", [128, 544], BF16)
        pC1 = T("pC1", [128, 544], BF16)
        pD1 = T("pD1", [128, 544], BF16)
        pA1 = T("pA1", [128, 512], BF16)
        pE1 = T("pE1", [128, 512], BF16)
        xe = T("xe", [128, 8, 4, 2, 8], BF16)
        xh = T("xh", [128, 8, 16], BF16)
        cst = T("cst_s", [128, NCST], F32)
        matf = T("matf", [128, 4, 128], F32)
        identb = T("identb", [128, 128], BF16)
        rmb = T("rmb", [128, 128], BF16)
        bonesb = T("bonesb", [128, 128], BF16)
        pwb = T("pwb", [128, 4, 128], BF16)
        kgbs = T("kgbs", [128, 2, 128], F32)
        epsc = T("epsc", [128, 1], F32)
        scs = T("scs", [128, 16], F32)
        gsc_s = T("gsc_s", [128, 32], F32)
        shc_s = T("shc_s", [128, 32], F32)
        ms = T("ms", [128, 16], F32)
        lnm = T("lnm", [128, 16], F32)
        rstd = T("rstd", [128, 16], F32)
        ssk = T("ssk", [128, 2], F32)
        lnk = T("lnk", [128, 2], F32)
        rk = T("rk", [128, 2], F32)
        hhs = T("hhs", [128, 32], F32)
        uh = T("uh", [128, 32], F32)
        uph = T("uph", [128, 32], F32)
        nks = T("nks", [128, 128], F32)
        nvs = T("nvs", [128, 128], F32)

        pp = [es.enter_context(nc.psum_tensor("pp%d" % i, [128, 1024], F32)) for i in range(4)]
        banks = [pp[i // 2][:, (i % 2) * 512:(i % 2 + 1) * 512] for i in range(8)]
        bankb = [b.bitcast(BF16) for b in banks]

        identf = matf[:, 0, :]
        swpf = matf[:, 3, :]
        gbc = rope[:].rearrange("p a b -> p (a b)")
        prc = ptb[:].rearrange("p a b -> p (a b)").bitcast(F32).rearrange("p (c n) -> p c n", c=2)
        cks = tA[:, 0:256].rearrange("p (t c) -> p t c", t=2)
        cvs = tB[:, 0:256].rearrange("p (t c) -> p t c", t=2)
        ckb = tE[:, 0:256]

        wst = [xb[:, 0:3, :].rearrange("p a b -> p (a b)"), xb[:, 3:6, :].rearrange("p a b -> p (a b)")]
        mrow = xb[0:2, 6:9, :].rearrange("p a b -> p (a b)")
        brow = xb[0:2, 9:12, :].rearrange("p a b -> p (a b)")
        XBR = ["xb%d" % t for t in range(16)]
        WIRA = ["wi%da" % k for k in range(8)]
        WIRB = ["wi%db" % k for k in range(8)]
        WIR = WIRA + WIRB
        WOR = ["wo%d" % k for k in range(8)]

        block = es.enter_context(nc.Block())
        s = Sched(nc, lambda n: es.enter_context(nc.semaphore(n)))

        def col(c):
            return cst[:, c:c + 1]

        s.dma("sp", lambda e: e.dma_start(out=cst[:], in_=cst_d), "cst", writes=["cst"])
        s.dma("sp", lambda e: e.dma_start(out=matf[:], in_=mats_d), "mats", writes=["matf"])
        s.dma("sp", lambda e: e.dma_start(out=kgbs[:], in_=kgb_d), "kgb", writes=["kgbs"])
        s.dma("sp", lambda e: e.dma_start(out=tA[:, 0:512].rearrange("p (a b) -> p a b", a=4), in_=pwm_d),
              "pwm", writes=["tA"])
        s.op("dve", lambda e: e.tensor_copy(out=identb[:], in_=matf[:, 0, :]), reads=["matf"], writes=["identb"])
        s.op("dve", lambda e: e.tensor_copy(out=rmb[:], in_=matf[:, 1, :]), reads=["matf"], writes=["rmb"])
        s.op("dve", lambda e: e.tensor_copy(out=bonesb[:], in_=matf[:, 2, :]), reads=["matf"], writes=["bonesb"])
        s.op("dve", lambda e: e.tensor_copy(out=pwb[:], in_=tA[:, 0:512].rearrange("p (a b) -> p a b", a=4)),
             reads=["tA"], writes=["pwb"])
        s.op("dve", lambda e: e.memset(epsc[:], EPS), writes=["epsc"])
        s.op("dve", lambda e: e.memset(Vst[:, :, 64:128], 1.0), writes=["Vones"])
        s.op("dve", lambda e: e.memset(qT[64:128, :, 0, :], 0.0), writes=["qTz"])
        s.op("dve", lambda e: e.memset(qT[0:64, :, 1, :], 0.0), writes=["qTz"])
        s.op("act", lambda e: e.activation(out=scs[:], in_=cst[:, C_CC:C_CC + 16], func=AF.Silu),
             reads=["cst"], writes=["scs"])

        def mod_begin(l):
            s.dma("sp", lambda e: e.dma_start(out=brow, in_=bada_d[l:l + 1, :].partition_broadcast(2)), "brow", writes=XBR[9:12])

        def mod_chunk(l, k):
            sl = k % 2
            s.dma("sp", lambda e: e.dma_start(out=wst[sl], in_=wada_d[l, k * 128:(k + 1) * 128, :]),
                  "wst%d" % sl, writes=XBR[3 * sl:3 * sl + 3])

            def mm(e):
                last = None
                for n in range(6):
                    last = e.matmul(banks[n][0:2, :], lhsT=scs[:, 2 * k:2 * k + 2], rhs=wst[sl][:, n * 512:(n + 1) * 512],
                                    start=(k == 0), stop=(k == 7))
                return last
            s.op("pe", mm, reads=["scs"] + XBR[3 * sl:3 * sl + 3], writes=["ps%d" % n for n in range(6)])

        def mod_end(l):
            for n in range(6):
                s.op("dve", lambda e, n=n: e.tensor_tensor(
                    out=mrow[:, n * 512:(n + 1) * 512], in0=banks[n][0:2, :], in1=brow[:, n * 512:(n + 1) * 512], op=ALU.add),
                    reads=["ps%d" % n] + XBR[9:12], writes=XBR[6:9])
            s.dma("act", lambda e: e.dma_start(out=gsc_d[2 * l:2 * l + 2, :], in_=mrow[:, 2048:3072]),
                  "gscw", reads=XBR[6:9], writes=["gsc_d%d" % l])

            def tr(e):
                last = None
                for j in range(16):
                    last = e.transpose(banks[6][:, 2 * j:2 * j + 2], mrow[:, j * 128:(j + 1) * 128], matf[0:2, 0, 0:2])
                return last
            s.op("pe", tr, reads=["matf"] + XBR[6:9], writes=["ps6"])
            tps3 = banks[6][:, 0:32].rearrange("p (j v) -> p j v", v=2)
            for v in range(2):
                lv = l * 2 + v
                s.op("dve", lambda e, v=v, lv=lv: e.tensor_copy(out=shc_s[:, lv * 8:(lv + 1) * 8], in_=tps3[:, 0:8, v]),
                     reads=["ps6"], writes=["shc%d" % lv])
                s.op("dve", lambda e, v=v, lv=lv: e.scalar_tensor_tensor(
                    out=gsc_s[:, lv * 8:(lv + 1) * 8], in0=tps3[:, 8:16, v], scalar=1.0,
                    in1=cst[:, C_NG + l * 8:C_NG + (l + 1) * 8], op0=ALU.add, op1=ALU.mult),
                    reads=["ps6", "cst"], writes=["gsc%d" % lv])

        def mod_layer(l):
            mod_begin(l)
            for k in range(8):
                mod_chunk(l, k)
            mod_end(l)

        stg_i = [0]

        def stage_slot():
            sl = stg_i[0] % 2
            stg_i[0] += 1
            return sl

        def prep_wi(l, ce="pool", kv_first=False):
            if kv_first:
                for k in range(8):
                    prep_wi_k(l, k, ce, pieces=(0,))
                for k in range(8):
                    prep_wi_k(l, k, ce, pieces=(1, 2))
            else:
                for k in range(8):
                    prep_wi_k(l, k, ce)

        def prep_wi_k(l, k, ce="pool", pieces=(0, 1, 2), q="sp"):
            rows = slice(k * 128, (k + 1) * 128)
            qv = wi[:, k, 0:512].rearrange("p (j t d) -> p j t d", j=4, t=2)
            zv = wi[:, k, ZAO:ZAO + 512].rearrange("p (j t d) -> p j t d", j=4, t=2)
            if 0 in pieces:
                sl = stage_slot()
                s.dma(q, lambda e, l=l, rows=rows, sl=sl: e.dma_start(out=stg[:, sl, :], in_=win_d[l, rows, 0:1024]),
                      "stg%d" % sl, writes=["stg%da" % sl, "stg%db" % sl])

                def c0(e, k=k, sl=sl, qv=qv, zv=zv):
                    e.tensor_copy(out=qv[:, :, 0, :], in_=stg[:, sl, 0:256].rearrange("p (j d) -> p j d", j=4))
                    e.tensor_copy(out=qv[:, :, 1, :], in_=stg[:, sl, 256:512].rearrange("p (j d) -> p j d", j=4))
                    e.tensor_copy(out=wi[:, k, 512:768], in_=stg[:, sl, 512:768])
                    return e.tensor_copy(out=zv[:, :, 0, :], in_=stg[:, sl, 768:1024].rearrange("p (j d) -> p j d", j=4))
                s.op(ce, c0, reads=["stg%da" % sl, "stg%db" % sl], writes=["wi%da" % k])
            if 1 in pieces:
                sl = stage_slot()
                s.dma(q, lambda e, l=l, rows=rows, sl=sl: e.dma_start(out=stg[:, sl, :], in_=win_d[l, rows, 1024:2048]),
                      "stg%d" % sl, writes=["stg%da" % sl, "stg%db" % sl])

                def c1(e, k=k, sl=sl, zv=zv):
                    e.tensor_copy(out=zv[:, :, 1, :], in_=stg[:, sl, 0:256].rearrange("p (j d) -> p j d", j=4))
                    return e.tensor_copy(out=wi[:, k, 1280:2048], in_=stg[:, sl, 256:1024])
                s.op(ce, c1, reads=["stg%da" % sl, "stg%db" % sl], writes=["wi%db" % k])
            if 2 in pieces:
                sl = stage_slot()
                s.dma(q, lambda e, l=l, rows=rows, sl=sl: e.dma_start(out=stg[:, sl, 0:768], in_=win_d[l, rows, 2048:2816]),
                      "stg%d" % sl, writes=["stg%da" % sl, "stg%db" % sl])
                s.op(ce, lambda e, k=k, sl=sl: e.tensor_copy(out=wi[:, k, 2048:2816], in_=stg[:, sl, 0:768]),
                     reads=["stg%da" % sl, "stg%db" % sl], writes=["wi%db" % k])

        def prep_wo(l, v, ce="pool"):
            s.dma("sp", lambda e, l=l, v=v: e.dma_start(out=gbc, in_=gsc_d[2 * l + v:2 * l + v + 1, :].partition_broadcast(128)),
                  "gbc", reads=["gsc_d%d" % l], writes=["rope", "rope2"])
            for k in range(8):
                sl = stage_slot()
                if k < 4:
                    s.dma("sp", lambda e, l=l, k=k, sl=sl: e.dma_start(out=stg[0:64, sl, :], in_=wout_d[l, k * 64:(k + 1) * 64, :]),
                          "stgh%da" % sl, writes=["stg%da" % sl])
                    s.dma("sp", lambda e, l=l, k=k, sl=sl: e.dma_start(out=stg[64:128, sl, :], in_=wout_d[l, (k + 4) * 64:(k + 5) * 64, :]),
                          "stgh%db" % sl, writes=["stg%db" % sl])
                else:
                    s.dma("sp", lambda e, l=l, k=k, sl=sl: e.dma_start(out=stg[:, sl, :], in_=wout_d[l, k * 128:(k + 1) * 128, :]),
                          "stg%d" % sl, writes=["stg%da" % sl, "stg%db" % sl])
                s.op(ce, lambda e, k=k, sl=sl: e.tensor_tensor(out=wo[:, k, :], in0=stg[:, sl, :], in1=gbc, op=ALU.mult),
                     reads=["stg%da" % sl, "stg%db" % sl, "rope", "rope2"], writes=["wo%d" % k])

        acc_i = [0]

        def next_acc():
            b = (0, 1, 4, 5)[acc_i[0] % 4]
            acc_i[0] += 1
            return b

        tp_i = [0]
        stats_done = set()

        def make_stats(tiles, l, v):
            t0 = tiles[0]
            key = (tuple(tiles), l, v)
            if key in stats_done:
                return
            stats_done.add(key)
            s.op("dve", lambda e, t0=t0: e.memset(ms[:, t0:t0 + 4], 0.0), writes=["ms%d" % t0])
            for t in tiles:
                s.op("act", lambda e, t=t: e.activation(out=ybf[:], in_=xb[:, t, :], func=AF.Square, scale=1.0 / 32.0,
                                                        accum_out=ms[:, t:t + 1]),
                     reads=[XBR[t], "ms%d" % t0], writes=["ybf", "msv%d" % t])
            s.op("act", lambda e, t0=t0: e.activation(out=lnm[:, t0:t0 + 4], in_=ms[:, t0:t0 + 4], func=AF.Ln, bias=epsc[:, 0:1]),
                 reads=["msv%d" % t for t in tiles] + ["epsc"], writes=["lnm%d" % t0])
            s.op("act", lambda e, t0=t0: e.activation(out=rstd[:, t0:t0 + 4], in_=lnm[:, t0:t0 + 4], func=AF.Exp, scale=-0.5),
                 reads=["lnm%d" % t0], writes=["rstd%d" % t0])

        def make_xn(tiles, l, v):
            lv = l * 2 + v
            t0 = tiles[0]
            make_stats(tiles, l, v)
            import os as _os
            KXN = int(_os.environ.get("KXN", "9"))
            if KXN < 1:
                return
            tpa = [pp[0][:].bitcast(BF16), pp[2][:].bitcast(BF16)]
            TPB = ["ps0", "ps1", "ps4", "ps5"]
            for i, t in enumerate(tiles):
                yb = ybf2[:, i % 2, :]
                ybn = "ybf" if i % 2 == 0 else "ybfB"
                s.op("dve", lambda e, t=t, yb=yb: e.tensor_scalar(out=yb, in0=xb[:, t, :], scalar1=rstd[:, t:t + 1], scalar2=None,
                                                                  op0=ALU.mult),
                     reads=[XBR[t], "rstd%d" % t0], writes=[ybn])

                def tr(e, yb=yb, i=i):
                    last = None
                    for k in range(8):
                        last = e.transpose(tpa[k // 4][:, (k % 4) * 512 + i * 128:(k % 4) * 512 + (i + 1) * 128],
                                           yb[:, k * 128:(k + 1) * 128], identb[:])
                    return last
                s.op("pe", tr, reads=[ybn, "identb"], writes=TPB)
            for k in range(8):
                src = tpa[k // 4][:, (k % 4) * 512:(k % 4 + 1) * 512]
                if k < 4:
                    s.op("act", lambda e, k=k, src=src: e.activation(
                        out=xn[:, k, :], in_=src, func=AF.Identity,
                        scale=gsc_s[:, lv * 8 + k:lv * 8 + k + 1], bias=shc_s[:, lv * 8 + k:lv * 8 + k + 1]),
                        reads=["ps0", "ps1", "gsc%d" % lv, "shc%d" % lv], writes=["xn_%d_%d" % (i, k) for i in range(4)])
                else:
                    s.op("dve", lambda e, k=k, src=src: e.tensor_scalar(
                        out=xn[:, k, :], in0=src, scalar1=gsc_s[:, lv * 8 + k:lv * 8 + k + 1],
                        scalar2=shc_s[:, lv * 8 + k:lv * 8 + k + 1], op0=ALU.mult, op1=ALU.add),
                        reads=["ps4", "ps5", "gsc%d" % lv, "shc%d" % lv], writes=["xn_%d_%d" % (i, k) for i in range(4)])

        XNR = ["xn_%d_%d" % (i, k) for i in range(4) for k in range(8)]

        def proj_fm(col0, nq=512, wres=None):
            b = next_acc()

            def mm(e, b=b, col0=col0):
                last = None
                for k in range(8):
                    last = e.matmul(banks[b][:, 0:nq], lhsT=wi[:, k, col0:col0 + 128], rhs=xn[:, k, 0:nq],
                                    start=(k == 0), stop=(k == 7))
                return last
            s.op("pe", mm, reads=(wres or WIR) + XNR, writes=["ps%d" % b])
            return b

        def load_rope(g):
            s.dma("act", lambda e, g=g: e.dma_start(out=rope[:, 0, :], in_=ropec_d[:, g * 512:(g + 1) * 512]),
                  "rope", writes=["rope"])
            s.dma("act", lambda e, g=g: e.dma_start(out=rope[:, 1, :], in_=ropes_d[:, g * 512:(g + 1) * 512]),
                  "rope2", writes=["rope2"])

        qk_i = [0]

        def qk_post(b, gcol, l_col, out_ap, out_res, use_rope):
            par = qk_i[0] % 2
            qk_i[0] += 1
            rb, rbn = ((tA, "tA"), (tB, "tB"))[par]
            if par == 0:
                sq_t, sqn, qg_t, qgn = tE, ["tE"], tF, ["tF"]
                c_t, cn, d_t, dn = tC, ["tC"], tD, ["tD"]
                ssb, rotb = 2, 3
            else:
                sq_t, sqn, qg_t, qgn = pA1, ["p1sa"], pE1, ["p1dt"]
                c_t, cn, d_t, dn = pB1, ["p1ub", "p1ubl", "p1ubr"], pC1, ["p1wc"]
                ssb, rotb = 6, 7
            s.op("act", lambda e, b=b: e.activation(out=sq_t[:, 0:512], in_=banks[b][:], func=AF.Square),
                 reads=["ps%d" % b], writes=sqn)
            s.op("act", lambda e, b=b: e.activation(out=qg_t[:, 0:512], in_=banks[b][:], func=AF.Copy, scale=col(gcol + l_col)),
                 reads=["ps%d" % b, "cst"], writes=qgn)
            s.op("pe", lambda e: e.matmul(banks[ssb][:], lhsT=bonesb[:], rhs=sq_t[:, 0:512], start=True, stop=True),
                 reads=["bonesb"] + sqn, writes=["ps%d" % ssb])
            if use_rope:
                s.op("pe", lambda e: e.matmul(banks[rotb][:], lhsT=rmb[:], rhs=qg_t[:, 0:512], start=True, stop=True),
                     reads=["rmb"] + qgn, writes=["ps%d" % rotb])
            s.op("act", lambda e: e.activation(out=rb[:, 0:512], in_=banks[ssb][:], func=AF.Ln, scale=1.0 / 64.0, bias=epsc[:, 0:1]),
                 reads=["ps%d" % ssb, "epsc"], writes=[rbn])
            s.op("act", lambda e: e.activation(out=rb[:, 0:512], in_=rb[:, 0:512], func=AF.Exp, scale=-0.5),
                 reads=[rbn], writes=[rbn])
            if use_rope:
                s.op("pool", lambda e: e.tensor_tensor(out=c_t[:, 0:512], in0=qg_t[:, 0:512], in1=rope[:, 0, :], op=ALU.mult),
                     reads=qgn + ["rope"], writes=cn)
                s.op("dve", lambda e: e.tensor_tensor(out=d_t[:, 0:512], in0=banks[rotb][:], in1=rope[:, 1, :], op=ALU.mult),
                     reads=["ps%d" % rotb, "rope2"], writes=dn)
                s.op("dve", lambda e: e.tensor_tensor(out=c_t[:, 0:512], in0=c_t[:, 0:512], in1=d_t[:, 0:512], op=ALU.add),
                     reads=cn + dn, writes=cn)
                src, srcn = c_t[:, 0:512], cn
            else:
                src, srcn = qg_t[:, 0:512], qgn
            if isinstance(out_ap, tuple):
                s.op("dve", lambda e: e.tensor_tensor(out=out_ap[0], in0=src[0:64, :], in1=rb[0:64, 0:512], op=ALU.mult),
                     reads=srcn + [rbn], writes=[out_res])
                s.op("dve", lambda e: e.tensor_tensor(out=out_ap[1], in0=src[64:128, :], in1=rb[64:128, 0:512], op=ALU.mult),
                     reads=srcn + [rbn], writes=[out_res])
            else:
                s.op("dve", lambda e: e.tensor_tensor(out=out_ap, in0=src, in1=rb[:, 0:512], op=ALU.mult),
                     reads=srcn + [rbn], writes=[out_res])

        def kv_pass(l, kcol0, vt0, sample, g=None, prompt_tiles=None):
            if sample:
                load_rope(g)
            b = proj_fm(KO, wres=WIRA)
            qk_post(b, C_KG, l, KT[:, kcol0:kcol0 + 512], "KT%d" % (kcol0 // 512), sample)
            for i in range(4):
                bb = next_acc()
                ncol = 128 if sample else 256
                c0 = VO if sample else KO

                def mm(e, bb=bb, i=i, ncol=ncol, c0=c0):
                    last = None
                    for k in range(8):
                        last = e.matmul(banks[bb][:, 0:ncol], lhsT=xn[:, k, i * 128:(i + 1) * 128], rhs=wi[:, k, c0:c0 + ncol],
                                        start=(k == 0), stop=(k == 7))
                    return last
                s.op("pe", mm, reads=WIRA + XNR, writes=["ps%d" % bb])
                vt = vt0 + i
                voff = 0 if sample else 128
                vout = Vst[:, vt, :].rearrange("p (a d) -> p a d", a=3)
                s.op("act", lambda e, bb=bb, vout=vout, voff=voff: e.activation(
                    out=vout[:, 0:3:2, :], in_=banks[bb][:, voff:voff + 128].rearrange("p (a d) -> p a d", a=2), func=AF.Copy),
                    reads=["ps%d" % bb], writes=["V%d" % vt])
                if not sample:
                    sq, ti = divmod(i, 2)
                    s.op("dve", lambda e, bb=bb: e.tensor_copy(out=nvs[:], in_=banks[bb][:, 128:256]),
                         reads=["ps%d" % bb], writes=["nvs"])
                    s.dma("act", lambda e, sq=sq, ti=ti, l=l: e.dma_start(out=nv_d[sq, l, ti * 128:(ti + 1) * 128, :], in_=nvs[:]),
                          "nvo", reads=["nvs"])
                    s.op("dve", lambda e: e.memset(ssk[:], 0.0), writes=["ssk"])
                    for kv in range(2):
                        s.op("act", lambda e, bb=bb, kv=kv: e.activation(
                            out=tE[:, kv * 64:(kv + 1) * 64], in_=banks[bb][:, kv * 64:(kv + 1) * 64], func=AF.Square,
                            accum_out=ssk[:, kv:kv + 1]),
                            reads=["ps%d" % bb, "ssk"], writes=["tE", "sskv%d" % kv])
                    s.op("act", lambda e: e.activation(out=lnk[:], in_=ssk[:], func=AF.Ln, scale=1.0 / 64.0, bias=epsc[:, 0:1]),
                         reads=["sskv0", "sskv1", "epsc"], writes=["lnk"])
                    s.op("act", lambda e: e.activation(out=rk[:], in_=lnk[:], func=AF.Exp, scale=-0.5),
                         reads=["lnk"], writes=["rk"])
                    for kv in range(2):
                        s.op("dve", lambda e, bb=bb, kv=kv, l=l: e.scalar_tensor_tensor(
                            out=nks[:, kv * 64:(kv + 1) * 64], in0=banks[bb][:, kv * 64:(kv + 1) * 64], scalar=rk[:, kv:kv + 1],
                            in1=kgbs[:, l, kv * 64:(kv + 1) * 64], op0=ALU.mult, op1=ALU.mult),
                            reads=["ps%d" % bb, "rk", "kgbs"], writes=["nks%d" % kv])
                    s.dma("act", lambda e, sq=sq, ti=ti, l=l: e.dma_start(out=nk_d[sq, l, ti * 128:(ti + 1) * 128, :], in_=nks[:]),
                          "nko", reads=["nks0", "nks1"])
            if sample:
                s.op("pool", lambda e, g=g: e.tensor_copy(out=xe[:, :, g, 0, :], in_=xn[:, :, 0:8]), reads=XNR, writes=["xe%d" % g])
                s.op("pool", lambda e, g=g: e.tensor_copy(out=xe[:, :, g, 1, :], in_=xn[:, :, 504:512]), reads=XNR,
                     writes=["xe%db" % g])

        def cache_kv(l):
            s.dma("act", lambda e, l=l: e.dma_start(out=cks, in_=ck_d[l].rearrange("(t p) c -> p t c", p=128)), "ckl",
                  writes=["tA"])
            s.dma("act", lambda e, l=l: e.dma_start(out=cvs, in_=cv_d[l].rearrange("(t p) c -> p t c", p=128)), "cvl",
                  writes=["tB"])
            s.op("dve", lambda e: e.tensor_copy(out=ckb.rearrange("p (t c) -> p t c", t=2), in_=cks), reads=["tA"], writes=["tE"])

            def tr(e):
                e.transpose(bankb[7][:, 0:128], ckb[:, 0:128], identb[:])
                return e.transpose(bankb[7][:, 128:256], ckb[:, 128:256], identb[:])
            s.op("pe", tr, reads=["tE", "identb"], writes=["ps7"])
            s.op("act", lambda e: e.activation(out=KT[:, 2048:2304], in_=bankb[7][:, 0:256], func=AF.Copy),
                 reads=["ps7"], writes=["KTc"])
            for i in range(2):
                vout = Vst[:, 16 + i, :].rearrange("p (a d) -> p a d", a=3)
                s.op("dve", lambda e, i=i, vout=vout: e.tensor_copy(
                    out=vout[:, 0:3:2, :], in_=cvs[:, i, :].rearrange("p (a d) -> p a d", a=2)),
                    reads=["tB"], writes=["V%d" % (16 + i)])

        def step_b(l, sample, g=None, inter=None, after_inter=None):
            inter = list(inter or [])

            def pop_inter(k=1):
                for _ in range(k):
                    if inter:
                        inter.pop(0)()
            nseg, L = (1, 512) if sample else (2, 256)
            W = L + 16

            def seg(t):
                return t[:, 0:nseg * W].rearrange("p (s w) -> p s w", s=nseg)

            def pv(b):
                return banks[b][:].rearrange("p (s w) -> p s w", s=nseg)

            def fl(t):
                return t.rearrange("p (s w) -> p s w", s=nseg)

            if sample:
                s.dma("act", lambda e, g=g: e.dma_start(out=prc, in_=prs_d[:, :, g * 512:(g + 1) * 512]), "prc", writes=["ptb0", "ptb1", "ptb2", "ptb3"])
                load_rope(g)
                gl, ml = [(3, 0), (0, 2), (1, 1), (2, 2)][g]
                gr, mr = [(1, 2), (2, 1), (3, 2), (0, 0)][g]
                s.op("pool", lambda e, gl=gl: e.tensor_copy(out=xh[:, :, 0:8], in_=xe[:, :, gl, 1, :]), reads=["xe%db" % gl],
                     writes=["xha"])
                s.op("pool", lambda e, gr=gr: e.tensor_copy(out=xh[:, :, 8:16], in_=xe[:, :, gr, 0, :]), reads=["xe%d" % gr],
                     writes=["xhb"])

                def hm(e):
                    last = None
                    for idx, c0 in enumerate([HCO, HCO + 128, CCO, CCO + 128, UPO, UPO + 128]):
                        for k in range(8):
                            last = e.matmul(banks[7][:, idx * 16:(idx + 1) * 16], lhsT=wi[:, k, c0:c0 + 128], rhs=xh[:, k, :],
                                            start=(k == 0), stop=(k == 7))
                    return last
                s.op("pe", hm, reads=WIR + ["xha", "xhb"], writes=["ps7"])
                s.op("act", lambda e: e.activation(out=hhs[:], in_=banks[7][:, 0:32], func=AF.Copy), reads=["ps7"], writes=["hhs"])
                s.op("dve", lambda e: e.tensor_tensor(out=uh[:], in0=hhs[:], in1=banks[7][:, 32:64], op=ALU.mult),
                     reads=["hhs", "ps7"], writes=["uh"])
                s.op("dve", lambda e: e.tensor_copy(out=uph[:], in_=banks[7][:, 64:96]), reads=["ps7"], writes=["uph"])
            else:
                s.dma("act", lambda e: e.dma_start(out=prc, in_=prp_d), "prc", writes=["ptb0", "ptb1", "ptb2", "ptb3"])

            bq = [proj_fm(QO + j * 128) for j in range(4)]
            for j in range(4):
                qk_post(bq[j], C_QG, l, (qT[0:64, j, 0, :], qT[64:128, j, 1, :]), "qT%d" % j, sample)
                pop_inter(1)
            for j in range(4):
                b = proj_fm(ZAO + j * 128)
                s.op("act", lambda e, b=b, j=j: e.activation(out=ga[:, j, :], in_=banks[b][:], func=AF.Silu),
                     reads=["ps%d" % b], writes=["ga%d" % j])
                pop_inter(1)
            pop_inter(99)
            if after_inter is not None:
                after_inter()
            for jj in range(2):
                cw = C_CW + (l * 2 + jj) * 3
                if jj == 0:
                    h_t, hn, u_t, un, y_t, yn, z_t, zn = tA, ["tA"], tB, ["tB"], tC, ["tC"], tF, ["tF"]
                    uln, urn = "tBl", "tBr"
                else:
                    h_t, hn, u_t, un, y_t, yn, z_t, zn = pA1, ["p1sa"], pB1, ["p1ub"], pC1, ["p1wc"], pE1, ["p1dt"]
                    uln, urn = "p1ubl", "p1ubr"

                def sgu(t=u_t):
                    return t[:, 0:nseg * W].rearrange("p (s w) -> p s w", s=nseg)
                b = proj_fm(HCO + jj * 128)
                s.op("act", lambda e, b=b, h_t=h_t: e.activation(out=h_t[:, 0:512], in_=banks[b][:], func=AF.Copy),
                     reads=["ps%d" % b], writes=hn)
                b = proj_fm(CCO + jj * 128)
                s.op("dve", lambda e, b=b, h_t=h_t, sgu=sgu: e.tensor_tensor(out=sgu()[:, :, 8:8 + L], in0=pv(b), in1=fl(h_t[:, 0:512]),
                                                                          op=ALU.mult),
                     reads=["ps%d" % b] + hn, writes=un)
                if sample:
                    s.op("dve", lambda e, jj=jj, u_t=u_t: e.tensor_scalar(
                        out=u_t[:, 7:8], in0=uh[:, jj * 16 + 7:jj * 16 + 8], scalar1=col(C_MSK + ml), scalar2=None, op0=ALU.mult),
                        reads=["uh", "cst"], writes=[uln])
                    s.op("dve", lambda e, jj=jj, u_t=u_t: e.tensor_scalar(
                        out=u_t[:, 8 + L:9 + L], in0=uh[:, jj * 16 + 8:jj * 16 + 9], scalar1=col(C_MSK + mr), scalar2=None, op0=ALU.mult),
                        reads=["uh", "cst"], writes=[urn])
                else:
                    s.op("dve", lambda e, sgu=sgu: e.memset(sgu()[:, :, 7:8], 0.0), writes=[uln])
                    s.op("dve", lambda e, sgu=sgu: e.memset(sgu()[:, :, 8 + L:9 + L], 0.0), writes=[urn])
                UA = un + [uln, urn]
                s.op("dve", lambda e, cw=cw, jj=jj, sgu=sgu, y_t=y_t: e.tensor_scalar(
                    out=fl(y_t[:, 0:512]), in0=sgu()[:, :, 7:7 + L], scalar1=col(cw), scalar2=col(C_CB + l * 2 + jj),
                    op0=ALU.mult, op1=ALU.add),
                    reads=UA + ["cst"], writes=yn)
                s.op("dve", lambda e, cw=cw, sgu=sgu, y_t=y_t: e.scalar_tensor_tensor(
                    out=fl(y_t[:, 0:512]), in0=sgu()[:, :, 8:8 + L], scalar=col(cw + 1), in1=fl(y_t[:, 0:512]),
                    op0=ALU.mult, op1=ALU.add),
                    reads=UA + yn + ["cst"], writes=yn)
                s.op("dve", lambda e, cw=cw, sgu=sgu, y_t=y_t: e.scalar_tensor_tensor(
                    out=fl(y_t[:, 0:512]), in0=sgu()[:, :, 9:9 + L], scalar=col(cw + 2), in1=fl(y_t[:, 0:512]),
                    op0=ALU.mult, op1=ALU.add),
                    reads=UA + yn + ["cst"], writes=yn)
                b = proj_fm(BCO + jj * 128)
                s.op("dve", lambda e, b=b, y_t=y_t: e.tensor_tensor(out=y_t[:, 0:512], in0=y_t[:, 0:512], in1=banks[b][:], op=ALU.mult),
                     reads=["ps%d" % b] + yn, writes=yn)
                b = proj_fm(ZCO + jj * 128)
                s.op("act", lambda e, b=b, z_t=z_t: e.activation(out=z_t[:, 0:512], in_=banks[b][:], func=AF.Silu),
                     reads=["ps%d" % b], writes=zn)
                s.op("pool", lambda e, jj=jj, y_t=y_t, z_t=z_t: e.tensor_tensor(out=mix[:, 4 + jj, :], in0=y_t[:, 0:512], in1=z_t[:, 0:512],
                                                                             op=ALU.mult),
                     reads=yn + zn, writes=["mix%d" % (4 + jj)])
            def pool_chain(c, eng, ub_t, wc_t, wd_t, sa_t, dt_t, N):
                def sg(t):
                    return t[:, 0:nseg * W].rearrange("p (s w) -> p s w", s=nseg)
                UBN = [N["ub"], N["ubl"], N["ubr"]]
                if sample:
                    s.op(eng, lambda e: e.tensor_scalar(
                        out=ub_t[:, 0:8], in0=uph[:, c * 16:c * 16 + 8], scalar1=col(C_MSK + ml), scalar2=None, op0=ALU.mult),
                        reads=["uph", "cst"], writes=[N["ubl"]])
                    s.op(eng, lambda e: e.tensor_scalar(
                        out=ub_t[:, 8 + L:16 + L], in0=uph[:, c * 16 + 8:c * 16 + 16], scalar1=col(C_MSK + mr), scalar2=None,
                        op0=ALU.mult),
                        reads=["uph", "cst"], writes=[N["ubr"]])
                else:
                    s.op(eng, lambda e: e.memset(sg(ub_t)[:, :, 0:8], 0.0), writes=[N["ubl"]])
                    s.op(eng, lambda e: e.memset(sg(ub_t)[:, :, 8 + L:16 + L], 0.0), writes=[N["ubr"]])
                s.op(eng, lambda e: e.tensor_tensor(out=sg(wc_t)[:, :, 1:W], in0=sg(ub_t)[:, :, 0:W - 1], in1=sg(ub_t)[:, :, 1:W],
                                                    op=ALU.add),
                     reads=UBN, writes=[N["wc"]])
                if c == 0:
                    s.op(eng, lambda e: e.tensor_tensor(out=sg(wd_t)[64:128, :, 2:W - 1], in0=sg(wc_t)[64:128, :, 1:W - 2],
                                                        in1=sg(wc_t)[64:128, :, 3:W], op=ALU.add),
                         reads=[N["wc"]], writes=[N["wd"]])
                else:
                    s.op(eng, lambda e: e.tensor_tensor(out=sg(wd_t)[:, :, 2:W - 1], in0=sg(wc_t)[:, :, 1:W - 2],
                                                        in1=sg(wc_t)[:, :, 3:W], op=ALU.add),
                         reads=[N["wc"]], writes=[N["wd"]])
                    s.op(eng, lambda e: e.tensor_tensor(out=sg(wc_t)[:, :, 4:W - 3], in0=sg(wd_t)[:, :, 2:W - 5],
                                                        in1=sg(wd_t)[:, :, 6:W - 1], op=ALU.add),
                         reads=[N["wd"]], writes=[N["wc"]])
                    s.op(eng, lambda e: e.tensor_tensor(out=sg(wd_t)[64:128, :, 8:W - 7], in0=sg(wc_t)[64:128, :, 4:W - 11],
                                                        in1=sg(wc_t)[64:128, :, 12:W - 3], op=ALU.add),
                         reads=[N["wc"]], writes=[N["wd"]])
                PTBA = ["ptb0", "ptb1", "ptb2", "ptb3"]
                s.op(eng, lambda e: e.tensor_tensor(out=fl(sa_t[0:64, 0:512]), in0=sg(wc_t)[0:64, :, 8:8 + L],
                                                    in1=fl(prc[0:64, c, :]), op=ALU.mult),
                     reads=[N["wc"]] + PTBA, writes=[N["sa"]])
                s.op(eng, lambda e: e.tensor_tensor(out=fl(sa_t[64:128, 0:512]), in0=sg(wd_t)[64:128, :, 8:8 + L],
                                                    in1=fl(prc[64:128, c, :]), op=ALU.mult),
                     reads=[N["wd"]] + PTBA, writes=[N["sa"]])
                s.op(eng, lambda e: e.tensor_tensor(out=fl(dt_t[:, 0:512]), in0=fl(sa_t[:, 0:512]), in1=sg(ub_t)[:, :, 8:8 + L],
                                                    op=ALU.subtract),
                     reads=[N["sa"]] + UBN, writes=[N["dt"]])

            bu1 = proj_fm(UPO + 128)
            bu0 = proj_fm(UPO)
            bz0 = proj_fm(ZPO)
            bz1 = proj_fm(ZPO + 128)
            s.op("act", lambda e: e.activation(out=pB1[:, 0:nseg * W].rearrange("p (s w) -> p s w", s=nseg)[:, :, 8:8 + L],
                                               in_=pv(bu1), func=AF.Copy),
                 reads=["ps%d" % bu1], writes=["p1ub"])
            s.op("act", lambda e: e.activation(out=seg(tB)[:, :, 8:8 + L], in_=pv(bu0), func=AF.Copy),
                 reads=["ps%d" % bu0], writes=["tB"])
            pool_chain(1, "pool", pB1, pC1, pD1, pA1, pE1,
                       dict(ub="p1ub", ubl="p1ubl", ubr="p1ubr", wc="p1wc", wd="p1wd", sa="p1sa", dt="p1dt"))
            pool_chain(0, "dve", tB, tC, tD, tA, tE,
                       dict(ub="tB", ubl="tBl", ubr="tBr", wc="tC", wd="tD", sa="tA", dt="tE"))
            s.op("act", lambda e: e.activation(out=tF[:], in_=banks[bz0][:], func=AF.Silu), reads=["ps%d" % bz0], writes=["tF"])
            s.op("pe", lambda e: e.matmul(banks[2][:], lhsT=pwb[:, l * 2 + 0, :], rhs=tE[:], start=True, stop=True),
                 reads=["pwb", "tE"], writes=["ps2"])
            s.op("dve", lambda e: e.scalar_tensor_tensor(
                out=mix[:, 6, :], in0=banks[2][:], scalar=col(C_PS + l * 2 + 0), in1=tF[:], op0=ALU.mult, op1=ALU.mult),
                reads=["ps2", "tF", "cst"], writes=["mix6"])
            s.op("act", lambda e: e.activation(out=tA[:, 0:512], in_=banks[bz1][:], func=AF.Silu),
                 reads=["ps%d" % bz1], writes=["tA"])
            s.op("pe", lambda e: e.matmul(banks[3][:], lhsT=pwb[:, l * 2 + 1, :], rhs=pE1[:, 0:512], start=True, stop=True),
                 reads=["pwb", "p1dt"], writes=["ps3"])
            s.op("dve", lambda e: e.scalar_tensor_tensor(
                out=mix[:, 7, :], in0=banks[3][:], scalar=col(C_PS + l * 2 + 1), in1=tA[:, 0:512], op0=ALU.mult, op1=ALU.mult),
                reads=["ps3", "tA", "cst"], writes=["mix7"])

        import os as _os2
        WARM = int(_os2.environ.get("KWARM", "20"))
        FILL = int(_os2.environ.get("KFILL", "5"))
        po_i = [0]
        KTALL = ["KT0", "KT1", "KT2", "KT3", "KTc"]

        def attention(q0, nq, kts, tail_fill=0):
            n = len(kts) // 2
            stpairs = (1, 3)
            pending = [None]

            def make_epilogue(j, pos, on_act=False):
                pa, pb_ = pos
                if nq >= 512 and not on_act:
                    acts = [
                        lambda: s.op("dve", lambda e: e.reciprocal(out=tB[64:128, 0:nq], in_=banks[pa][64:128, 0:nq]),
                                     reads=["ps%d" % pa], writes=["tB"]),
                        lambda: s.op("dve", lambda e: e.reciprocal(out=tB[0:64, 0:nq], in_=banks[pb_][0:64, 0:nq]),
                                     reads=["ps%d" % pb_], writes=["tB"]),
                    ]
                else:
                    acts = [
                        lambda: s.op("act", lambda e: e.activation(out=tA[64:128, 0:nq], in_=banks[pa][64:128, 0:nq], func=AF.Ln),
                                     reads=["ps%d" % pa], writes=["tA"]),
                        lambda: s.op("act", lambda e: e.activation(out=tA[0:64, 0:nq], in_=banks[pb_][0:64, 0:nq], func=AF.Ln),
                                     reads=["ps%d" % pb_], writes=["tA"]),
                        lambda: s.op("act", lambda e: e.activation(out=tB[:, 0:nq], in_=tA[:, 0:nq], func=AF.Exp, scale=-1.0),
                                     reads=["tA"], writes=["tB"]),
                    ]

                def rest():
                    s.dma("act", lambda e: e.dma_start(out=tD[0:64, 0:nq], in_=tB[64:128, 0:nq]), "swpA", reads=["tB"], writes=["tD"])
                    s.dma("act", lambda e: e.dma_start(out=tD[64:128, 0:nq], in_=tB[0:64, 0:nq]), "swpB", reads=["tB"], writes=["tDb"])
                    s.op("dve", lambda e: e.tensor_tensor(out=tC[0:64, 0:nq], in0=banks[pa][0:64, 0:nq],
                                                          in1=ga[0:64, j, q0:q0 + nq], op=ALU.mult),
                         reads=["ps%d" % pa, "ga%d" % j], writes=["tC"])
                    s.op("dve", lambda e: e.tensor_tensor(out=tC[64:128, 0:nq], in0=banks[pb_][64:128, 0:nq],
                                                          in1=ga[64:128, j, q0:q0 + nq], op=ALU.mult),
                         reads=["ps%d" % pb_, "ga%d" % j], writes=["tC"])
                    s.op("dve", lambda e: e.tensor_tensor(out=mix[:, j, q0:q0 + nq], in0=tC[:, 0:nq], in1=tD[:, 0:nq], op=ALU.mult),
                         reads=["tC", "tD", "tDb"], writes=["mix%d" % j])
                return acts, rest

            def flush():
                if pending[0] is not None:
                    acts, rest = pending[0]
                    for a in acts:
                        a()
                    rest()
                    pending[0] = None

            items = [(j, ab, p) for j in range(4) for ab in range(2) for p in range(n)]
            base = po_i[0]
            po_i[0] += 4

            def pos_of(j):
                return (4, 5) if (base + j) % 2 == 0 else (0, 1)

            def qk(idx):
                j, ab, p = items[idx]
                st = stpairs[idx % 2]

                def f(e):
                    e.matmul(pp[st][:, 0:nq], lhsT=KT[:, kts[2 * p][0]:kts[2 * p][0] + 128],
                             rhs=qT[:, j, ab, q0:q0 + nq], start=True, stop=True)
                    return e.matmul(pp[st][:, 512:512 + nq], lhsT=KT[:, kts[2 * p + 1][0]:kts[2 * p + 1][0] + 128],
                                    rhs=qT[:, j, ab, q0:q0 + nq], start=True, stop=True)
                s.op("pe", f, reads=KTALL + ["qT%d" % j, "qTz"], writes=["ps%d" % (2 * st), "ps%d" % (2 * st + 1)])

            def ex(idx):
                st = stpairs[idx % 2]
                sl = 2 * (idx % 2)
                s.op("act", lambda e: e.activation(
                    out=ptb[:, sl:sl + 2, 0:nq], in_=pp[st][:].rearrange("p (b n) -> p b n", b=2)[:, :, 0:nq],
                    func=AF.Exp, scale=0.125),
                    reads=["ps%d" % (2 * st), "ps%d" % (2 * st + 1)], writes=["ptb%d" % sl, "ptb%d" % (sl + 1)])

            def pvm(idx):
                j, ab, p = items[idx]
                po = pos_of(j)[ab]
                lo = 0 if ab == 0 else 64
                sl = 2 * (idx % 2)

                def f(e):
                    e.matmul(banks[po][:, 0:nq], lhsT=Vst[:, kts[2 * p][1], lo:lo + 128], rhs=ptb[:, sl, 0:nq],
                             start=(p == 0), stop=False)
                    return e.matmul(banks[po][:, 0:nq], lhsT=Vst[:, kts[2 * p + 1][1], lo:lo + 128], rhs=ptb[:, sl + 1, 0:nq],
                                    start=False, stop=(p == n - 1))
                s.op("pe", f, reads=["V%d" % kts[2 * p][1], "V%d" % kts[2 * p + 1][1], "Vones", "ptb%d" % sl, "ptb%d" % (sl + 1)],
                     writes=["ps%d" % po])

            qk(0)
            for idx, (j, ab, p) in enumerate(items):
                if idx + 1 < len(items):
                    qk(idx + 1)
                if idx == 0 and nq == 512 and WARM > 0:
                    def burst(e):
                        last = None
                        for r in range(WARM):
                            last = e.matmul(banks[pos_of(0)[1]][:, 0:512], lhsT=bonesb[:], rhs=mix[:, 4 + (r % 4), :], start=True, stop=True)
                        return last
                    s.op("pe", burst, reads=["bonesb", "mix4", "mix5", "mix6", "mix7"], writes=["ps%d" % pos_of(0)[1]])
                ex(idx)
                defer = (n >= 6 and ab == 0 and pending[0] is not None)
                if defer and p in (1, 2):
                    pending[0][0][p - 1]()
                if nq == 512 and j == 0 and ab == 0 and p < 4 and FILL > 0:
                    def fill(e, j=j):
                        last = None
                        for r in range(FILL):
                            last = e.matmul(banks[pos_of(j)[1]][:, 0:512], lhsT=bonesb[:], rhs=mix[:, 4 + (r % 4), :],
                                            start=True, stop=True)
                        return last
                    s.op("pe", fill, reads=["bonesb", "mix4", "mix5", "mix6", "mix7"], writes=["ps%d" % pos_of(j)[1]])
                pvm(idx)
                if defer and p == 4:
                    pending[0][1]()
                    pending[0] = None
                if n < 6 and ab == 0 and p == n - 1:
                    flush()
                if ab == 1 and p == n - 1:
                    pending[0] = make_epilogue(j, pos_of(j), on_act=(j == 3))
            if tail_fill > 0:
                def tfill(e):
                    last = None
                    for r in range(tail_fill):
                        last = e.matmul(pp[1][:, 0:512], lhsT=bonesb[:], rhs=KT[:, (r % 4) * 512:(r % 4 + 1) * 512], start=True, stop=True)
                    return last
                s.op("pe", tfill, reads=["bonesb"] + KTALL, writes=["ps2", "ps3"])
            flush()

        MIXR = ["mix%d" % j for j in range(8)]

        def out_proj_units(tiles, final, out_d, row0):
            units = []
            for i, t in enumerate(tiles):
                for nn in range(2):
                    def unit(i=i, t=t, nn=nn):
                        b = next_acc()

                        def mm(e, b=b):
                            last = None
                            for k in range(8):
                                last = e.matmul(banks[b][:], lhsT=mix[:, k, i * 128:(i + 1) * 128], rhs=wo[:, k, nn * 512:(nn + 1) * 512],
                                                start=(k == 0), stop=(k == 7))
                            return last
                        s.op("pe", mm, reads=MIXR + WOR, writes=["ps%d" % b])
                        s.op("dve", lambda e, b=b: e.tensor_tensor(
                            out=xb[:, t, nn * 512:(nn + 1) * 512], in0=banks[b][:], in1=xb[:, t, nn * 512:(nn + 1) * 512], op=ALU.add),
                            reads=["ps%d" % b, XBR[t]], writes=[XBR[t]])
                        if final and nn == 1:
                            s.op("dve", lambda e: e.memset(ms[:, t:t + 1], 0.0), writes=["msf%d" % t])
                            s.op("act", lambda e: e.activation(out=ybf[:], in_=xb[:, t, :], func=AF.Square, scale=1.0 / 32.0,
                                                               accum_out=ms[:, t:t + 1]),
                                 reads=[XBR[t], "msf%d" % t], writes=["ybf", "msfv%d" % t])
                            s.op("act", lambda e: e.activation(out=lnm[:, t:t + 1], in_=ms[:, t:t + 1], func=AF.Ln, bias=epsc[:, 0:1]),
                                 reads=["msfv%d" % t, "epsc"], writes=["lnf%d" % t])
                            s.op("act", lambda e: e.activation(out=rstd[:, t:t + 1], in_=lnm[:, t:t + 1], func=AF.Exp, scale=-0.5),
                                 reads=["lnf%d" % t], writes=["rsf%d" % t])
                            s.op("dve", lambda e: e.scalar_tensor_tensor(out=xb[:, t, :], in0=xb[:, t, :], scalar=rstd[:, t:t + 1],
                                                                         in1=gbc, op0=ALU.mult, op1=ALU.mult),
                                 reads=[XBR[t], "rsf%d" % t, "rope", "rope2"], writes=[XBR[t]])
                            s.dma("act", lambda e: e.dma_start(out=out_d[row0 + i * 128:row0 + (i + 1) * 128, :], in_=xb[:, t, :]),
                                  "yo%d" % (i % 2), reads=[XBR[t]])
                    units.append(unit)
            return units

        def out_proj(tiles, final, out_d, row0):
            for u in out_proj_units(tiles, final, out_d, row0):
                u()

        def load_x(src_d, g_src, g_dst):
            s.dma("sp", lambda e: e.dma_start(out=xb[:, 4 * g_dst:4 * g_dst + 4, :],
                                              in_=src_d[g_src * 512:(g_src + 1) * 512, :].rearrange("(t p) d -> p t d", p=128)),
                  "xg%d" % g_dst, writes=XBR[4 * g_dst:4 * g_dst + 4])

        def load_fgb():
            s.dma("act", lambda e: e.dma_start(out=gbc, in_=fgb_d), "gbc", writes=["rope", "rope2"])

        import os
        STAGE = os.environ.get("KSTAGE", "all")

        def program():
            PT = [12, 13, 14, 15]
            load_x(xp_d, 0, 3)
            mod_begin(0)
            for k in range(8):
                mod_chunk(0, k)
                prep_wi_k(0, k, "dve" if k % 2 == 0 else "pool", q="act")
            mod_end(0)
            mod_layer(1)
            prep_wo(0, 1, "dve")
            if STAGE == "mod":
                return
            if STAGE == "prep":
                return
            for l in range(2):
                make_xn(PT, l, 1)
                if STAGE == "p_xn":
                    return
                kv_pass(l, 0, 0, sample=False)
                if l == 0:
                    for g in range(3):
                        load_x(xs_d, g, g)
                if STAGE == "p_kv":
                    return
                step_b(l, sample=False)
                if STAGE == "p_b":
                    return
                if l == 0:
                    prep_wi(1, "pool")
                else:
                    prep_wi(0, "pool", kv_first=True)
                for sq in range(2):
                    attention(sq * 256, 256, [(sq * 256, 2 * sq), (sq * 256 + 128, 2 * sq + 1)])
                if STAGE == "p_att":
                    return
                if l == 1:
                    load_fgb()
                out_proj(PT, final=(l == 1), out_d=yp_d, row0=0)
                if l == 1:
                    for g in range(3):
                        make_stats([4 * g + i for i in range(4)], 0, 0)
                    cache_kv(0)
                    for g in (1, 2):
                        make_xn([4 * g + i for i in range(4)], 0, 0)
                        kv_pass(0, g * 512, 4 * g, sample=True, g=g)
                if STAGE == "p_out":
                    return
                if l == 0:
                    prep_wo(1, 1, "pool")
                else:
                    prep_wo(0, 0, "pool")
            if STAGE == "p":
                return
            load_x(xs_d, 3, 3)
            SKT = [(t * 128, t) for t in range(18)]
            for g in (0, 3):
                make_xn([4 * g + i for i in range(4)], 0, 0)
                kv_pass(0, g * 512, 4 * g, sample=True, g=g)
            if STAGE == "s0kv":
                return
            order0 = [3, 0, 1, 2]
            pend_units = None
            for n_, g in enumerate(order0):
                tiles = [4 * g + i for i in range(4)]
                ai = None
                if pend_units is not None:
                    gp = order0[n_ - 1]
                    ai = (lambda gp=gp: make_stats([4 * gp + i for i in range(4)], 1, 0))
                step_b(0, sample=True, g=g, inter=pend_units, after_inter=ai)
                pend_units = None
                if STAGE == "s0b":
                    return
                if n_ == 3:
                    prep_wi(1)
                else:
                    make_xn([4 * order0[n_ + 1] + i for i in range(4)], 0, 0)
                attention(0, 512, SKT, tail_fill=(28 if n_ == 3 else 0))
                if STAGE == "s0a":
                    return
                if n_ < 3:
                    pend_units = out_proj_units(tiles, final=False, out_d=None, row0=0)
                else:
                    out_proj(tiles, final=False, out_d=None, row0=0)
            prep_wo(1, 0)
            if STAGE == "s0":
                return
            cache_kv(1)
            for g in [2, 3, 0, 1]:
                make_xn([4 * g + i for i in range(4)], 1, 0)
                kv_pass(1, g * 512, 4 * g, sample=True, g=g)
            first = True
            for g in [1, 0]:
                tiles = [4 * g + i for i in range(4)]
                step_b(1, sample=True, g=g)
                if first:
                    make_xn([0, 1, 2, 3], 1, 0)
                attention(0, 512, SKT, tail_fill=28)
                load_fgb()
                first = False
                out_proj(tiles, final=True, out_d=ys_d, row0=g * 512)
        program()
        s.final_wait("sp")
        _DBG["sched"] = s
        s.replay(block)
    return nc


def _const_mats():
    ident = np.eye(128, dtype=np.float32)
    rm = np.zeros((128, 128), np.float32)
    for d in range(128):
        if (d % 32) < 16:
            rm[d + 16, d] = -1.0
        else:
            rm[d - 16, d] = 1.0
    bones = np.zeros((128, 128), np.float32)
    bones[0:64, 0:64] = 1.0
    bones[64:128, 64:128] = 1.0
    swp = np.zeros((128, 128), np.float32)
    swp[np.arange(128), (np.arange(128) + 64) % 128] = 1.0
    return np.ascontiguousarray(np.stack([ident, rm, bones, swp], axis=1))


def _rope_tables(nat):
    half = 32
    inv = (1.0 / (10000.0 ** (np.arange(0, half, 2, dtype=np.float32) / np.float32(half)))).astype(np.float32)
    row = (nat // 64).astype(np.float32)
    colp = (nat % 64).astype(np.float32)
    c = np.zeros((128, nat.shape[0]), np.float32)
    sn = np.zeros((128, nat.shape[0]), np.float32)
    for q in range(128):
        d = q % 64
        idx = d % 16
        ang = (row if d < 32 else colp) * inv[idx]
        ang = ang.astype(np.float32)
        c[q] = np.cos(ang)
        sn[q] = np.sin(ang)
    return c, sn


def _pool_rc(nat, n):
    wins = (2, 4, 8, 16)
    out = np.zeros((128, 2, nat.shape[0]), np.float32)
    for c in range(2):
        for hf in range(2):
            win = wins[2 * c + hf]
            lo = np.clip(nat - win // 2, 0, n - 1)
            hi = np.clip(nat - win // 2 + win - 1, 0, n - 1)
            cnt = (hi - lo + 1).astype(np.float32)
            out[hf * 64:(hf + 1) * 64, c, :] = (1.0 / cnt)[None, :]
    return out


_NC_CACHE = {}


def kernel(x_prompt, x_sample, cache_k, cache_v, c, c_ctx, norm_g, w_ada, b_ada, w_in,
           q_norm_g, k_norm_g, conv_w, conv_b, pool_w, pool_scale, w_out, final_g):
    f = lambda a: np.ascontiguousarray(np.asarray(a, dtype=np.float32))
    x_prompt, x_sample, cache_k, cache_v = f(x_prompt), f(x_sample), f(cache_k), f(cache_v)
    c, c_ctx, norm_g, w_ada, b_ada, w_in = f(c), f(c_ctx), f(norm_g), f(w_ada), f(b_ada), f(w_in)
    q_norm_g, k_norm_g, conv_w, conv_b = f(q_norm_g), f(k_norm_g), f(conv_w), f(conv_b)
    pool_w, pool_scale, w_out, final_g = f(pool_w), f(pool_scale), f(w_out), f(final_g)

    if "nc" not in _NC_CACHE:
        _NC_CACHE["nc"] = build_nc()
    nc = _NC_CACHE["nc"]

    mats = _const_mats()
    p = np.arange(128)
    pwm = np.zeros((128, 4, 128), np.float32)
    for l in range(2):
        for cc in range(2):
            for hf in range(2):
                pwm[hf * 64:(hf + 1) * 64, l * 2 + cc, hf * 64:(hf + 1) * 64] = pool_w[l, 2 * cc + hf]
    kgb = np.ascontiguousarray(np.broadcast_to(np.tile(k_norm_g, (1, 2))[None, :, :], (128, 2, 128))).astype(np.float32)
    fgb = np.ascontiguousarray(np.broadcast_to(final_g[None, :], (128, D))).astype(np.float32)
    prp = _pool_rc(np.tile(np.arange(256), 2), 256)

    in_maps = []
    for i in range(NCORES):
        b, h = i // 2, i % 2
        own = slice(h * 1024, (h + 1) * 1024)
        oth = slice((1 - h) * 1024, (2 - h) * 1024)
        xs = np.ascontiguousarray(np.concatenate([x_sample[b, own], x_sample[b, oth]], axis=0))
        xp = np.ascontiguousarray(x_prompt[2 * i:2 * i + 2].reshape(512, D))
        nat = np.concatenate([np.arange(h * 1024, (h + 1) * 1024), np.arange((1 - h) * 1024, (2 - h) * 1024)])
        rc, rs = _rope_tables(nat)
        prs = _pool_rc(nat, 2048)
        cst = np.zeros((128, NCST), np.float32)
        for k in range(8):
            cst[:, C_CC + 2 * k] = c[b, k * 128:(k + 1) * 128]
            cst[:, C_CC + 2 * k + 1] = c_ctx[k * 128:(k + 1) * 128]
        for l in range(2):
            for k in range(8):
                cst[:, C_NG + l * 8 + k] = norm_g[l, k * 128:(k + 1) * 128]
            cst[:, C_QG + l] = q_norm_g[l, p % 64]
            cst[:, C_KG + l] = k_norm_g[l, p % 64]
            for jj in range(2):
                for tap in range(3):
                    cst[:, C_CW + (l * 2 + jj) * 3 + tap] = conv_w[l, tap, jj * 128:(jj + 1) * 128]
                cst[:, C_CB + l * 2 + jj] = conv_b[l, jj * 128:(jj + 1) * 128]
                cst[:, C_PS + l * 2 + jj] = pool_scale[l, jj * 128:(jj + 1) * 128]
        cst[:, C_MSK + 0] = float(h)
        cst[:, C_MSK + 1] = float(1 - h)
        cst[:, C_MSK + 2] = 1.0
        in_maps.append({
            "xs": xs, "xp": xp,
            "ck": np.ascontiguousarray(cache_k[b].reshape(2, 256, 128)),
            "cv": np.ascontiguousarray(cache_v[b].reshape(2, 256, 128)),
            "w_ada": w_ada, "b_ada": b_ada, "w_in": w_in, "w_out": w_out,
            "cst": cst, "mats": mats, "pwm": pwm, "kgb": kgb, "fgb": fgb,
            "ropec": rc, "ropes": rs, "prs": prs, "prp": prp,
        })

    res = run_bass_kernel_spmd(nc, in_maps, core_ids=list(range(NCORES)))
    y_prompt = np.zeros((16, 256, D), np.float32)
    y_sample = np.zeros((4, 2048, D), np.float32)
    new_k = np.zeros((16, 2, 256, 2, 64), np.float32)
    new_v = np.zeros((16, 2, 256, 2, 64), np.float32)
    for i in range(NCORES):
        r = res.results[i]
        b, h = i // 2, i % 2
        y_prompt[2 * i:2 * i + 2] = np.asarray(r["yp"]).reshape(2, 256, D)
        y_sample[b, h * 1024:(h + 1) * 1024] = np.asarray(r["ys"])
        new_k[2 * i:2 * i + 2] = np.asarray(r["nk"]).reshape(2, 2, 256, 2, 64)
        new_v[2 * i:2 * i + 2] = np.asarray(r["nv"]).reshape(2, 2, 256, 2, 64)
    return (y_prompt, y_sample, new_k, new_v)
```

```python
import contextlib
import numpy as np
import concourse.bass as bass
import concourse.mybir as mybir
from concourse.bass_utils import run_bass_kernel_spmd

F32 = mybir.dt.float32
BF16 = mybir.dt.bfloat16
F32R = mybir.dt.float32r
ALU = mybir.AluOpType
AF = mybir.ActivationFunctionType

NCORES = 8
D = 1024
QO, KO, VO, ZAO, HCO, BCO, CCO, ZCO, UPO, ZPO = 0, 512, 640, 768, 1280, 1536, 1792, 2048, 2304, 2560
IN_W = 2816
EPS = 1e-6

C_CC, C_NG, C_QG, C_KG, C_CW, C_CB, C_PS, C_MSK = 0, 16, 32, 34, 36, 48, 52, 56
NCST = 64

import os as _osk
STRICT = _osk.environ.get("KSTRICT", "1") == "1"
COMPUTE = ("pe", "act", "dve", "pool")
ALL_ENG = COMPUTE + ("sp",)


_DBG = {}


class Sched:
    def __init__(self, nc, sem_alloc):
        self.nc = nc
        self.sem_alloc = sem_alloc
        self.items = {e: [] for e in ALL_ENG}
        self.sems = {}
        self.eng_cnt = {e: 0 for e in COMPUTE}
        for e in COMPUTE:
            self.sems["c_" + e] = sem_alloc("c_" + e)
        self.known = {e: {} for e in ALL_ENG}
        self.stream_cnt = {}
        self.last_w = {}
        self.readers = {}
        self.all_tokens = {}
        self.nops = 0

    def _deps(self, eng, reads, writes):
        toks = []
        for r in reads:
            t = self.last_w.get(r)
            if t is not None:
                toks.append(t)
        for w in writes:
            t = self.last_w.get(w)
            if t is not None and (t[2] != eng or STRICT):
                toks.append(t)
            for t in self.readers.get(w, ()):
                if t[2] != eng or STRICT:
                    toks.append(t)
        return toks

    def _emit_waits(self, eng, toks):
        need = {}
        for (sname, val, src) in toks:
            if src == eng and eng == "pe":
                continue
            if self.known[eng].get(sname, 0) >= val:
                continue
            if need.get(sname, 0) < val:
                need[sname] = val
        for sname, val in need.items():
            self.known[eng][sname] = val
            self.items[eng].append(("wait", sname, val))

    def _commit(self, tok, reads, writes):
        for r in reads:
            self.readers.setdefault(r, []).append(tok)
        for w in writes:
            self.last_w[w] = tok
            self.readers[w] = []
        self.all_tokens[tok[0]] = max(self.all_tokens.get(tok[0], 0), tok[1])

    def op(self, eng, fn, reads=(), writes=()):
        reads = tuple(reads)
        writes = tuple(writes)
        writes = writes + tuple(r for r in reads if r.startswith("ps") and r not in writes)
        self._emit_waits(eng, self._deps(eng, reads, writes))
        self.eng_cnt[eng] += 1
        tok = ("c_" + eng, self.eng_cnt[eng], eng)
        self.items[eng].append(("op", fn, "c_" + eng, 1))
        self._commit(tok, reads, writes)
        self.nops += 1
        return tok

    def dma(self, q, fn, stream, reads=(), writes=()):
        reads = tuple(reads)
        writes = tuple(writes)
        sname = "d_" + stream
        if sname not in self.sems:
            self.sems[sname] = self.sem_alloc(sname)
            self.stream_cnt[sname] = 0
        self._emit_waits(q, self._deps(None, reads, writes))
        self.stream_cnt[sname] += 16
        tok = (sname, self.stream_cnt[sname], "dma")
        self.items[q].append(("op", fn, sname, 16))
        self._commit(tok, reads, writes)
        return tok

    def final_wait(self, eng="sp"):
        self._emit_waits(eng, [(s, v, "x") for s, v in self.all_tokens.items()])

    def replay(self, block):
        sems = self.sems

        def run(eng_name):
            def body(e):
                for it in self.items[eng_name]:
                    if it[0] == "wait":
                        e.wait_ge(sems[it[1]], it[2])
                    else:
                        it[1](e).then_inc(sems[it[2]], it[3])
            return body
        block.tensor(run("pe"))
        block.scalar(run("act"))
        block.vector(run("dve"))
        block.gpsimd(run("pool"))
        block.sync(run("sp"))


def build_nc():
    nc = bass.Bass("TRN2", target_bir_lowering=False)

    def din(name, shape):
        return nc.dram_tensor(name, list(shape), F32, kind="ExternalInput").ap()

    def dout(name, shape):
        return nc.dram_tensor(name, list(shape), F32, kind="ExternalOutput").ap()

    xs_d = din("xs", [2048, D])
    xp_d = din("xp", [512, D])
    ck_d = din("ck", [2, 256, 128])
    cv_d = din("cv", [2, 256, 128])
    wada_d = din("w_ada", [2, D, 3 * D])
    bada_d = din("b_ada", [2, 3 * D])
    win_d = din("w_in", [2, D, IN_W])
    wout_d = din("w_out", [2, D, D])
    cst_d = din("cst", [128, NCST])
    mats_d = din("mats", [128, 4, 128])
    pwm_d = din("pwm", [128, 4, 128])
    kgb_d = din("kgb", [128, 2, 128])
    fgb_d = din("fgb", [128, D])
    ropec_d = din("ropec", [128, 2048])
    ropes_d = din("ropes", [128, 2048])
    prs_d = din("prs", [128, 2, 2048])
    prp_d = din("prp", [128, 2, 512])
    yp_d = dout("yp", [512, D])
    ys_d = dout("ys", [1024, D])
    nk_d = dout("nk", [2, 2, 256, 128])
    nv_d = dout("nv", [2, 2, 256, 128])
    gsc_d = nc.dram_tensor("gsc", [4, D], F32).ap()

    es = contextlib.ExitStack()
    with es:
        def T(name, shape, dt):
            return es.enter_context(nc.sbuf_tensor(name, list(shape), dt))

        xb = T("xb", [128, 16, D], F32)
        wi = T("wi", [128, 8, IN_W], BF16)
        wo = T("wo", [128, 8, D], BF16)
        stg = T("stg", [128, 2, D], F32)
        xn = T("xn", [128, 8, 512], BF16)
        KT = T("KT", [128, 2304], BF16)
        Vst = T("Vst", [128, 18, 192], BF16)
        qT = T("qT", [128, 4, 2, 512], BF16)
        ga = T("ga", [128, 4, 512], BF16)
        mix = T("mix", [128, 8, 512], BF16)
        ybf2 = T("ybf", [128, 2, D], BF16)
        ybf = ybf2[:, 0, :]
        rope = T("rope", [128, 2, 512], F32)
        ptb = T("ptb", [128, 4, 512], BF16)
        tA = T("tA", [128, 544], F32)
        tB = T("tB", [128, 544], F32)
        tC = T("tC", [128, 544], F32)
        tD = T("tD", [128, 544], F32)
        tE = T("tE", [128, 512], BF16)
        tF = T("tF", [128, 512], BF16)
        pB1 = T("pB1", [128, 544], BF16)
        pC1 = T("pC1", [128, 544], BF16)
        pD1 = T("pD1", [128, 544], BF16)
        pA1 = T("pA1", [128, 512], BF16)
        pE1 = T("pE1", [128, 512], BF16)
        xe = T("xe", [128, 8, 4, 2, 8], BF16)
        xh = T("xh", [128, 8, 16], BF16)
        cst = T("cst_s", [128, NCST], F32)
        matf = T("matf", [128, 4, 128], F32)
        identb = T("identb", [128, 128], BF16)
        rmb = T("rmb", [128, 128], BF16)
        bonesb = T("bonesb", [128, 128], BF16)
        pwb = T("pwb", [128, 4, 128], BF16)
        kgbs = T("kgbs", [128, 2, 128], F32)
        epsc = T("epsc", [128, 1], F32)
        scs = T("scs", [128, 16], F32)
        gsc_s = T("gsc_s", [128, 32], F32)
        shc_s = T("shc_s", [128, 32], F32)
        ms = T("ms", [128, 16], F32)
        lnm = T("lnm", [128, 16], F32)
        rstd = T("rstd", [128, 16], F32)
        ssk = T("ssk", [128, 2], F32)
        lnk = T("lnk", [128, 2], F32)
        rk = T("rk", [128, 2], F32)
        hhs = T("hhs", [128, 32], F32)
        uh = T("uh", [128, 32], F32)
        uph = T("uph", [128, 32], F32)
        nks = T("nks", [128, 128], F32)
        nvs = T("nvs", [128, 128], F32)

        pp = [es.enter_context(nc.psum_tensor("pp%d" % i, [128, 1024], F32)) for i in range(4)]
        banks = [pp[i // 2][:, (i % 2) * 512:(i % 2 + 1) * 512] for i in range(8)]
        bankb = [b.bitcast(BF16) for b in banks]

        identf = matf[:, 0, :]
        swpf = matf[:, 3, :]
        gbc = rope[:].rearrange("p a b -> p (a b)")
        prc = ptb[:].rearrange("p a b -> p (a b)").bitcast(F32).rearrange("p (c n) -> p c n", c=2)
        cks = tA[:, 0:256].rearrange("p (t c) -> p t c", t=2)
        cvs = tB[:, 0:256].rearrange("p (t c) -> p t c", t=2)
        ckb = tE[:, 0:256]

        wst = [xb[:, 0:3, :].rearrange("p a b -> p (a b)"), xb[:, 3:6, :].rearrange("p a b -> p (a b)")]
        mrow = xb[0:2, 6:9, :].rearrange("p a b -> p (a b)")
        brow = xb[0:2, 9:12, :].rearrange("p a b -> p (a b)")
        XBR = ["xb%d" % t for t in range(16)]
        WIRA = ["wi%da" % k for k in range(8)]
        WIRB = ["wi%db" % k for k in range(8)]
        WIR = WIRA + WIRB
        WOR = ["wo%d" % k for k in range(8)]

        block = es.enter_context(nc.Block())
        s = Sched(nc, lambda n: es.enter_context(nc.semaphore(n)))

        def col(c):
            return cst[:, c:c + 1]

        s.dma("sp", lambda e: e.dma_start(out=cst[:], in_=cst_d), "cst", writes=["cst"])
        s.dma("sp", lambda e: e.dma_start(out=matf[:], in_=mats_d), "mats", writes=["matf"])
        s.dma("sp", lambda e: e.dma_start(out=kgbs[:], in_=kgb_d), "kgb", writes=["kgbs"])
        s.dma("sp", lambda e: e.dma_start(out=tA[:, 0:512].rearrange("p (a b) -> p a b", a=4), in_=pwm_d),
              "pwm", writes=["tA"])
        s.op("dve", lambda e: e.tensor_copy(out=identb[:], in_=matf[:, 0, :]), reads=["matf"], writes=["identb"])
        s.op("dve", lambda e: e.tensor_copy(out=rmb[:], in_=matf[:, 1, :]), reads=["matf"], writes=["rmb"])
        s.op("dve", lambda e: e.tensor_copy(out=bonesb[:], in_=matf[:, 2, :]), reads=["matf"], writes=["bonesb"])
        s.op("dve", lambda e: e.tensor_copy(out=pwb[:], in_=tA[:, 0:512].rearrange("p (a b) -> p a b", a=4)),
             reads=["tA"], writes=["pwb"])
        s.op("dve", lambda e: e.memset(epsc[:], EPS), writes=["epsc"])
        s.op("dve", lambda e: e.memset(Vst[:, :, 64:128], 1.0), writes=["Vones"])
        s.op("dve", lambda e: e.memset(qT[64:128, :, 0, :], 0.0), writes=["qTz"])
        s.op("dve", lambda e: e.memset(qT[0:64, :, 1, :], 0.0), writes=["qTz"])
        s.op("act", lambda e: e.activation(out=scs[:], in_=cst[:, C_CC:C_CC + 16], func=AF.Silu),
             reads=["cst"], writes=["scs"])

        def mod_begin(l):
            s.dma("sp", lambda e: e.dma_start(out=brow, in_=bada_d[l:l + 1, :].partition_broadcast(2)), "brow", writes=XBR[9:12])

        def mod_chunk(l, k):
            sl = k % 2
            s.dma("sp", lambda e: e.dma_start(out=wst[sl], in_=wada_d[l, k * 128:(k + 1) * 128, :]),
                  "wst%d" % sl, writes=XBR[3 * sl:3 * sl + 3])

            def mm(e):
                last = None
                for n in range(6):
                    last = e.matmul(banks[n][0:2, :], lhsT=scs[:, 2 * k:2 * k + 2], rhs=wst[sl][:, n * 512:(n + 1) * 512],
                                    start=(k == 0), stop=(k == 7))
                return last
            s.op("pe", mm, reads=["scs"] + XBR[3 * sl:3 * sl + 3], writes=["ps%d" % n for n in range(6)])

        def mod_end(l):
            for n in range(6):
                s.op("dve", lambda e, n=n: e.tensor_tensor(
                    out=mrow[:, n * 512:(n + 1) * 512], in0=banks[n][0:2, :], in1=brow[:, n * 512:(n + 1) * 512], op=ALU.add),
                    reads=["ps%d" % n] + XBR[9:12], writes=XBR[6:9])
            s.dma("act", lambda e: e.dma_start(out=gsc_d[2 * l:2 * l + 2, :], in_=mrow[:, 2048:3072]),
                  "gscw", reads=XBR[6:9], writes=["gsc_d%d" % l])

            def tr(e):
                last = None
                for j in range(16):
                    last = e.transpose(banks[6][:, 2 * j:2 * j + 2], mrow[:, j * 128:(j + 1) * 128], matf[0:2, 0, 0:2])
                return last
            s.op("pe", tr, reads=["matf"] + XBR[6:9], writes=["ps6"])
            tps3 = banks[6][:, 0:32].rearrange("p (j v) -> p j v", v=2)
            for v in range(2):
                lv = l * 2 + v
                s.op("dve", lambda e, v=v, lv=lv: e.tensor_copy(out=shc_s[:, lv * 8:(lv + 1) * 8], in_=tps3[:, 0:8, v]),
                     reads=["ps6"], writes=["shc%d" % lv])
                s.op("dve", lambda e, v=v, lv=lv: e.scalar_tensor_tensor(
                    out=gsc_s[:, lv * 8:(lv + 1) * 8], in0=tps3[:, 8:16, v], scalar=1.0,
                    in1=cst[:, C_NG + l * 8:C_NG + (l + 1) * 8], op0=ALU.add, op1=ALU.mult),
                    reads=["ps6", "cst"], writes=["gsc%d" % lv])

        def mod_layer(l):
            mod_begin(l)
            for k in range(8):
                mod_chunk(l, k)
            mod_end(l)

        stg_i = [0]

        def stage_slot():
            sl = stg_i[0] % 2
            stg_i[0] += 1
            return sl

        def prep_wi(l, ce="pool", kv_first=False):
            if kv_first:
                for k in range(8):
                    prep_wi_k(l, k, ce, pieces=(0,))
                for k in range(8):
                    prep_wi_k(l, k, ce, pieces=(1, 2))
            else:
                for k in range(8):
                    prep_wi_k(l, k, ce)

        def prep_wi_k(l, k, ce="pool", pieces=(0, 1, 2), q="sp"):
            rows = slice(k * 128, (k + 1) * 128)
            qv = wi[:, k, 0:512].rearrange("p (j t d) -> p j t d", j=4, t=2)
            zv = wi[:, k, ZAO:ZAO + 512].rearrange("p (j t d) -> p j t d", j=4, t=2)
            if 0 in pieces:
                sl = stage_slot()
                s.dma(q, lambda e, l=l, rows=rows, sl=sl: e.dma_start(out=stg[:, sl, :], in_=win_d[l, rows, 0:1024]),
                      "stg%d" % sl, writes=["stg%da" % sl, "stg%db" % sl])

                def c0(e, k=k, sl=sl, qv=qv, zv=zv):
                    e.tensor_copy(out=qv[:, :, 0, :], in_=stg[:, sl, 0:256].rearrange("p (j d) -> p j d", j=4))
                    e.tensor_copy(out=qv[:, :, 1, :], in_=stg[:, sl, 256:512].rearrange("p (j d) -> p j d", j=4))
                    e.tensor_copy(out=wi[:, k, 512:768], in_=stg[:, sl, 512:768])
                    return e.tensor_copy(out=zv[:, :, 0, :], in_=stg[:, sl, 768:1024].rearrange("p (j d) -> p j d", j=4))
                s.op(ce, c0, reads=["stg%da" % sl, "stg%db" % sl], writes=["wi%da" % k])
            if 1 in pieces:
                sl = stage_slot()
                s.dma(q, lambda e, l=l, rows=rows, sl=sl: e.dma_start(out=stg[:, sl, :], in_=win_d[l, rows, 1024:2048]),
                      "stg%d" % sl, writes=["stg%da" % sl, "stg%db" % sl])

                def c1(e, k=k, sl=sl, zv=zv):
                    e.tensor_copy(out=zv[:, :, 1, :], in_=stg[:, sl, 0:256].rearrange("p (j d) -> p j d", j=4))
                    return e.tensor_copy(out=wi[:, k, 1280:2048], in_=stg[:, sl, 256:1024])
                s.op(ce, c1, reads=["stg%da" % sl, "stg%db" % sl], writes=["wi%db" % k])
            if 2 in pieces:
                sl = stage_slot()
                s.dma(q, lambda e, l=l, rows=rows, sl=sl: e.dma_start(out=stg[:, sl, 0:768], in_=win_d[l, rows, 2048:2816]),
                      "stg%d" % sl, writes=["stg%da" % sl, "stg%db" % sl])
                s.op(ce, lambda e, k=k, sl=sl: e.tensor_copy(out=wi[:, k, 2048:2816], in_=stg[:, sl, 0:768]),
                     reads=["stg%da" % sl, "stg%db" % sl], writes=["wi%db" % k])

        def prep_wo(l, v, ce="pool"):
            s.dma("sp", lambda e, l=l, v=v: e.dma_start(out=gbc, in_=gsc_d[2 * l + v:2 * l + v + 1, :].partition_broadcast(128)),
                  "gbc", reads=["gsc_d%d" % l], writes=["rope", "rope2"])
            for k in range(8):
                sl = stage_slot()
                if k < 4:
                    s.dma("sp", lambda e, l=l, k=k, sl=sl: e.dma_start(out=stg[0:64, sl, :], in_=wout_d[l, k * 64:(k + 1) * 64, :]),
                          "stgh%da" % sl, writes=["stg%da" % sl])
                    s.dma("sp", lambda e, l=l, k=k, sl=sl: e.dma_start(out=stg[64:128, sl, :], in_=wout_d[l, (k + 4) * 64:(k + 5) * 64, :]),
                          "stgh%db" % sl, writes=["stg%db" % sl])
                else:
                    s.dma("sp", lambda e, l=l, k=k, sl=sl: e.dma_start(out=stg[:, sl, :], in_=wout_d[l, k * 128:(k + 1) * 128, :]),
                          "stg%d" % sl, writes=["stg%da" % sl, "stg%db" % sl])
                s.op(ce, lambda e, k=k, sl=sl: e.tensor_tensor(out=wo[:, k, :], in0=stg[:, sl, :], in1=gbc, op=ALU.mult),
                     reads=["stg%da" % sl, "stg%db" % sl, "rope", "rope2"], writes=["wo%d" % k])

        acc_i = [0]

        def next_acc():
            b = (0, 1, 4, 5)[acc_i[0] % 4]
            acc_i[0] += 1
            return b

        tp_i = [0]
        stats_done = set()

        def make_stats(tiles, l, v):
            t0 = tiles[0]
            key = (tuple(tiles), l, v)
            if key in stats_done:
                return
            stats_done.add(key)
            s.op("dve", lambda e, t0=t0: e.memset(ms[:, t0:t0 + 4], 0.0), writes=["ms%d" % t0])
            for t in tiles:
                s.op("act", lambda e, t=t: e.activation(out=ybf[:], in_=xb[:, t, :], func=AF.Square, scale=1.0 / 32.0,
                                                        accum_out=ms[:, t:t + 1]),
                     reads=[XBR[t], "ms%d" % t0], writes=["ybf", "msv%d" % t])
            s.op("act", lambda e, t0=t0: e.activation(out=lnm[:, t0:t0 + 4], in_=ms[:, t0:t0 + 4], func=AF.Ln, bias=epsc[:, 0:1]),
                 reads=["msv%d" % t for t in tiles] + ["epsc"], writes=["lnm%d" % t0])
            s.op("act", lambda e, t0=t0: e.activation(out=rstd[:, t0:t0 + 4], in_=lnm[:, t0:t0 + 4], func=AF.Exp, scale=-0.5),
                 reads=["lnm%d" % t0], writes=["rstd%d" % t0])

        def make_xn(tiles, l, v):
            lv = l * 2 + v
            t0 = tiles[0]
            make_stats(tiles, l, v)
            import os as _os
            KXN = int(_os.environ.get("KXN", "9"))
            if KXN < 1:
                return
            tpa = [pp[0][:].bitcast(BF16), pp[2][:].bitcast(BF16)]
            TPB = ["ps0", "ps1", "ps4", "ps5"]
            for i, t in enumerate(tiles):
                yb = ybf2[:, i % 2, :]
                ybn = "ybf" if i % 2 == 0 else "ybfB"
                s.op("dve", lambda e, t=t, yb=yb: e.tensor_scalar(out=yb, in0=xb[:, t, :], scalar1=rstd[:, t:t + 1], scalar2=None,
                                                                  op0=ALU.mult),
                     reads=[XBR[t], "rstd%d" % t0], writes=[ybn])

                def tr(e, yb=yb, i=i):
                    last = None
                    for k in range(8):
                        last = e.transpose(tpa[k // 4][:, (k % 4) * 512 + i * 128:(k % 4) * 512 + (i + 1) * 128],
                                           yb[:, k * 128:(k + 1) * 128], identb[:])
                    return last
                s.op("pe", tr, reads=[ybn, "identb"], writes=TPB)
            for k in range(8):
                src = tpa[k // 4][:, (k % 4) * 512:(k % 4 + 1) * 512]
                if k < 4:
                    s.op("act", lambda e, k=k, src=src: e.activation(
                        out=xn[:, k, :], in_=src, func=AF.Identity,
                        scale=gsc_s[:, lv * 8 + k:lv * 8 + k + 1], bias=shc_s[:, lv * 8 + k:lv * 8 + k + 1]),
                        reads=["ps0", "ps1", "gsc%d" % lv, "shc%d" % lv], writes=["xn_%d_%d" % (i, k) for i in range(4)])
                else:
                    s.op("dve", lambda e, k=k, src=src: e.tensor_scalar(
                        out=xn[:, k, :], in0=src, scalar1=gsc_s[:, lv * 8 + k:lv * 8 + k + 1],
                        scalar2=shc_s[:, lv * 8 + k:lv * 8 + k + 1], op0=ALU.mult, op1=ALU.add),
                        reads=["ps4", "ps5", "gsc%d" % lv, "shc%d" % lv], writes=["xn_%d_%d" % (i, k) for i in range(4)])

        XNR = ["xn_%d_%d" % (i, k) for i in range(4) for k in range(8)]

        def proj_fm(col0, nq=512, wres=None):
            b = next_acc()

            def mm(e, b=b, col0=col0):
                last = None
                for k in range(8):
                    last = e.matmul(banks[b][:, 0:nq], lhsT=wi[:, k, col0:col0 + 128], rhs=xn[:, k, 0:nq],
                                    start=(k == 0), stop=(k == 7))
                return last
            s.op("pe", mm, reads=(wres or WIR) + XNR, writes=["ps%d" % b])
            return b

        def load_rope(g):
            s.dma("act", lambda e, g=g: e.dma_start(out=rope[:, 0, :], in_=ropec_d[:, g * 512:(g + 1) * 512]),
                  "rope", writes=["rope"])
            s.dma("act", lambda e, g=g: e.dma_start(out=rope[:, 1, :], in_=ropes_d[:, g * 512:(g + 1) * 512]),
                  "rope2", writes=["rope2"])

        qk_i = [0]

        def qk_post(b, gcol, l_col, out_ap, out_res, use_rope):
            par = qk_i[0] % 2
            qk_i[0] += 1
            rb, rbn = ((tA, "tA"), (tB, "tB"))[par]
            if par == 0:
                sq_t, sqn, qg_t, qgn = tE, ["tE"], tF, ["tF"]
                c_t, cn, d_t, dn = tC, ["tC"], tD, ["tD"]
                ssb, rotb = 2, 3
            else:
                sq_t, sqn, qg_t, qgn = pA1, ["p1sa"], pE1, ["p1dt"]
                c_t, cn, d_t, dn = pB1, ["p1ub", "p1ubl", "p1ubr"], pC1, ["p1wc"]
                ssb, rotb = 6, 7
            s.op("act", lambda e, b=b: e.activation(out=sq_t[:, 0:512], in_=banks[b][:], func=AF.Square),
                 reads=["ps%d" % b], writes=sqn)
            s.op("act", lambda e, b=b: e.activation(out=qg_t[:, 0:512], in_=banks[b][:], func=AF.Copy, scale=col(gcol + l_col)),
                 reads=["ps%d" % b, "cst"], writes=qgn)
            s.op("pe", lambda e: e.matmul(banks[ssb][:], lhsT=bonesb[:], rhs=sq_t[:, 0:512], start=True, stop=True),
                 reads=["bonesb"] + sqn, writes=["ps%d" % ssb])
            if use_rope:
                s.op("pe", lambda e: e.matmul(banks[rotb][:], lhsT=rmb[:], rhs=qg_t[:, 0:512], start=True, stop=True),
                     reads=["rmb"] + qgn, writes=["ps%d" % rotb])
            s.op("act", lambda e: e.activation(out=rb[:, 0:512], in_=banks[ssb][:], func=AF.Ln, scale=1.0 / 64.0, bias=epsc[:, 0:1]),
                 reads=["ps%d" % ssb, "epsc"], writes=[rbn])
            s.op("act", lambda e: e.activation(out=rb[:, 0:512], in_=rb[:, 0:512], func=AF.Exp, scale=-0.5),
                 reads=[rbn], writes=[rbn])
            if use_rope:
                s.op("pool", lambda e: e.tensor_tensor(out=c_t[:, 0:512], in0=qg_t[:, 0:512], in1=rope[:, 0, :], op=ALU.mult),
                     reads=qgn + ["rope"], writes=cn)
                s.op("dve", lambda e: e.tensor_tensor(out=d_t[:, 0:512], in0=banks[rotb][:], in1=rope[:, 1, :], op=ALU.mult),
                     reads=["ps%d" % rotb, "rope2"], writes=dn)
                s.op("dve", lambda e: e.tensor_tensor(out=c_t[:, 0:512], in0=c_t[:, 0:512], in1=d_t[:, 0:512], op=ALU.add),
                     reads=cn + dn, writes=cn)
                src, srcn = c_t[:, 0:512], cn
            else:
                src, srcn = qg_t[:, 0:512], qgn
            if isinstance(out_ap, tuple):
                s.op("dve", lambda e: e.tensor_tensor(out=out_ap[0], in0=src[0:64, :], in1=rb[0:64, 0:512], op=ALU.mult),
                     reads=srcn + [rbn], writes=[out_res])
                s.op("dve", lambda e: e.tensor_tensor(out=out_ap[1], in0=src[64:128, :], in1=rb[64:128, 0:512], op=ALU.mult),
                     reads=srcn + [rbn], writes=[out_res])
            else:
                s.op("dve", lambda e: e.tensor_tensor(out=out_ap, in0=src, in1=rb[:, 0:512], op=ALU.mult),
                     reads=srcn + [rbn], writes=[out_res])

        def kv_pass(l, kcol0, vt0, sample, g=None, prompt_tiles=None):
            if sample:
                load_rope(g)
            b = proj_fm(KO, wres=WIRA)
            qk_post(b, C_KG, l, KT[:, kcol0:kcol0 + 512], "KT%d" % (kcol0 // 512), sample)
            for i in range(4):
                bb = next_acc()
                ncol = 128 if sample else 256
                c0 = VO if sample else KO

                def mm(e, bb=bb, i=i, ncol=ncol, c0=c0):
                    last = None
                    for k in range(8):
                        last = e.matmul(banks[bb][:, 0:ncol], lhsT=xn[:, k, i * 128:(i + 1) * 128], rhs=wi[:, k, c0:c0 + ncol],
                                        start=(k == 0), stop=(k == 7))
                    return last
                s.op("pe", mm, reads=WIRA + XNR, writes=["ps%d" % bb])
                vt = vt0 + i
                voff = 0 if sample else 128
                vout = Vst[:, vt, :].rearrange("p (a d) -> p a d", a=3)
                s.op("act", lambda e, bb=bb, vout=vout, voff=voff: e.activation(
                    out=vout[:, 0:3:2, :], in_=banks[bb][:, voff:voff + 128].rearrange("p (a d) -> p a d", a=2), func=AF.Copy),
                    reads=["ps%d" % bb], writes=["V%d" % vt])
                if not sample:
                    sq, ti = divmod(i, 2)
                    s.op("dve", lambda e, bb=bb: e.tensor_copy(out=nvs[:], in_=banks[bb][:, 128:256]),
                         reads=["ps%d" % bb], writes=["nvs"])
                    s.dma("act", lambda e, sq=sq, ti=ti, l=l: e.dma_start(out=nv_d[sq, l, ti * 128:(ti + 1) * 128, :], in_=nvs[:]),
                          "nvo", reads=["nvs"])
                    s.op("dve", lambda e: e.memset(ssk[:], 0.0), writes=["ssk"])
                    for kv in range(2):
                        s.op("act", lambda e, bb=bb, kv=kv: e.activation(
                            out=tE[:, kv * 64:(kv + 1) * 64], in_=banks[bb][:, kv * 64:(kv + 1) * 64], func=AF.Square,
                            accum_out=ssk[:, kv:kv + 1]),
                            reads=["ps%d" % bb, "ssk"], writes=["tE", "sskv%d" % kv])
                    s.op("act", lambda e: e.activation(out=lnk[:], in_=ssk[:], func=AF.Ln, scale=1.0 / 64.0, bias=epsc[:, 0:1]),
                         reads=["sskv0", "sskv1", "epsc"], writes=["lnk"])
                    s.op("act", lambda e: e.activation(out=rk[:], in_=lnk[:], func=AF.Exp, scale=-0.5),
                         reads=["lnk"], writes=["rk"])
                    for kv in range(2):
                        s.op("dve", lambda e, bb=bb, kv=kv, l=l: e.scalar_tensor_tensor(
                            out=nks[:, kv * 64:(kv + 1) * 64], in0=banks[bb][:, kv * 64:(kv + 1) * 64], scalar=rk[:, kv:kv + 1],
                            in1=kgbs[:, l, kv * 64:(kv + 1) * 64], op0=ALU.mult, op1=ALU.mult),
                            reads=["ps%d" % bb, "rk", "kgbs"], writes=["nks%d" % kv])
                    s.dma("act", lambda e, sq=sq, ti=ti, l=l: e.dma_start(out=nk_d[sq, l, ti * 128:(ti + 1) * 128, :], in_=nks[:]),
                          "nko", reads=["nks0", "nks1"])
            if sample:
                s.op("pool", lambda e, g=g: e.tensor_copy(out=xe[:, :, g, 0, :], in_=xn[:, :, 0:8]), reads=XNR, writes=["xe%d" % g])
                s.op("pool", lambda e, g=g: e.tensor_copy(out=xe[:, :, g, 1, :], in_=xn[:, :, 504:512]), reads=XNR,
                     writes=["xe%db" % g])

        def cache_kv(l):
            s.dma("act", lambda e, l=l: e.dma_start(out=cks, in_=ck_d[l].rearrange("(t p) c -> p t c", p=128)), "ckl",
                  writes=["tA"])
            s.dma("act", lambda e, l=l: e.dma_start(out=cvs, in_=cv_d[l].rearrange("(t p) c -> p t c", p=128)), "cvl",
                  writes=["tB"])
            s.op("dve", lambda e: e.tensor_copy(out=ckb.rearrange("p (t c) -> p t c", t=2), in_=cks), reads=["tA"], writes=["tE"])

            def tr(e):
                e.transpose(bankb[7][:, 0:128], ckb[:, 0:128], identb[:])
                return e.transpose(bankb[7][:, 128:256], ckb[:, 128:256], identb[:])
            s.op("pe", tr, reads=["tE", "identb"], writes=["ps7"])
            s.op("act", lambda e: e.activation(out=KT[:, 2048:2304], in_=bankb[7][:, 0:256], func=AF.Copy),
                 reads=["ps7"], writes=["KTc"])
            for i in range(2):
                vout = Vst[:, 16 + i, :].rearrange("p (a d) -> p a d", a=3)
                s.op("dve", lambda e, i=i, vout=vout: e.tensor_copy(
                    out=vout[:, 0:3:2, :], in_=cvs[:, i, :].rearrange("p (a d) -> p a d", a=2)),
                    reads=["tB"], writes=["V%d" % (16 + i)])

        def step_b(l, sample, g=None, inter=None, after_inter=None):
            inter = list(inter or [])

            def pop_inter(k=1):
                for _ in range(k):
                    if inter:
                        inter.pop(0)()
            nseg, L = (1, 512) if sample else (2, 256)
            W = L + 16

            def seg(t):
                return t[:, 0:nseg * W].rearrange("p (s w) -> p s w", s=nseg)

            def pv(b):
                return banks[b][:].rearrange("p (s w) -> p s w", s=nseg)

            def fl(t):
                return t.rearrange("p (s w) -> p s w", s=nseg)

            if sample:
                s.dma("act", lambda e, g=g: e.dma_start(out=prc, in_=prs_d[:, :, g * 512:(g + 1) * 512]), "prc", writes=["ptb0", "ptb1", "ptb2", "ptb3"])
                load_rope(g)
                gl, ml = [(3, 0), (0, 2), (1, 1), (2, 2)][g]
                gr, mr = [(1, 2), (2, 1), (3, 2), (0, 0)][g]
                s.op("pool", lambda e, gl=gl: e.tensor_copy(out=xh[:, :, 0:8], in_=xe[:, :, gl, 1, :]), reads=["xe%db" % gl],
                     writes=["xha"])
                s.op("pool", lambda e, gr=gr: e.tensor_copy(out=xh[:, :, 8:16], in_=xe[:, :, gr, 0, :]), reads=["xe%d" % gr],
                     writes=["xhb"])

                def hm(e):
                    last = None
                    for idx, c0 in enumerate([HCO, HCO + 128, CCO, CCO + 128, UPO, UPO + 128]):
                        for k in range(8):
                            last = e.matmul(banks[7][:, idx * 16:(idx + 1) * 16], lhsT=wi[:, k, c0:c0 + 128], rhs=xh[:, k, :],
                                            start=(k == 0), stop=(k == 7))
                    return last
                s.op("pe", hm, reads=WIR + ["xha", "xhb"], writes=["ps7"])
                s.op("act", lambda e: e.activation(out=hhs[:], in_=banks[7][:, 0:32], func=AF.Copy), reads=["ps7"], writes=["hhs"])
                s.op("dve", lambda e: e.tensor_tensor(out=uh[:], in0=hhs[:], in1=banks[7][:, 32:64], op=ALU.mult),
                     reads=["hhs", "ps7"], writes=["uh"])
                s.op("dve", lambda e: e.tensor_copy(out=uph[:], in_=banks[7][:, 64:96]), reads=["ps7"], writes=["uph"])
            else:
                s.dma("act", lambda e: e.dma_start(out=prc, in_=prp_d), "prc", writes=["ptb0", "ptb1", "ptb2", "ptb3"])

            bq = [proj_fm(QO + j * 128) for j in range(4)]
            for j in range(4):
                qk_post(bq[j], C_QG, l, (qT[0:64, j, 0, :], qT[64:128, j, 1, :]), "qT%d" % j, sample)
                pop_inter(1)
            for j in range(4):
                b = proj_fm(ZAO + j * 128)
                s.op("act", lambda e, b=b, j=j: e.activation(out=ga[:, j, :], in_=banks[b][:], func=AF.Silu),
                     reads=["ps%d" % b], writes=["ga%d" % j])
                pop_inter(1)
            pop_inter(99)
            if after_inter is not None:
                after_inter()
            for jj in range(2):
                cw = C_CW + (l * 2 + jj) * 3
                if jj == 0:
                    h_t, hn, u_t, un, y_t, yn, z_t, zn = tA, ["tA"], tB, ["tB"], tC, ["tC"], tF, ["tF"]
                    uln, urn = "tBl", "tBr"
                else:
                    h_t, hn, u_t, un, y_t, yn, z_t, zn = pA1, ["p1sa"], pB1, ["p1ub"], pC1, ["p1wc"], pE1, ["p1dt"]
                    uln, urn = "p1ubl", "p1ubr"

                def sgu(t=u_t):
                    return t[:, 0:nseg * W].rearrange("p (s w) -> p s w", s=nseg)
                b = proj_fm(HCO + jj * 128)
                s.op("act", lambda e, b=b, h_t=h_t: e.activation(out=h_t[:, 0:512], in_=banks[b][:], func=AF.Copy),
                     reads=["ps%d" % b], writes=hn)
                b = proj_fm(CCO + jj * 128)
                s.op("dve", lambda e, b=b, h_t=h_t, sgu=sgu: e.tensor_tensor(out=sgu()[:, :, 8:8 + L], in0=pv(b), in1=fl(h_t[:, 0:512]),
                                                                          op=ALU.mult),
                     reads=["ps%d" % b] + hn, writes=un)
                if sample:
                    s.op("dve", lambda e, jj=jj, u_t=u_t: e.tensor_scalar(
                        out=u_t[:, 7:8], in0=uh[:, jj * 16 + 7:jj * 16 + 8], scalar1=col(C_MSK + ml), scalar2=None, op0=ALU.mult),
                        reads=["uh", "cst"], writes=[uln])
                    s.op("dve", lambda e, jj=jj, u_t=u_t: e.tensor_scalar(
                        out=u_t[:, 8 + L:9 + L], in0=uh[:, jj * 16 + 8:jj * 16 + 9], scalar1=col(C_MSK + mr), scalar2=None, op0=ALU.mult),
                        reads=["uh", "cst"], writes=[urn])
                else:
                    s.op("dve", lambda e, sgu=sgu: e.memset(sgu()[:, :, 7:8], 0.0), writes=[uln])
                    s.op("dve", lambda e, sgu=sgu: e.memset(sgu()[:, :, 8 + L:9 + L], 0.0), writes=[urn])
                UA = un + [uln, urn]
                s.op("dve", lambda e, cw=cw, jj=jj, sgu=sgu, y_t=y_t: e.tensor_scalar(
                    out=fl(y_t[:, 0:512]), in0=sgu()[:, :, 7:7 + L], scalar1=col(cw), scalar2=col(C_CB + l * 2 + jj),
                    op0=ALU.mult, op1=ALU.add),
                    reads=UA + ["cst"], writes=yn)
                s.op("dve", lambda e, cw=cw, sgu=sgu, y_t=y_t: e.scalar_tensor_tensor(
                    out=fl(y_t[:, 0:512]), in0=sgu()[:, :, 8:8 + L], scalar=col(cw + 1), in1=fl(y_t[:, 0:512]),
                    op0=ALU.mult, op1=ALU.add),
                    reads=UA + yn + ["cst"], writes=yn)
                s.op("dve", lambda e, cw=cw, sgu=sgu, y_t=y_t: e.scalar_tensor_tensor(
                    out=fl(y_t[:, 0:512]), in0=sgu()[:, :, 9:9 + L], scalar=col(cw + 2), in1=fl(y_t[:, 0:512]),
                    op0=ALU.mult, op1=ALU.add),
                    reads=UA + yn + ["cst"], writes=yn)
                b = proj_fm(BCO + jj * 128)
                s.op("dve", lambda e, b=b, y_t=y_t: e.tensor_tensor(out=y_t[:, 0:512], in0=y_t[:, 0:512], in1=banks[b][:], op=ALU.mult),
                     reads=["ps%d" % b] + yn, writes=yn)
                b = proj_fm(ZCO + jj * 128)
                s.op("act", lambda e, b=b, z_t=z_t: e.activation(out=z_t[:, 0:512], in_=banks[b][:], func=AF.Silu),
                     reads=["ps%d" % b], writes=zn)
                s.op("pool", lambda e, jj=jj, y_t=y_t, z_t=z_t: e.tensor_tensor(out=mix[:, 4 + jj, :], in0=y_t[:, 0:512], in1=z_t[:, 0:512],
                                                                             op=ALU.mult),
                     reads=yn + zn, writes=["mix%d" % (4 + jj)])
            def pool_chain(c, eng, ub_t, wc_t, wd_t, sa_t, dt_t, N):
                def sg(t):
                    return t[:, 0:nseg * W].rearrange("p (s w) -> p s w", s=nseg)
                UBN = [N["ub"], N["ubl"], N["ubr"]]
                if sample:
                    s.op(eng, lambda e: e.tensor_scalar(
                        out=ub_t[:, 0:8], in0=uph[:, c * 16:c * 16 + 8], scalar1=col(C_MSK + ml), scalar2=None, op0=ALU.mult),
                        reads=["uph", "cst"], writes=[N["ubl"]])
                    s.op(eng, lambda e: e.tensor_scalar(
                        out=ub_t[:, 8 + L:16 + L], in0=uph[:, c * 16 + 8:c * 16 + 16], scalar1=col(C_MSK + mr), scalar2=None,
                        op0=ALU.mult),
                        reads=["uph", "cst"], writes=[N["ubr"]])
                else:
                    s.op(eng, lambda e: e.memset(sg(ub_t)[:, :, 0:8], 0.0), writes=[N["ubl"]])
                    s.op(eng, lambda e: e.memset(sg(ub_t)[:, :, 8 + L:16 + L], 0.0), writes=[N["ubr"]])
                s.op(eng, lambda e: e.tensor_tensor(out=sg(wc_t)[:, :, 1:W], in0=sg(ub_t)[:, :, 0:W - 1], in1=sg(ub_t)[:, :, 1:W],
                                                    op=ALU.add),
                     reads=UBN, writes=[N["wc"]])
                if c == 0:
                    s.op(eng, lambda e: e.tensor_tensor(out=sg(wd_t)[64:128, :, 2:W - 1], in0=sg(wc_t)[64:128, :, 1:W - 2],
                                                        in1=sg(wc_t)[64:128, :, 3:W], op=ALU.add),
                         reads=[N["wc"]], writes=[N["wd"]])
                else:
                    s.op(eng, lambda e: e.tensor_tensor(out=sg(wd_t)[:, :, 2:W - 1], in0=sg(wc_t)[:, :, 1:W - 2],
                                                        in1=sg(wc_t)[:, :, 3:W], op=ALU.add),
                         reads=[N["wc"]], writes=[N["wd"]])
                    s.op(eng, lambda e: e.tensor_tensor(out=sg(wc_t)[:, :, 4:W - 3], in0=sg(wd_t)[:, :, 2:W - 5],
                                                        in1=sg(wd_t)[:, :, 6:W - 1], op=ALU.add),
                         reads=[N["wd"]], writes=[N["wc"]])
                    s.op(eng, lambda e: e.tensor_tensor(out=sg(wd_t)[64:128, :, 8:W - 7], in0=sg(wc_t)[64:128, :, 4:W - 11],
                                                        in1=sg(wc_t)[64:128, :, 12:W - 3], op=ALU.add),
                         reads=[N["wc"]], writes=[N["wd"]])
                PTBA = ["ptb0", "ptb1", "ptb2", "ptb3"]
                s.op(eng, lambda e: e.tensor_tensor(out=fl(sa_t[0:64, 0:512]), in0=sg(wc_t)[0:64, :, 8:8 + L],
                                                    in1=fl(prc[0:64, c, :]), op=ALU.mult),
                     reads=[N["wc"]] + PTBA, writes=[N["sa"]])
                s.op(eng, lambda e: e.tensor_tensor(out=fl(sa_t[64:128, 0:512]), in0=sg(wd_t)[64:128, :, 8:8 + L],
                                                    in1=fl(prc[64:128, c, :]), op=ALU.mult),
                     reads=[N["wd"]] + PTBA, writes=[N["sa"]])
                s.op(eng, lambda e: e.tensor_tensor(out=fl(dt_t[:, 0:512]), in0=fl(sa_t[:, 0:512]), in1=sg(ub_t)[:, :, 8:8 + L],
                                                    op=ALU.subtract),
                     reads=[N["sa"]] + UBN, writes=[N["dt"]])

            bu1 = proj_fm(UPO + 128)
            bu0 = proj_fm(UPO)
            bz0 = proj_fm(ZPO)
            bz1 = proj_fm(ZPO + 128)
            s.op("act", lambda e: e.activation(out=pB1[:, 0:nseg * W].rearrange("p (s w) -> p s w", s=nseg)[:, :, 8:8 + L],
                                               in_=pv(bu1), func=AF.Copy),
                 reads=["ps%d" % bu1], writes=["p1ub"])
            s.op("act", lambda e: e.activation(out=seg(tB)[:, :, 8:8 + L], in_=pv(bu0), func=AF.Copy),
                 reads=["ps%d" % bu0], writes=["tB"])
            pool_chain(1, "pool", pB1, pC1, pD1, pA1, pE1,
                       dict(ub="p1ub", ubl="p1ubl", ubr="p1ubr", wc="p1wc", wd="p1wd", sa="p1sa", dt="p1dt"))
            pool_chain(0, "dve", tB, tC, tD, tA, tE,
                       dict(ub="tB", ubl="tBl", ubr="tBr", wc="tC", wd="tD", sa="tA", dt="tE"))
            s.op("act", lambda e: e.activation(out=tF[:], in_=banks[bz0][:], func=AF.Silu), reads=["ps%d" % bz0], writes=["tF"])
            s.op("pe", lambda e: e.matmul(banks[2][:], lhsT=pwb[:, l * 2 + 0, :], rhs=tE[:], start=True, stop=True),
                 reads=["pwb", "tE"], writes=["ps2"])
            s.op("dve", lambda e: e.scalar_tensor_tensor(
                out=mix[:, 6, :], in0=banks[2][:], scalar=col(C_PS + l * 2 + 0), in1=tF[:], op0=ALU.mult, op1=ALU.mult),
                reads=["ps2", "tF", "cst"], writes=["mix6"])
            s.op("act", lambda e: e.activation(out=tA[:, 0:512], in_=banks[bz1][:], func=AF.Silu),
                 reads=["ps%d" % bz1], writes=["tA"])
            s.op("pe", lambda e: e.matmul(banks[3][:], lhsT=pwb[:, l * 2 + 1, :], rhs=pE1[:, 0:512], start=True, stop=True),
                 reads=["pwb", "p1dt"], writes=["ps3"])
            s.op("dve", lambda e: e.scalar_tensor_tensor(
                out=mix[:, 7, :], in0=banks[3][:], scalar=col(C_PS + l * 2 + 1), in1=tA[:, 0:512], op0=ALU.mult, op1=ALU.mult),
                reads=["ps3", "tA", "cst"], writes=["mix7"])

        import os as _os2
        WARM = int(_os2.environ.get("KWARM", "20"))
        FILL = int(_os2.environ.get("KFILL", "5"))
        po_i = [0]
        KTALL = ["KT0", "KT1", "KT2", "KT3", "KTc"]

        def attention(q0, nq, kts, tail_fill=0):
            n = len(kts) // 2
            stpairs = (1, 3)
            pending = [None]

            def make_epilogue(j, pos, on_act=False):
                pa, pb_ = pos
                if nq >= 512 and not on_act:
                    acts = [
                        lambda: s.op("dve", lambda e: e.reciprocal(out=tB[64:128, 0:nq], in_=banks[pa][64:128, 0:nq]),
                                     reads=["ps%d" % pa], writes=["tB"]),
                        lambda: s.op("dve", lambda e: e.reciprocal(out=tB[0:64, 0:nq], in_=banks[pb_][0:64, 0:nq]),
                                     reads=["ps%d" % pb_], writes=["tB"]),
                    ]
                else:
                    acts = [
                        lambda: s.op("act", lambda e: e.activation(out=tA[64:128, 0:nq], in_=banks[pa][64:128, 0:nq], func=AF.Ln),
                                     reads=["ps%d" % pa], writes=["tA"]),
                        lambda: s.op("act", lambda e: e.activation(out=tA[0:64, 0:nq], in_=banks[pb_][0:64, 0:nq], func=AF.Ln),
                                     reads=["ps%d" % pb_], writes=["tA"]),
                        lambda: s.op("act", lambda e: e.activation(out=tB[:, 0:nq], in_=tA[:, 0:nq], func=AF.Exp, scale=-1.0),
                                     reads=["tA"], writes=["tB"]),
                    ]

                def rest():
                    s.dma("act", lambda e: e.dma_start(out=tD[0:64, 0:nq], in_=tB[64:128, 0:nq]), "swpA", reads=["tB"], writes=["tD"])
                    s.dma("act", lambda e: e.dma_start(out=tD[64:128, 0:nq], in_=tB[0:64, 0:nq]), "swpB", reads=["tB"], writes=["tDb"])
                    s.op("dve", lambda e: e.tensor_tensor(out=tC[0:64, 0:nq], in0=banks[pa][0:64, 0:nq],
                                                          in1=ga[0:64, j, q0:q0 + nq], op=ALU.mult),
                         reads=["ps%d" % pa, "ga%d" % j], writes=["tC"])
                    s.op("dve", lambda e: e.tensor_tensor(out=tC[64:128, 0:nq], in0=banks[pb_][64:128, 0:nq],
                                                          in1=ga[64:128, j, q0:q0 + nq], op=ALU.mult),
                         reads=["ps%d" % pb_, "ga%d" % j], writes=["tC"])
                    s.op("dve", lambda e: e.tensor_tensor(out=mix[:, j, q0:q0 + nq], in0=tC[:, 0:nq], in1=tD[:, 0:nq], op=ALU.mult),
                         reads=["tC", "tD", "tDb"], writes=["mix%d" % j])
                return acts, rest

            def flush():
                if pending[0] is not None:
                    acts, rest = pending[0]
                    for a in acts:
                        a()
                    rest()
                    pending[0] = None

            items = [(j, ab, p) for j in range(4) for ab in range(2) for p in range(n)]
            base = po_i[0]
            po_i[0] += 4

            def pos_of(j):
                return (4, 5) if (base + j) % 2 == 0 else (0, 1)

            def qk(idx):
                j, ab, p = items[idx]
                st = stpairs[idx % 2]

                def f(e):
                    e.matmul(pp[st][:, 0:nq], lhsT=KT[:, kts[2 * p][0]:kts[2 * p][0] + 128],
                             rhs=qT[:, j, ab, q0:q0 + nq], start=True, stop=True)
                    return e.matmul(pp[st][:, 512:512 + nq], lhsT=KT[:, kts[2 * p + 1][0]:kts[2 * p + 1][0] + 128],
                                    rhs=qT[:, j, ab, q0:q0 + nq], start=True, stop=True)
                s.op("pe", f, reads=KTALL + ["qT%d" % j, "qTz"], writes=["ps%d" % (2 * st), "ps%d" % (2 * st + 1)])

            def ex(idx):
                st = stpairs[idx % 2]
                sl = 2 * (idx % 2)
                s.op("act", lambda e: e.activation(
                    out=ptb[:, sl:sl + 2, 0:nq], in_=pp[st][:].rearrange("p (b n) -> p b n", b=2)[:, :, 0:nq],
                    func=AF.Exp, scale=0.125),
                    reads=["ps%d" % (2 * st), "ps%d" % (2 * st + 1)], writes=["ptb%d" % sl, "ptb%d" % (sl + 1)])

            def pvm(idx):
                j, ab, p = items[idx]
                po = pos_of(j)[ab]
                lo = 0 if ab == 0 else 64
                sl = 2 * (idx % 2)

                def f(e):
                    e.matmul(banks[po][:, 0:nq], lhsT=Vst[:, kts[2 * p][1], lo:lo + 128], rhs=ptb[:, sl, 0:nq],
                             start=(p == 0), stop=False)
                    return e.matmul(banks[po][:, 0:nq], lhsT=Vst[:, kts[2 * p + 1][1], lo:lo + 128], rhs=ptb[:, sl + 1, 0:nq],
                                    start=False, stop=(p == n - 1))
                s.op("pe", f, reads=["V%d" % kts[2 * p][1], "V%d" % kts[2 * p + 1][1], "Vones", "ptb%d" % sl, "ptb%d" % (sl + 1)],
                     writes=["ps%d" % po])

            qk(0)
            for idx, (j, ab, p) in enumerate(items):
                if idx + 1 < len(items):
                    qk(idx + 1)
                if idx == 0 and nq == 512 and WARM > 0:
                    def burst(e):
                        last = None
                        for r in range(WARM):
                            last = e.matmul(banks[pos_of(0)[1]][:, 0:512], lhsT=bonesb[:], rhs=mix[:, 4 + (r % 4), :], start=True, stop=True)
                        return last
                    s.op("pe", burst, reads=["bonesb", "mix4", "mix5", "mix6", "mix7"], writes=["ps%d" % pos_of(0)[1]])
                ex(idx)
                defer = (n >= 6 and ab == 0 and pending[0] is not None)
                if defer and p in (1, 2):
                    pending[0][0][p - 1]()
                if nq == 512 and j == 0 and ab == 0 and p < 4 and FILL > 0:
                    def fill(e, j=j):
                        last = None
                        for r in range(FILL):
                            last = e.matmul(banks[pos_of(j)[1]][:, 0:512], lhsT=bonesb[:], rhs=mix[:, 4 + (r % 4), :],
                                            start=True, stop=True)
                        return last
                    s.op("pe", fill, reads=["bonesb", "mix4", "mix5", "mix6", "mix7"], writes=["ps%d" % pos_of(j)[1]])
                pvm(idx)
                if defer and p == 4:
                    pending[0][1]()
                    pending[0] = None
                if n < 6 and ab == 0 and p == n - 1:
                    flush()
                if ab == 1 and p == n - 1:
                    pending[0] = make_epilogue(j, pos_of(j), on_act=(j == 3))
            if tail_fill > 0:
                def tfill(e):
                    last = None
                    for r in range(tail_fill):
                        last = e.matmul(pp[1][:, 0:512], lhsT=bonesb[:], rhs=KT[:, (r % 4) * 512:(r % 4 + 1) * 512], start=True, stop=True)
                    return last
                s.op("pe", tfill, reads=["bonesb"] + KTALL, writes=["ps2", "ps3"])
            flush()

        MIXR = ["mix%d" % j for j in range(8)]

        def out_proj_units(tiles, final, out_d, row0):
            units = []
            for i, t in enumerate(tiles):
                for nn in range(2):
                    def unit(i=i, t=t, nn=nn):
                        b = next_acc()

                        def mm(e, b=b):
                            last = None
                            for k in range(8):
                                last = e.matmul(banks[b][:], lhsT=mix[:, k, i * 128:(i + 1) * 128], rhs=wo[:, k, nn * 512:(nn + 1) * 512],
                                                start=(k == 0), stop=(k == 7))
                            return last
                        s.op("pe", mm, reads=MIXR + WOR, writes=["ps%d" % b])
                        s.op("dve", lambda e, b=b: e.tensor_tensor(
                            out=xb[:, t, nn * 512:(nn + 1) * 512], in0=banks[b][:], in1=xb[:, t, nn * 512:(nn + 1) * 512], op=ALU.add),
                            reads=["ps%d" % b, XBR[t]], writes=[XBR[t]])
                        if final and nn == 1:
                            s.op("dve", lambda e: e.memset(ms[:, t:t + 1], 0.0), writes=["msf%d" % t])
                            s.op("act", lambda e: e.activation(out=ybf[:], in_=xb[:, t, :], func=AF.Square, scale=1.0 / 32.0,
                                                               accum_out=ms[:, t:t + 1]),
                                 reads=[XBR[t], "msf%d" % t], writes=["ybf", "msfv%d" % t])
                            s.op("act", lambda e: e.activation(out=lnm[:, t:t + 1], in_=ms[:, t:t + 1], func=AF.Ln, bias=epsc[:, 0:1]),
                                 reads=["msfv%d" % t, "epsc"], writes=["lnf%d" % t])
                            s.op("act", lambda e: e.activation(out=rstd[:, t:t + 1], in_=lnm[:, t:t + 1], func=AF.Exp, scale=-0.5),
                                 reads=["lnf%d" % t], writes=["rsf%d" % t])
                            s.op("dve", lambda e: e.scalar_tensor_tensor(out=xb[:, t, :], in0=xb[:, t, :], scalar=rstd[:, t:t + 1],
                                                                         in1=gbc, op0=ALU.mult, op1=ALU.mult),
                                 reads=[XBR[t], "rsf%d" % t, "rope", "rope2"], writes=[XBR[t]])
                            s.dma("sp" if out_d is ys_d else "act",
                                  lambda e: e.dma_start(out=out_d[row0 + i * 128:row0 + (i + 1) * 128, :], in_=xb[:, t, :]),
                                  "yo%d" % (i % 2), reads=[XBR[t]])
                    units.append(unit)
            return units

        def out_proj(tiles, final, out_d, row0):
            for u in out_proj_units(tiles, final, out_d, row0):
                u()

        def load_x(src_d, g_src, g_dst):
            s.dma("sp", lambda e: e.dma_start(out=xb[:, 4 * g_dst:4 * g_dst + 4, :],
                                              in_=src_d[g_src * 512:(g_src + 1) * 512, :].rearrange("(t p) d -> p t d", p=128)),
                  "xg%d" % g_dst, writes=XBR[4 * g_dst:4 * g_dst + 4])

        def load_fgb():
            s.dma("act", lambda e: e.dma_start(out=gbc, in_=fgb_d), "gbc", writes=["rope", "rope2"])

        import os
        STAGE = os.environ.get("KSTAGE", "all")

        def program():
            PT = [12, 13, 14, 15]
            load_x(xp_d, 0, 3)
            mod_begin(0)
            for k in range(8):
                mod_chunk(0, k)
                prep_wi_k(0, k, "dve" if k % 2 == 0 else "pool", q="act")
            mod_end(0)
            mod_layer(1)
            if STAGE == "mod":
                return
            if STAGE == "prep":
                return
            for l in range(2):
                make_xn(PT, l, 1)
                if STAGE == "p_xn":
                    return
                kv_pass(l, 0, 0, sample=False)
                if l == 0:
                    prep_wo(0, 1, "dve")
                    for g in range(3):
                        load_x(xs_d, g, g)
                if STAGE == "p_kv":
                    return
                step_b(l, sample=False)
                if STAGE == "p_b":
                    return
                if l == 0:
                    prep_wi(1, "pool")
                else:
                    prep_wi(0, "pool", kv_first=True)
                for sq in range(2):
                    attention(sq * 256, 256, [(sq * 256, 2 * sq), (sq * 256 + 128, 2 * sq + 1)])
                if STAGE == "p_att":
                    return
                if l == 1:
                    load_fgb()
                out_proj(PT, final=(l == 1), out_d=yp_d, row0=0)
                if l == 1:
                    for g in range(3):
                        make_stats([4 * g + i for i in range(4)], 0, 0)
                    cache_kv(0)
                    for g in (1, 2):
                        make_xn([4 * g + i for i in range(4)], 0, 0)
                        kv_pass(0, g * 512, 4 * g, sample=True, g=g)
                if STAGE == "p_out":
                    return
                if l == 0:
                    prep_wo(1, 1, "pool")
                else:
                    prep_wo(0, 0, "pool")
            if STAGE == "p":
                return
            load_x(xs_d, 3, 3)
            SKT = [(t * 128, t) for t in range(18)]
            for g in (0, 3):
                make_xn([4 * g + i for i in range(4)], 0, 0)
                kv_pass(0, g * 512, 4 * g, sample=True, g=g)
            if STAGE == "s0kv":
                return
            order0 = [3, 0, 1, 2]
            pend_units = None
            for n_, g in enumerate(order0):
                tiles = [4 * g + i for i in range(4)]
                ai = None
                if pend_units is not None:
                    gp = order0[n_ - 1]
                    ai = (lambda gp=gp: make_stats([4 * gp + i for i in range(4)], 1, 0))
                step_b(0, sample=True, g=g, inter=pend_units, after_inter=ai)
                pend_units = None
                if STAGE == "s0b":
                    return
                if n_ == 3:
                    prep_wi(1)
                else:
                    make_xn([4 * order0[n_ + 1] + i for i in range(4)], 0, 0)
                attention(0, 512, SKT, tail_fill=(28 if n_ == 3 else 0))
                if STAGE == "s0a":
                    return
                if n_ < 3:
                    pend_units = out_proj_units(tiles, final=False, out_d=None, row0=0)
                else:
                    out_proj(tiles, final=False, out_d=None, row0=0)
            prep_wo(1, 0)
            if STAGE == "s0":
                return
            cache_kv(1)
            for g in [2, 3, 0, 1]:
                make_xn([4 * g + i for i in range(4)], 1, 0)
                kv_pass(1, g * 512, 4 * g, sample=True, g=g)
            first = True
            for g in [1, 0]:
                tiles = [4 * g + i for i in range(4)]
                step_b(1, sample=True, g=g)
                if first:
                    make_xn([0, 1, 2, 3], 1, 0)
                attention(0, 512, SKT, tail_fill=28)
                load_fgb()
                first = False
                out_proj(tiles, final=True, out_d=ys_d, row0=g * 512)
        program()
        s.final_wait("sp")
        _DBG["sched"] = s
        s.replay(block)
    return nc


def _const_mats():
    ident = np.eye(128, dtype=np.float32)
    rm = np.zeros((128, 128), np.float32)
    for d in range(128):
        if (d % 32) < 16:
            rm[d + 16, d] = -1.0
        else:
            rm[d - 16, d] = 1.0
    bones = np.zeros((128, 128), np.float32)
    bones[0:64, 0:64] = 1.0
    bones[64:128, 64:128] = 1.0
    swp = np.zeros((128, 128), np.float32)
    swp[np.arange(128), (np.arange(128) + 64) % 128] = 1.0
    return np.ascontiguousarray(np.stack([ident, rm, bones, swp], axis=1))


def _rope_tables(nat):
    half = 32
    inv = (1.0 / (10000.0 ** (np.arange(0, half, 2, dtype=np.float32) / np.float32(half)))).astype(np.float32)
    row = (nat // 64).astype(np.float32)
    colp = (nat % 64).astype(np.float32)
    c = np.zeros((128, nat.shape[0]), np.float32)
    sn = np.zeros((128, nat.shape[0]), np.float32)
    for q in range(128):
        d = q % 64
        idx = d % 16
        ang = (row if d < 32 else colp) * inv[idx]
        ang = ang.astype(np.float32)
        c[q] = np.cos(ang)
        sn[q] = np.sin(ang)
    return c, sn


def _pool_rc(nat, n):
    wins = (2, 4, 8, 16)
    out = np.zeros((128, 2, nat.shape[0]), np.float32)
    for c in range(2):
        for hf in range(2):
            win = wins[2 * c + hf]
            lo = np.clip(nat - win // 2, 0, n - 1)
            hi = np.clip(nat - win // 2 + win - 1, 0, n - 1)
            cnt = (hi - lo + 1).astype(np.float32)
            out[hf * 64:(hf + 1) * 64, c, :] = (1.0 / cnt)[None, :]
    return out


_NC_CACHE = {}


def kernel(x_prompt, x_sample, cache_k, cache_v, c, c_ctx, norm_g, w_ada, b_ada, w_in,
           q_norm_g, k_norm_g, conv_w, conv_b, pool_w, pool_scale, w_out, final_g):
    f = lambda a: np.ascontiguousarray(np.asarray(a, dtype=np.float32))
    x_prompt, x_sample, cache_k, cache_v = f(x_prompt), f(x_sample), f(cache_k), f(cache_v)
    c, c_ctx, norm_g, w_ada, b_ada, w_in = f(c), f(c_ctx), f(norm_g), f(w_ada), f(b_ada), f(w_in)
    q_norm_g, k_norm_g, conv_w, conv_b = f(q_norm_g), f(k_norm_g), f(conv_w), f(conv_b)
    pool_w, pool_scale, w_out, final_g = f(pool_w), f(pool_scale), f(w_out), f(final_g)

    if "nc" not in _NC_CACHE:
        _NC_CACHE["nc"] = build_nc()
    nc = _NC_CACHE["nc"]

    mats = _const_mats()
    p = np.arange(128)
    pwm = np.zeros((128, 4, 128), np.float32)
    for l in range(2):
        for cc in range(2):
            for hf in range(2):
                pwm[hf * 64:(hf + 1) * 64, l * 2 + cc, hf * 64:(hf + 1) * 64] = pool_w[l, 2 * cc + hf]
    kgb = np.ascontiguousarray(np.broadcast_to(np.tile(k_norm_g, (1, 2))[None, :, :], (128, 2, 128))).astype(np.float32)
    fgb = np.ascontiguousarray(np.broadcast_to(final_g[None, :], (128, D))).astype(np.float32)
    prp = _pool_rc(np.tile(np.arange(256), 2), 256)

    in_maps = []
    for i in range(NCORES):
        b, h = i // 2, i % 2
        own = slice(h * 1024, (h + 1) * 1024)
        oth = slice((1 - h) * 1024, (2 - h) * 1024)
        xs = np.ascontiguousarray(np.concatenate([x_sample[b, own], x_sample[b, oth]], axis=0))
        xp = np.ascontiguousarray(x_prompt[2 * i:2 * i + 2].reshape(512, D))
        nat = np.concatenate([np.arange(h * 1024, (h + 1) * 1024), np.arange((1 - h) * 1024, (2 - h) * 1024)])
        rc, rs = _rope_tables(nat)
        prs = _pool_rc(nat, 2048)
        cst = np.zeros((128, NCST), np.float32)
        for k in range(8):
            cst[:, C_CC + 2 * k] = c[b, k * 128:(k + 1) * 128]
            cst[:, C_CC + 2 * k + 1] = c_ctx[k * 128:(k + 1) * 128]
        for l in range(2):
            for k in range(8):
                cst[:, C_NG + l * 8 + k] = norm_g[l, k * 128:(k + 1) * 128]
            cst[:, C_QG + l] = q_norm_g[l, p % 64]
            cst[:, C_KG + l] = k_norm_g[l, p % 64]
            for jj in range(2):
                for tap in range(3):
                    cst[:, C_CW + (l * 2 + jj) * 3 + tap] = conv_w[l, tap, jj * 128:(jj + 1) * 128]
                cst[:, C_CB + l * 2 + jj] = conv_b[l, jj * 128:(jj + 1) * 128]
                cst[:, C_PS + l * 2 + jj] = pool_scale[l, jj * 128:(jj + 1) * 128]
        cst[:, C_MSK + 0] = float(h)
        cst[:, C_MSK + 1] = float(1 - h)
        cst[:, C_MSK + 2] = 1.0
        in_maps.append({
            "xs": xs, "xp": xp,
            "ck": np.ascontiguousarray(cache_k[b].reshape(2, 256, 128)),
            "cv": np.ascontiguousarray(cache_v[b].reshape(2, 256, 128)),
            "w_ada": w_ada, "b_ada": b_ada, "w_in": w_in, "w_out": w_out,
            "cst": cst, "mats": mats, "pwm": pwm, "kgb": kgb, "fgb": fgb,
            "ropec": rc, "ropes": rs, "prs": prs, "prp": prp,
        })

    res = run_bass_kernel_spmd(nc, in_maps, core_ids=list(range(NCORES)))
    y_prompt = np.zeros((16, 256, D), np.float32)
    y_sample = np.zeros((4, 2048, D), np.float32)
    new_k = np.zeros((16, 2, 256, 2, 64), np.float32)
    new_v = np.zeros((16, 2, 256, 2, 64), np.float32)
    for i in range(NCORES):
        r = res.results[i]
        b, h = i // 2, i % 2
        y_prompt[2 * i:2 * i + 2] = np.asarray(r["yp"]).reshape(2, 256, D)
        y_sample[b, h * 1024:(h + 1) * 1024] = np.asarray(r["ys"])
        new_k[2 * i:2 * i + 2] = np.asarray(r["nk"]).reshape(2, 2, 256, 2, 64)
        new_v[2 * i:2 * i + 2] = np.asarray(r["nv"]).reshape(2, 2, 256, 2, 64)
    return (y_prompt, y_sample, new_k, new_v)
```
